# Optimizing a Trainium2 kernel written in Bass

```python
import jax, jax.numpy as jnp
from jax import lax
import numpy as np

D_MODEL = 1024
BATCH = 2
SEQ = 8192
DEPTH = 2
DEC_BATCH = 16
DEC_SEQ = 16
PAST_LEN = 4096

CHUNK = 64
Q_BLOCK = 128
DH = 64
H_A = 8
A_WIDTH = H_A * DH
B_WIDTH = D_MODEL - A_WIDTH
POOL_WINDOWS = (2, 4, 8, 16)
N_POOL_GROUPS = len(POOL_WINDOWS)
POOL_GC = B_WIDTH // N_POOL_GROUPS
POOL_HIST = max(POOL_WINDOWS) - 1
H_C = D_MODEL // DH
D_FF = 4 * D_MODEL
N_AB = (DEPTH + 1) // 2
N_C = DEPTH // 2
AB_IN = 3 * A_WIDTH + H_A + B_WIDTH
EPS = 1e-6

kernel_name = "fox_pool_stickbreak_stream_step"


def _rmsnorm(x, g):
    xf = x.astype(jnp.float32)
    y = xf * lax.rsqrt(jnp.mean(xf * xf, axis=-1, keepdims=True) + EPS)
    return (y * g.astype(jnp.float32)).astype(x.dtype)


def _sq_relu_mlp(h, w_up, w_down):
    return jnp.square(jax.nn.relu(h @ w_up)) @ w_down


def _fox_attend(q, fq, q_pos, k, v, fk, k_pos):
    s = jnp.einsum("bqhd,bkhd->bhqk", q, k).astype(jnp.float32) * (DH ** -0.5)
    s = s + jnp.swapaxes(fq, 1, 2)[..., :, None] - jnp.swapaxes(fk, 1, 2)[..., None, :]
    mask = k_pos[None, :] <= q_pos[:, None]
    p = jax.nn.softmax(jnp.where(mask, s, -jnp.inf), axis=-1)
    return jnp.einsum("bhqk,bkhd->bqhd", p.astype(v.dtype), v)


def _sb_attend(q, q_pos, k, v, k_pos):
    z = jnp.einsum("bqhd,bkhd->bhqk", q, k).astype(jnp.float32) * (DH ** -0.5)
    valid = k_pos[None, :] < q_pos[:, None]
    log1m = jnp.where(valid, jax.nn.log_sigmoid(-z), 0.0)
    later = lax.cumsum(log1m, axis=3, reverse=True) - log1m
    a = jnp.where(valid, jnp.exp(jax.nn.log_sigmoid(z) + later), 0.0)
    return jnp.einsum("bhqk,bkhd->bqhd", a.astype(v.dtype), v)


def _query_blocks(attend, q_side, q_pos, kv_side):
    b, l = q_side[0].shape[:2]
    nb = l // Q_BLOCK
    qs = tuple(jnp.moveaxis(a.reshape((b, nb, Q_BLOCK) + a.shape[2:]), 1, 0) for a in q_side)
    ps = q_pos.reshape(nb, Q_BLOCK)
    out = lax.map(lambda blk: attend(*blk[0], blk[1], *kv_side), (qs, ps))
    return jnp.moveaxis(out, 0, 1).reshape((b, l) + out.shape[3:])


def _multiscale_pool(u, u_hist, p0):
    b, l, _ = u.shape
    u_ext = jnp.concatenate([u_hist.astype(u.dtype), u], axis=1)
    csum = jnp.cumsum(u_ext.astype(jnp.float32), axis=1)
    csum = jnp.concatenate([jnp.zeros((b, 1, B_WIDTH), jnp.float32), csum], axis=1)
    end = csum[:, POOL_HIST + 1:POOL_HIST + 1 + l]
    pos = p0 + jnp.arange(l)
    means = []
    for g, w in enumerate(POOL_WINDOWS):
        sl = slice(g * POOL_GC, (g + 1) * POOL_GC)
        start = csum[:, POOL_HIST + 1 - w:POOL_HIST + 1 - w + l, sl]
        cnt = jnp.minimum(pos + 1, w).astype(jnp.float32)[None, :, None]
        means.append((end[..., sl] - start) / cnt)
    pooled = jnp.concatenate(means, axis=-1) - u.astype(jnp.float32)
    return pooled.astype(u.dtype), u_ext[:, -POOL_HIST:]


def _ab_mixer(h, w_in, b_f, w_pool, pool_scale, w_out, past):
    b, l, _ = h.shape
    proj = h @ w_in
    q, k, v, f_logit, u = jnp.split(proj, [A_WIDTH, 2 * A_WIDTH, 3 * A_WIDTH, 3 * A_WIDTH + H_A], axis=-1)
    q = q.reshape(b, l, H_A, DH)
    k = k.reshape(b, l, H_A, DH)
    v = v.reshape(b, l, H_A, DH)
    logf = jax.nn.log_sigmoid((f_logit + b_f).astype(jnp.float32))
    if past is None:
        p0 = 0
        fcum = jnp.cumsum(logf, axis=1)
        pos = jnp.arange(l)
        a_out = _query_blocks(_fox_attend, (q, fcum), pos, (k, v, fcum, pos))
        u_hist = jnp.zeros((b, POOL_HIST, B_WIDTH), u.dtype)
    else:
        pk, pv, plogf, u_hist = past
        p0 = pk.shape[1]
        k_all = jnp.concatenate([pk.astype(k.dtype), k], axis=1)
        v_all = jnp.concatenate([pv.astype(v.dtype), v], axis=1)
        fcum = jnp.cumsum(jnp.concatenate([plogf.astype(jnp.float32), logf], axis=1), axis=1)
        k_pos = jnp.arange(p0 + l)
        q_pos = p0 + jnp.arange(l)
        a_out = _fox_attend(q, fcum[:, p0:], q_pos, k_all, v_all, fcum, k_pos)
    pooled, new_rows = _multiscale_pool(u, u_hist, p0)
    pooled = jnp.einsum("blgc,gce->blge", pooled.reshape(b, l, N_POOL_GROUPS, POOL_GC), w_pool)
    pooled = pooled.reshape(b, l, B_WIDTH) * pool_scale
    y = jnp.concatenate([a_out.reshape(b, l, A_WIDTH), pooled], axis=-1) @ w_out
    return y, (k, v, logf, new_rows)


def _sb_mixer(h, w_in, w_out, past):
    b, l, _ = h.shape
    q, k, v = jnp.split(h @ w_in, 3, axis=-1)
    q = q.reshape(b, l, H_C, DH)
    k = k.reshape(b, l, H_C, DH)
    v = v.reshape(b, l, H_C, DH)
    if past is None:
        pos = jnp.arange(l)
        out = _query_blocks(_sb_attend, (q,), pos, (k, v, pos))
    else:
        pk, pv = past
        p0 = pk.shape[1]
        k_all = jnp.concatenate([pk.astype(k.dtype), k], axis=1)
        v_all = jnp.concatenate([pv.astype(v.dtype), v], axis=1)
        out = _sb_attend(q, p0 + jnp.arange(l), k_all, v_all, jnp.arange(p0 + l))
    return out.reshape(b, l, D_MODEL) @ w_out, (k, v)


def _trunk(x, c, past, w_ada, b_ada, norm_g, w_in_ab, b_forget, w_pool, pool_scale,
           w_out_ab, w_in_sb, w_out_sb, w_up, w_down, final_g):
    fk_l, fv_l, fl_l, pool_l, sk_l, sv_l = [], [], [], [], [], []
    cond = jax.nn.silu(c)
    for i in range(DEPTH):
        j = i // 2
        mod = cond @ w_ada[i] + b_ada[i]
        sh1, sc1, g1, sh2, sc2, g2 = (m[:, None, :] for m in jnp.split(mod, 6, axis=-1))
        h = _rmsnorm(x, norm_g[i, 0]) * (1 + sc1) + sh1
        if i % 2 == 0:
            lp = None if past is None else (past[0][j], past[1][j], past[2][j], past[3][j])
            y, (k, v, lf, rows) = _ab_mixer(h, w_in_ab[j], b_forget[j], w_pool[j], pool_scale[j], w_out_ab[j], lp)
            fk_l.append(k)
            fv_l.append(v)
            fl_l.append(lf)
            pool_l.append(rows)
        else:
            lp = None if past is None else (past[4][j], past[5][j])
            y, (k, v) = _sb_mixer(h, w_in_sb[j], w_out_sb[j], lp)
            sk_l.append(k)
            sv_l.append(v)
        x = x + g1 * y
        h = _rmsnorm(x, norm_g[i, 1]) * (1 + sc2) + sh2
        x = x + g2 * _sq_relu_mlp(h, w_up[i], w_down[i])
    states = (jnp.stack(fk_l), jnp.stack(fv_l), jnp.stack(fl_l), jnp.stack(pool_l), jnp.stack(sk_l), jnp.stack(sv_l))
    return _rmsnorm(x, final_g), states


def setup_inputs(seed: int = 0) -> dict:
    key = jax.random.key(seed)
    ks = jax.random.split(key, 24)

    def nrm(k, shape, scale):
        return jax.random.normal(k, shape, jnp.float32) * scale

    return {
        "x_prompt": nrm(ks[0], (BATCH, SEQ, D_MODEL), 1.0),
        "x_sample": nrm(ks[1], (DEC_BATCH, DEC_SEQ, D_MODEL), 1.0),
        "c_prompt": nrm(ks[2], (BATCH, D_MODEL), 1.0),
        "c_sample": nrm(ks[3], (DEC_BATCH, D_MODEL), 1.0),
        "cache_fox_k": nrm(ks[4], (N_AB, DEC_BATCH, PAST_LEN, H_A, DH), 1.0),
        "cache_fox_v": nrm(ks[5], (N_AB, DEC_BATCH, PAST_LEN, H_A, DH), 1.0),
        "cache_fox_logf": jax.nn.log_sigmoid(2.0 + nrm(ks[6], (N_AB, DEC_BATCH, PAST_LEN, H_A), 1.0)),
        "state_pool": nrm(ks[7], (N_AB, DEC_BATCH, POOL_HIST, B_WIDTH), 1.0),
        "cache_sb_k": nrm(ks[8], (N_C, DEC_BATCH, PAST_LEN, H_C, DH), 1.0),
        "cache_sb_v": nrm(ks[9], (N_C, DEC_BATCH, PAST_LEN, H_C, DH), 1.0),
        "w_ada": nrm(ks[10], (DEPTH, D_MODEL, 6 * D_MODEL), 0.5 * D_MODEL ** -0.5),
        "b_ada": nrm(ks[11], (DEPTH, 6 * D_MODEL), 0.02),
        "norm_g": 1.0 + nrm(ks[12], (DEPTH, 2, D_MODEL), 0.1),
        "w_in_ab": nrm(ks[13], (N_AB, D_MODEL, AB_IN), D_MODEL ** -0.5),
        "b_forget": 2.0 + nrm(ks[14], (N_AB, H_A), 0.5),
        "w_pool": nrm(ks[15], (N_AB, N_POOL_GROUPS, POOL_GC, POOL_GC), POOL_GC ** -0.5),
        "pool_scale": 1.0 + nrm(ks[16], (N_AB, B_WIDTH), 0.1),
        "w_out_ab": nrm(ks[17], (N_AB, D_MODEL, D_MODEL), D_MODEL ** -0.5),
        "w_in_sb": nrm(ks[18], (N_C, D_MODEL, 3 * D_MODEL), D_MODEL ** -0.5),
        "w_out_sb": nrm(ks[19], (N_C, D_MODEL, D_MODEL), D_MODEL ** -0.5),
        "w_up": nrm(ks[20], (DEPTH, D_MODEL, D_FF), D_MODEL ** -0.5),
        "w_down": nrm(ks[21], (DEPTH, D_FF, D_MODEL), D_FF ** -0.5),
        "final_g": 1.0 + nrm(ks[22], (D_MODEL,), 0.1),
    }


def reference(x_prompt, x_sample, c_prompt, c_sample, cache_fox_k, cache_fox_v, cache_fox_logf,
              state_pool, cache_sb_k, cache_sb_v, w_ada, b_ada, norm_g, w_in_ab, b_forget, w_pool,
              pool_scale, w_out_ab, w_in_sb, w_out_sb, w_up, w_down, final_g):
    y_prompt, (fk_p, fv_p, fl_p, pool_p, sk_p, sv_p) = _trunk(
        x_prompt, c_prompt, None, w_ada, b_ada, norm_g, w_in_ab, b_forget, w_pool, pool_scale,
        w_out_ab, w_in_sb, w_out_sb, w_up, w_down, final_g)
    past = (cache_fox_k, cache_fox_v, cache_fox_logf, state_pool, cache_sb_k, cache_sb_v)
    y_sample, (fk_s, fv_s, fl_s, pool_s, sk_s, sv_s) = _trunk(
        x_sample, c_sample, past, w_ada, b_ada, norm_g, w_in_ab, b_forget, w_pool, pool_scale,
        w_out_ab, w_in_sb, w_out_sb, w_up, w_down, final_g)
    return (y_prompt, y_sample, fk_p, fv_p, fl_p, pool_p, sk_p, sv_p, fk_s, fv_s, fl_s, pool_s, sk_s, sv_s)
```

```python
import contextlib
import numpy as np
import ml_dtypes
from concourse.bass_utils import run_bass_kernel_spmd
import concourse.bass as bass
import concourse.mybir as mybir

F32 = mybir.dt.float32
BF16 = mybir.dt.bfloat16
AF = mybir.ActivationFunctionType
ALU = mybir.AluOpType

DOMS = {
    "pe": "tensor", "act": "scalar", "dve": "vector", "pool": "gpsimd",
    "sp0": "sync", "sp1": "sync", "sp2": "sync", "sp3": "sync",
    "gq0": "gpsimd", "gq1": "gpsimd", "cc": "gpsimd",
}
DMA_DOMS = ("sp0", "sp1", "sp2", "sp3", "gq0", "gq1", "cc")
PHYS = ("tensor", "scalar", "vector", "gpsimd", "sync")


class Sched:
    def __init__(self, nc, cc_inc=16):
        self.nc = nc
        self.stream = {p: [] for p in PHYS}
        self.count = {d: 0 for d in DOMS}
        self.last_w = {}
        self.readers = {}
        self.waited = {p: {} for p in PHYS}
        self.inc = {d: (16 if d in DMA_DOMS else 1) for d in DOMS}
        self.inc["cc"] = cc_inc
        self.rr = 0

    def op(self, dom, fn, reads=(), writes=()):
        phys = DOMS[dom]
        idx = self.count[dom]
        self.count[dom] += 1
        deps = {}

        def add(d, raw=False):
            if d is None:
                return
            d2, i2 = d
            if d2 == dom:
                if dom in ("act", "dve", "pool") and deps.get(d2, -1) < i2:
                    deps[d2] = i2
                return
            if deps.get(d2, -1) < i2:
                deps[d2] = i2

        for r in reads:
            add(self.last_w.get(r), True)
        for w in writes:
            add(self.last_w.get(w), True)
            for d2, i2 in self.readers.get(w, {}).items():
                add((d2, i2))
        waits = []
        wd = self.waited[phys]
        for d2, i2 in deps.items():
            if wd.get(d2, -1) < i2:
                wd[d2] = i2
                waits.append((d2, i2))
        if dom in DMA_DOMS and idx > 0:
            if wd.get(dom, -1) < idx - 1:
                wd[dom] = idx - 1
                waits.append((dom, idx - 1))
        self.stream[phys].append((dom, idx, fn, waits))
        for r in reads:
            rd = self.readers.setdefault(r, {})
            if rd.get(dom, -1) < idx:
                rd[dom] = idx
        for w in writes:
            self.last_w[w] = (dom, idx)
            self.readers[w] = {}
        return (dom, idx)

    def pe(self, fn, reads=(), writes=()):
        return self.op("pe", fn, reads, writes)

    def act(self, fn, reads=(), writes=()):
        return self.op("act", fn, reads, writes)

    def dve(self, fn, reads=(), writes=()):
        return self.op("dve", fn, reads, writes)

    def pool(self, fn, reads=(), writes=()):
        return self.op("pool", fn, reads, writes)

    def dma(self, fn, reads=(), writes=()):
        d = ("sp0", "sp1", "sp2", "sp3")[self.rr % 4]
        self.rr += 1
        return self.op(d, fn, reads, writes)

    def gdma(self, fn, reads=(), writes=()):
        d = ("gq0", "gq1")[self.rr % 2]
        self.rr += 1
        return self.op(d, fn, reads, writes)

    def emit(self, final_waits=True):
        nc = self.nc
        with contextlib.ExitStack() as es:
            sems = {d: es.enter_context(nc.semaphore("s_" + d)) for d in DOMS}
            block = es.enter_context(nc.Block())
            sched = self

            def make(phys):
                def body(eng):
                    for dom, idx, fn, waits in sched.stream[phys]:
                        for d2, i2 in waits:
                            eng.wait_ge(sems[d2], (i2 + 1) * sched.inc[d2])
                        ins = fn(eng)
                        ins.then_inc(sems[dom], sched.inc[dom])
                    if phys == "sync" and final_waits:
                        for d in DOMS:
                            if sched.count[d] > 0:
                                eng.wait_ge(sems[d], sched.count[d] * sched.inc[d])
                return body

            block.tensor(make("tensor"))
            block.scalar(make("scalar"))
            block.vector(make("vector"))
            block.gpsimd(make("gpsimd"))
            block.sync(make("sync"))


D = 1024
SEQ = 8192
NT = 1024
NCH = SEQ // NT
EPS = 1e-6
NEG = -30000.0
SAMPLE_ATT = True
DEBUG = False
DEBUG_OUT = ("fsQ", "fsK", "x1s", "k0T", "v0")
_LAST = {}


def build_program():
    nc = bass.Bass("TRN2", target_bir_lowering=False)
    S = Sched(nc)
    es = contextlib.ExitStack()

    def din(name, shape, dt=F32):
        return nc.dram_tensor(name, list(shape), dt, kind="ExternalInput").ap()

    def dout(name, shape, dt=F32):
        return nc.dram_tensor(name, list(shape), dt, kind="ExternalOutput").ap()

    def dscr(name, shape, dt):
        return nc.dram_tensor(name, list(shape), dt, kind=("ExternalOutput" if (DEBUG and name in DEBUG_OUT) else "Internal")).ap()

    def sb(name, shape, dt):
        return es.enter_context(nc.sbuf_tensor(name, list(shape), dt))

    xp = din("xp", [SEQ, D]); xs = din("xs", [32, D]); cT = din("cT", [128, 24])
    cfk = din("cfk", [2, 4096, 512]); cfv = din("cfv", [2, 4096, 512]); cfl = din("cfl", [2, 4096, 8])
    spool = din("spool", [2, 15, 512]); csk = din("csk", [2, 4096, 1024]); csv = din("csv", [2, 4096, 1024])
    w_ada = din("w_ada", [2, D, 6 * D]); bexp = din("bexp", [128, 2 * 48 * 3]); ngT = din("ngT", [128, 32])
    w_in_ab = din("w_in_ab", [D, 2056]); bfb = din("bfb", [128, 8]); w_pool = din("w_pool", [4, 128, 128])
    pscT = din("pscT", [128, 4]); w_out_ab = din("w_out_ab", [D, D]); w_in_sb = din("w_in_sb", [D, 3 * D])
    w_out_sb = din("w_out_sb", [D, D]); w_up = din("w_up", [2, D, 4 * D]); w_dn = din("w_dn", [2, 4 * D, D])
    fgT = din("fgT", [128, 8])
    identd = din("ident", [128, 128]); maskFd = din("maskF", [128, 128], BF16); msbd = din("msb", [128, 512], BF16)
    trinegd = din("trineg", [128, 128], BF16); seld = din("sel", [128, 4]); rc0d = din("rc0", [128, 64])
    masksd = din("masks", [128, 32], BF16)

    y_o = dout("y_o", [2048, D]); ys_o = dout("ys_o", [32, D])
    fk_o = dout("fk_o", [SEQ, 512]); fv_o = dout("fv_o", [SEQ, 512]); fl_o = dout("fl_o", [SEQ, 8])
    pp_o = dout("pp_o", [128, 512]); sk_o = dout("sk_o", [SEQ, D]); sv_o = dout("sv_o", [SEQ, D])
    fks_o = dout("fks_o", [32, 512]); fvs_o = dout("fvs_o", [32, 512]); fls_o = dout("fls_o", [32, 8])
    pps_o = dout("pps_o", [32, 512]); sks_o = dout("sks_o", [32, D]); svs_o = dout("svs_o", [32, D])

    Wab = dscr("Wab", [D, 2056], BF16); Woab = dscr("Woab", [D, D], BF16); Wsb = dscr("Wsb", [D, 3 * D], BF16)
    Wosb = dscr("Wosb", [D, D], BF16); Wup = dscr("Wup", [2, D, 4 * D], BF16); Wdn = dscr("Wdn", [2, 4 * D, D], BF16)
    Wpl = dscr("Wpl", [4, 128, 128], BF16)
    k0T = dscr("k0T", [512, SEQ + 32], BF16); v0 = dscr("v0", [SEQ + 32, 512], BF16)
    k1T = dscr("k1T", [D, SEQ + 32], BF16); v1 = dscr("v1", [SEQ + 32, D], BF16)
    fsQ = dscr("fsQ", [8, 3, SEQ], BF16); fsK = dscr("fsK", [8, 3, SEQ], BF16)
    x1s = dscr("x1s", [128, 8, SEQ], F32)

    xT = sb("xT", [128, 8, NT], F32); hT = sb("hT", [128, 8, NT], BF16); aT = sb("aT", [128, 8, NT], BF16)
    KA = sb("KA", [128, SEQ], BF16); VA = sb("VA", [128, 64, 65], BF16); QAs = [sb("QA%d" % i, [128, NT], BF16) for i in range(2)]; QA = QAs[0]
    wb = [sb("wb%d" % i, [128, 4096], BF16) for i in range(4)]
    wfb = sb("wfb", [128, 8, 8], BF16); wplb = sb("wplb", [128, 4, 128], BF16)
    ident = sb("identS", [128, 128], F32); identb = sb("identb", [128, 128], BF16)
    maskF = sb("maskFS", [128, 128], BF16); msb = sb("msbS", [128, 512], BF16); trineg = sb("trinegS", [128, 128], BF16)
    masks = sb("masksS", [128, 32], BF16)
    onesb = sb("onesb", [128, 128], BF16); negones = sb("negones", [128, 128], BF16); onesf = sb("onesf", [128, 512], F32)
    sel = sb("selS", [128, 4], F32); rc0 = sb("rc0S", [128, 64], F32); epsb = sb("epsb", [128, 1], F32)
    cTs = sb("cTs", [128, 24], F32); silc = sb("silc", [128, 24], F32); bexps = sb("bexps", [128, 288], F32)
    ngs = sb("ngs", [128, 32], F32); fgs = sb("fgs", [128, 8], F32); pscs = sb("pscs", [128, 4], F32); bfbs = sb("bfbs", [128, 8], F32)
    modT = sb("modT", [128, 2, 48, 3], F32); scl = sb("scl", [128, 2, 2, 8, 3], F32)
    xst = [sb("xst%d" % i, [128, D], F32) for i in range(1)]
    sq = [sb("sq%d" % i, [128, 512], BF16) for i in range(2)]
    rstd = sb("rstd", [128, 512], F32); tmpf = [sb("tmpf%d" % i, [128, 512], F32) for i in range(2)]
    stg = [sb("stg%d" % i, [128, 512], F32) for i in range(2)]
    stb = [sb("stb%d" % i, [128, 512], BF16) for i in range(2)]
    pt = [sb("pt%d" % i, [128, 512], BF16) for i in range(3)]
    Lb = [sb("Lb%d" % i, [128, 512], BF16) for i in range(3)]; Lsum = sb("Lsum", [128, 512], BF16); e1b = [sb("e1b%d" % i, [128, 512], BF16) for i in range(2)]
    lf_tm = sb("lf_tm", [128, 8, 8], F32); fx = sb("fx", [128, 8], F32)
    FT = sb("FT", [8, 512], F32); Fr = sb("Fr", [8, 512], F32); Fcar = sb("Fcar", [8, 1], F32)
    Fs = [sb("Fs%d" % i, [8, 512], BF16) for i in range(3)]
    uext = sb("uext", [128, 528], F32); pa = sb("pa", [128, 528], F32); pb = sb("pb", [128, 528], F32)
    halo = sb("halo", [128, 4, 16], F32); plbs = [sb("plb%d" % i, [128, 512], BF16) for i in range(8)]; t16 = sb("t16", [128, 16], F32)
    rec = sb("rec", [1, 512], F32); osb = sb("osb", [64, 512], F32); osb2 = sb("osb2", [64, 512], F32)
    pbank = [es.enter_context(nc.psum_tensor("pb%d" % i, [128, 512], F32)) for i in range(8)]

    st = {"ps": 0, "pz": 0, "po": 0, "wb": 0, "x": 0, "t": 0, "g": 0, "b": 0, "p": 0, "q": 0, "l": 0, "e": 0, "ev": 0, "pl": 0, "pl3": 0, "gx": 0}

    def rot(key, n):
        st[key] = (st[key] + 1) % n
        return st[key]

    def psg():
        i = rot("ps", 6)
        return pbank[i], "ps%d" % i

    def psl():
        i = rot("pl3", 3)
        return pbank[i], "ps%d" % i

    def psz():
        i = 3 + rot("pz", 3)
        return pbank[i], "ps%d" % i

    def pso():
        i = 6 + rot("po", 2)
        return pbank[i], "ps%d" % i

    def hTn(col):
        return "hT%d" % (col // 512)

    def MM(out, lhsT, rhs, start, stop, reads, writes):
        S.pe(lambda e: e.matmul(out=out, lhsT=lhsT, rhs=rhs, start=start, stop=stop), reads=reads, writes=writes)

    def ld(out, in_, reads, writes):
        S.dma(lambda e: e.dma_start(out=out, in_=in_), reads=reads, writes=writes)

    def stdma(out, in_, reads, writes):
        S.gdma(lambda e: e.dma_start(out=out, in_=in_), reads=reads, writes=writes)

    def wview(i, k):
        return wb[i][:, :].rearrange("p (k n) -> p k n", k=k)

    def load_w(src2d, c0, ncol, rname):
        i = rot("wb", 4)
        v = wview(i, 8)
        ld(v[:, :, 0:ncol], src2d[:, c0:c0 + ncol].rearrange("(k p) n -> p k n", p=128), [rname], ["wb%d" % i])
        return v, "wb%d" % i

    def load_wrows(src2d, r0, rname):
        i = rot("wb", 4)
        v = wview(i, 4)
        ld(v, src2d[r0:r0 + 512, :].rearrange("(k p) n -> p k n", p=128), [rname], ["wb%d" % i])
        return v, "wb%d" % i

    for (t, d, n) in [(ident, identd, "ident"), (maskF, maskFd, "maskF"), (msb, msbd, "msb"), (trineg, trinegd, "trineg"),
                      (sel, seld, "sel"), (rc0, rc0d, "rc0"), (cTs, cT, "cTs"), (bexps, bexp, "bexps"), (ngs, ngT, "ngs"),
                      (fgs, fgT, "fgs"), (pscs, pscT, "pscs"), (bfbs, bfb, "bfbs"), (masks, masksd, "masks")]:
        ld(t[:], d, [], [n])
    S.pool(lambda e: e.memset(onesb[:], 1.0), writes=["onesb"])
    S.pool(lambda e: e.memset(negones[:], -1.0), writes=["negones"])
    S.pool(lambda e: e.memset(onesf[:], 1.0), writes=["onesf"])
    S.pool(lambda e: e.memset(epsb[:], EPS), writes=["epsb"])
    S.pool(lambda e: e.memset(Fcar[:], 0.0), writes=["Fcar"])
    S.pool(lambda e: e.memset(halo[:], 0.0), writes=["halo"])
    S.pool(lambda e: e.memset(VA[:], 1.0), writes=["VA%d" % _i for _i in range(8)])
    S.pool(lambda e: e.memset(KA[64:70, :], 1.0), writes=["KA%d" % _i for _i in range(8)])
    for _q in range(2):
        S.pool(lambda e, _q=_q: e.memset(QAs[_q][64:70, :], 1.0), writes=["QA%d" % _q])
    S.dve(lambda e: e.tensor_copy(out=identb[:], in_=ident[:]), reads=["ident"], writes=["identb"])

    def cast2d(dst, src, rows, name, step=256):
        d2 = dst.rearrange("a b -> (a b)").rearrange("(r c) -> r c", c=1024)
        s2 = src.rearrange("a b -> (a b)").rearrange("(r c) -> r c", c=1024)
        nr = d2.shape[0]
        for r0 in range(0, nr, 256):
            r1 = min(nr, r0 + 256)
            S.gdma(lambda e, r0=r0, r1=r1: e.dma_start(out=d2[r0:r1, :], in_=s2[r0:r1, :]), reads=[], writes=[name])
    cast2d(Wab, w_in_ab, D, "Wab"); cast2d(Woab, w_out_ab, D, "Woab")
    S.gdma(lambda e: e.dma_start(out=Wpl.rearrange("g c e -> (g c) e"), in_=w_pool.rearrange("g c e -> (g c) e")), reads=[], writes=["Wpl"])
    for l in range(2):
        cast2d(Wup[l], w_up[l], D, "Wup%d" % l, 128); cast2d(Wdn[l], w_dn[l], 4 * D, "Wdn%d" % l, 512)
    cast2d(Wsb, w_in_sb, D, "Wsb", 128); cast2d(Wosb, w_out_sb, D, "Wosb")
    ld(wplb[:], Wpl.rearrange("g c e -> c g e"), ["Wpl"], ["wplb"])
    ld(wfb[:], Wab[:, 1536:1544].rearrange("(k p) n -> p k n", p=128), ["Wab"], ["wfb"])

    S.act(lambda e: e.activation(out=silc[:], in_=cTs[:], func=AF.Silu), reads=["cTs"], writes=["silc"])
    for l in range(2):
        bank, bn = psg()
        for ct in range(24):
            ab = wb[ct % 2][:, :].bitcast(F32).rearrange("p (k n) -> p k n", k=8); an = "wb%d" % (ct % 2)
            ld(ab, w_ada[l][:, ct * 256:(ct + 1) * 256].rearrange("(k p) n -> p k n", p=128), [], [an])
            for f2 in range(2):
                fc = ct * 2 + f2
                for kc in range(8):
                    MM(bank[:, fc * 3:fc * 3 + 3], ab[:, kc, f2 * 128:(f2 + 1) * 128], silc[:, kc * 3:kc * 3 + 3],
                       kc == 0, kc == 7, [an, "silc"], [bn])
        S.dve(lambda e, l=l, bank=bank: e.tensor_tensor(out=modT[:, l].rearrange("p f s -> p (f s)"), in0=bank[:, 0:144],
                                                        in1=bexps[:, l * 144:(l + 1) * 144], op=ALU.add),
              reads=[bn, "bexps"], writes=["modT"])
        for w in range(2):
            for c in range(8):
                S.dve(lambda e, l=l, w=w, c=c: e.tensor_scalar(out=scl[:, l, w, c, :], in0=modT[:, l, (8 if w == 0 else 32) + c, :],
                                                               scalar1=1.0, scalar2=ngs[:, (l * 2 + w) * 8 + c:(l * 2 + w) * 8 + c + 1],
                                                               op0=ALU.add, op1=ALU.mult),
                      reads=["modT", "ngs"], writes=["scl"])

    def mod(l, k, c, s):
        return modT[:, l, k * 8 + c, s:s + 1]

    def groups(ncols):
        return [(a, min(a + 512, ncols)) for a in range(0, ncols, 512)]

    def segsplit(a, b, segs):
        return [(max(a, sa), min(b, sb_), s) for (sa, sb_, s) in segs if max(a, sa) < min(b, sb_)]

    def load_x(src, ntok):
        XB = [(tmpf[0], "tmpf0"), (tmpf[1], "tmpf1"), (stg[0], "stg0"), (stg[1], "stg1")]
        for t0 in range(0, ntok, 128):
            n = min(128, ntok - t0)
            for c4 in range(2):
                buf, bufn = XB[rot("gx", 4)]
                ld(buf[0:n, :], src[t0:t0 + n, c4 * 512:(c4 + 1) * 512], [], [bufn])
                bank, bn = psg()
                for cc in range(4):
                    S.pe(lambda e, bank=bank, cc=cc, buf=buf, n=n: e.transpose(out=bank[:, cc * 128:cc * 128 + n], in_=buf[0:n, cc * 128:(cc + 1) * 128],
                                                                               identity=ident[0:n, 0:n]),
                         reads=[bufn, "ident"], writes=[bn])
                outv = xT[:, c4 * 4:c4 * 4 + 4, t0:t0 + n]
                inv = bank[:, :].rearrange("p (c t) -> p c t", c=4)[:, :, 0:n]
                if c4 == 0:
                    S.act(lambda e, outv=outv, inv=inv: e.activation(out=outv, in_=inv, func=AF.Copy), reads=[bn], writes=["xT"])
                else:
                    S.dve(lambda e, outv=outv, inv=inv: e.tensor_copy(out=outv, in_=inv), reads=[bn], writes=["xT"])

    def rms_rstd(a, b):
        bank, bn = psg()
        for c in range(8):
            i = rot("q", 2)
            S.act(lambda e, c=c, i=i: e.activation(out=sq[i][:, 0:b - a], in_=xT[:, c, a:b], func=AF.Square), reads=["xT"], writes=["sq%d" % i])
            MM(bank[:, 0:b - a], onesb[:, :], sq[i][:, 0:b - a], c == 0, c == 7, ["sq%d" % i, "onesb"], [bn])
        S.act(lambda e, bank=bank: e.activation(out=rstd[:, 0:b - a], in_=bank[:, 0:b - a], func=AF.Sqrt, bias=epsb[:, 0:1], scale=1.0 / D),
              reads=[bn, "epsb"], writes=["rstd"])
        S.dve(lambda e: e.reciprocal(out=rstd[:, 0:b - a], in_=rstd[:, 0:b - a]), reads=["rstd"], writes=["rstd"])

    def norm_mod(l, w, ncols, segs):
        for (a, b) in groups(ncols):
            rms_rstd(a, b)
            for c in range(8):
                for (sa, sb_, s) in segsplit(a, b, segs):
                    i = rot("t", 2)
                    S.dve(lambda e, c=c, sa=sa, sb_=sb_, s=s, i=i, a=a: e.scalar_tensor_tensor(
                        out=tmpf[i][:, 0:sb_ - sa], in0=xT[:, c, sa:sb_], scalar=scl[:, l, w, c, s:s + 1], in1=rstd[:, sa - a:sb_ - a],
                        op0=ALU.mult, op1=ALU.mult), reads=["xT", "scl", "rstd"], writes=["tmpf%d" % i])
                    S.act(lambda e, c=c, sa=sa, sb_=sb_, s=s, i=i: e.activation(
                        out=hT[:, c, sa:sb_], in_=tmpf[i][:, 0:sb_ - sa], func=AF.Identity, bias=mod(l, 0 if w == 0 else 3, c, s)),
                        reads=["tmpf%d" % i, "modT"], writes=[hTn(sa)])

    def final_norm_store(ncols, dst):
        for (a, b) in groups(ncols):
            rms_rstd(a, b)
            for t0 in range(a, b, 128):
                n = min(128, b - t0)
                i = rot("x", 1)
                for c4 in range(2):
                    bank, bn = psg()
                    for cc in range(4):
                        c = c4 * 4 + cc
                        j = rot("t", 2)
                        S.dve(lambda e, c=c, t0=t0, n=n, j=j, a=a: e.scalar_tensor_tensor(
                            out=tmpf[j][:, 0:n], in0=xT[:, c, t0:t0 + n], scalar=fgs[:, c:c + 1], in1=rstd[:, t0 - a:t0 - a + n],
                            op0=ALU.mult, op1=ALU.mult), reads=["xT", "fgs", "rstd"], writes=["tmpf%d" % j])
                        S.pe(lambda e, bank=bank, cc=cc, j=j, n=n: e.transpose(out=bank[0:n, cc * 128:(cc + 1) * 128], in_=tmpf[j][:, 0:n], identity=ident[:, :]),
                             reads=["tmpf%d" % j, "ident"], writes=[bn])
                    k = rot("g", 2)
                    S.act(lambda e, bank=bank, k=k, n=n: e.activation(out=stg[k][0:n, :], in_=bank[0:n, :], func=AF.Copy),
                          reads=[bn], writes=["stg%d" % k])
                    stdma(dst[t0:t0 + n, c4 * 512:(c4 + 1) * 512], stg[k][0:n, :], ["stg%d" % k], [])

    def tokmajor(ncols, wlist, out_d, scr, row0, vrow0):
        for t0 in range(0, ncols, 128):
            n = min(128, ncols - t0)
            for wi, (wt, wn) in enumerate(wlist):
                bank, bn = psg()
                for kc in range(8):
                    MM(bank[0:n, :], hT[:, kc, t0:t0 + n], wt[:, kc, :], kc == 0, kc == 7, [hTn(t0), wn], [bn])
                i = rot("g", 2)
                if rot("ev", 2) == 0:
                    S.act(lambda e, bank=bank, i=i, n=n: e.activation(out=stg[i][0:n, :], in_=bank[0:n, :], func=AF.Copy), reads=[bn], writes=["stg%d" % i])
                else:
                    S.dve(lambda e, bank=bank, i=i, n=n: e.tensor_copy(out=stg[i][0:n, :], in_=bank[0:n, :]), reads=[bn], writes=["stg%d" % i])
                stdma(out_d[row0 + t0:row0 + t0 + n, wi * 512:(wi + 1) * 512], stg[i][0:n, :], ["stg%d" % i], [])
                if scr is not None:
                    j = rot("b", 2)
                    S.pool(lambda e, i=i, j=j, n=n: e.tensor_copy(out=stb[j][0:n, 0:512], in_=stg[i][0:n, :]), reads=["stg%d" % i], writes=["stb%d" % j])
                    stdma(scr[vrow0 + t0:vrow0 + t0 + n, wi * 512:(wi + 1) * 512], stb[j][0:n, 0:512], ["stb%d" % j], ["vscr"])

    def logf_proj(ncols, nf, row0):
        for t0 in range(0, ncols, 128):
            n = min(128, ncols - t0)
            bank, bn = psg()
            for kc in range(8):
                MM(bank[0:n, 0:8], hT[:, kc, t0:t0 + n], wfb[:, kc, :], kc == 0, kc == 7, [hTn(t0), "wfb"], [bn])
            tt = t0 // 128
            S.dve(lambda e, bank=bank, n=n: e.tensor_tensor(out=fx[0:n, :], in0=bank[0:n, 0:8], in1=bfbs[0:n, :], op=ALU.add), reads=[bn, "bfbs"], writes=["fx"])
            S.act(lambda e, n=n: e.activation(out=fx[0:n, :], in_=fx[0:n, :], func=AF.Exp, scale=-1.0), reads=["fx"], writes=["fx"])
            S.act(lambda e, n=n: e.activation(out=fx[0:n, :], in_=fx[0:n, :], func=AF.Ln, bias=onesf[0:n, 0:1]), reads=["fx"], writes=["fx"])
            S.dve(lambda e, n=n, tt=tt: e.tensor_scalar(out=lf_tm[0:n, tt, :], in0=fx[0:n, :], scalar1=-1.0, scalar2=None, op0=ALU.mult), reads=["fx"], writes=["lf_tm"])
            stdma(nf[row0 + t0:row0 + t0 + n, :], lf_tm[0:n, tt, :], ["lf_tm"], [])

    def kT_proj(ncols, wlist, kscr, col0):
        for wi, (wt, wn) in enumerate(wlist):
            for fc in range(4):
                for (a, b) in groups(ncols):
                    j = rot("b", 2)
                    bank, bn = psg()
                    for kc in range(8):
                        MM(bank[:, 0:b - a], wt[:, kc, fc * 128:(fc + 1) * 128], hT[:, kc, a:b], kc == 0, kc == 7, [hTn(a), wn], [bn])
                    if rot("ev", 2) == 0:
                        S.act(lambda e, bank=bank, a=a, b=b, j=j: e.activation(out=stb[j][:, 0:b - a], in_=bank[:, 0:b - a], func=AF.Copy), reads=[bn], writes=["stb%d" % j])
                    else:
                        S.dve(lambda e, bank=bank, a=a, b=b, j=j: e.tensor_copy(out=stb[j][:, 0:b - a], in_=bank[:, 0:b - a]), reads=[bn], writes=["stb%d" % j])
                    r0 = wi * 512 + fc * 128
                    stdma(kscr[r0:r0 + 128, col0 + a:col0 + b], stb[j][:, 0:b - a], ["stb%d" % j], ["kscr"])

    def q_proj(ncols, wt, wn, hcol, scale, qi=0):
        QA = QAs[qi]
        for (a, b) in groups(ncols):
            bank, bn = psl()
            for kc in range(8):
                MM(bank[0:64, 0:b - a], wt[:, kc, hcol:hcol + 64], hT[:, kc, a:b], kc == 0, kc == 7, [hTn(a), wn], [bn])
            S.act(lambda e, bank=bank, a=a, b=b: e.activation(out=QA[0:64, a:b], in_=bank[0:64, 0:b - a], func=AF.Copy, scale=scale), reads=[bn], writes=["QA%d" % qi])

    def linear_res(ncols, segs, wsrc, wname, l, gk):
        for wi in range(2):
            wt, wn = load_w(wsrc, wi * 512, 512, wname)
            for oc in range(4):
                c = wi * 4 + oc
                for (a, b) in groups(ncols):
                    bank, bn = psg()
                    for kc in range(8):
                        MM(bank[:, 0:b - a], wt[:, kc, oc * 128:(oc + 1) * 128], aT[:, kc, a:b], kc == 0, kc == 7, ["aT", wn], [bn])
                    for (sa, sb_, s) in segsplit(a, b, segs):
                        S.dve(lambda e, bank=bank, c=c, sa=sa, sb_=sb_, s=s, a=a: e.scalar_tensor_tensor(
                            out=xT[:, c, sa:sb_], in0=bank[:, sa - a:sb_ - a], scalar=mod(l, gk, c, s), in1=xT[:, c, sa:sb_], op0=ALU.mult, op1=ALU.add),
                            reads=[bn, "modT", "xT"], writes=["xT"])

    def mlp(ncols, segs, l):
        for ff2 in range(4):
            wus = [load_w(Wup[l], ff2 * 1024 + hh * 512, 512, "Wup%d" % l) for hh in range(2)]
            for hh, (wu, wun) in enumerate(wus):
                for fc in range(4):
                    for (a, b) in groups(ncols):
                        bank, bn = psg()
                        for kc in range(8):
                            MM(bank[:, 0:b - a], wu[:, kc, fc * 128:(fc + 1) * 128], hT[:, kc, a:b], kc == 0, kc == 7, [hTn(a), wun], [bn])
                        i = rot("t", 2)
                        S.act(lambda e, bank=bank, a=a, b=b, i=i: e.activation(out=tmpf[i][:, 0:b - a], in_=bank[:, 0:b - a], func=AF.Relu), reads=[bn], writes=["tmpf%d" % i])
                        S.dve(lambda e, fc=fc, hh=hh, a=a, b=b, i=i: e.tensor_tensor(out=aT[:, hh * 4 + fc, a:b], in0=tmpf[i][:, 0:b - a], in1=tmpf[i][:, 0:b - a], op=ALU.mult),
                              reads=["tmpf%d" % i], writes=["aT"])
            wds = [load_wrows(Wdn[l], ff2 * 1024 + hh * 512, "Wdn%d" % l) for hh in range(2)]
            for c in range(8):
                for (a, b) in groups(ncols):
                    bank, bn = psg()
                    for f8 in range(8):
                        wd, wdn = wds[f8 // 4]
                        MM(bank[:, 0:b - a], wd[:, f8 % 4, c * 128:(c + 1) * 128], aT[:, f8, a:b], f8 == 0, f8 == 7, ["aT", wdn], [bn])
                    for (sa, sb_, s) in segsplit(a, b, segs):
                        S.dve(lambda e, bank=bank, c=c, sa=sa, sb_=sb_, s=s, a=a: e.scalar_tensor_tensor(
                            out=xT[:, c, sa:sb_], in0=bank[:, sa - a:sb_ - a], scalar=mod(l, 5, c, s), in1=xT[:, c, sa:sb_], op0=ALU.mult, op1=ALU.add),
                            reads=[bn, "modT", "xT"], writes=["xT"])

    def store_head(h, a, b, src_ap, srcn):
        c, p0 = h // 2, (h % 2) * 64
        S.act(lambda e: e.activation(out=aT[p0:p0 + 64, c, a:b], in_=src_ap, func=AF.Copy), reads=[srcn], writes=["aT"])

    def fox_finish_a(h, a, b, ob, obn):
        n = b - a
        S.act(lambda e: e.activation(out=rec[0:1, 0:n], in_=ob[64:65, 0:n], func=AF.Copy), reads=[obn], writes=["rec"])
        S.dve(lambda e: e.reciprocal(out=rec[0:1, 0:n], in_=rec[0:1, 0:n]), reads=["rec"], writes=["rec"])
        S.act(lambda e: e.activation(out=osb[:, 0:n], in_=ob[0:64, 0:n], func=AF.Copy), reads=[obn], writes=["osb"])

    def fox_finish_b(h, a, b, ob, obn):
        n = b - a
        bank, bn = psg()
        MM(bank[0:64, 0:n], onesf[0:1, 0:64], rec[0:1, 0:n], True, True, ["onesf", "rec"], [bn])
        S.dve(lambda e: e.tensor_tensor(out=osb2[:, 0:n], in0=osb[:, 0:n], in1=bank[0:64, 0:n], op=ALU.mult), reads=["osb", bn], writes=["osb2"])
        store_head(h, a, b, osb2[:, 0:n], "osb2")

    def fox_finish(h, a, b, ob, obn):
        fox_finish_a(h, a, b, ob, obn)
        fox_finish_b(h, a, b, ob, obn)

    def pipeline(stages, T, skews=None, hook=None, hook_at=2, hook2=None, hook2_at=9, hook3=None, hook3_at=4):
        if skews is None:
            skews = list(range(len(stages)))
        for step in range(T + max(skews)):
            if hook is not None and step == hook_at:
                hook()
            if hook2 is not None and step == hook2_at:
                hook2()
            if hook3 is not None and step == hook3_at:
                hook3()
            for fn, k in zip(stages, skews):
                t = step - k
                if 0 <= t < T:
                    fn(t)
        if hook is not None and T + max(skews) <= hook_at:
            hook()
        if hook2 is not None and T + max(skews) <= hook2_at:
            hook2()
        if hook3 is not None and T + max(skews) <= hook3_at:
            hook3()

    pending = []
    pending_b = []

    def flush_pending_a():
        while pending:
            fa, fb = pending.pop(0)
            fa()
            pending_b.append(fb)

    def flush_pending_b():
        while pending_b:
            pending_b.pop(0)()

    def flush_pending():
        flush_pending_a()
        flush_pending_b()

    def fox_prompt(h, ci, qi=0, next_q=None):
        QA = QAs[qi]
        nk = (ci + 1) * NT
        ld(QA[67:70, 0:NT], fsQ[h, :, ci * NT:(ci + 1) * NT], ["fsQ"], ["QA%d" % qi])
        for pc in range(nk // 1024):
            ka, kb = pc * 1024, (pc + 1) * 1024
            ld(KA[0:64, ka:kb], k0T[h * 64:(h + 1) * 64, ka:kb], ["kscr"], ["KA%d" % pc])
            ld(KA[64:67, ka:kb], fsK[h, :, ka:kb], ["fsK"], ["KA%d" % pc])
            ld(VA[:, pc * 8:pc * 8 + 8, 0:64], v0[ka:kb, h * 64:(h + 1) * 64].rearrange("(k p) d -> p k d", p=128), ["vscr"], ["VA%d" % pc])
        for g in range(2):
            qa = g * 512
            ob, obn = pso()
            nfull = 8 * ci + 4 * g
            tiles = [(kt, 0, None) for kt in range(nfull)] + [(nfull + i, i * 128, i) for i in range(4)]
            T = len(tiles)
            stt = [None] * T

            def A(t, tiles=tiles, stt=stt, qa=qa):
                kt, c0, di = tiles[t]
                zb, zbn = psz()
                MM(zb[:, c0:512], KA[0:70, kt * 128:(kt + 1) * 128], QA[0:70, qa + c0:qa + 512], True, di is None, ["KA%d" % (kt // 8), "QA%d" % qi], [zbn])
                if di is not None:
                    MM(zb[:, c0:c0 + 128], identb[:, :], maskF[:, :], False, True, ["identb", "maskF"], [zbn])
                i = rot("p", 3)
                S.act(lambda e, zb=zb, c0=c0, i=i: e.activation(out=pt[i][:, c0:512], in_=zb[:, c0:512], func=AF.Exp), reads=[zbn], writes=["pt%d" % i])
                stt[t] = i

            def C(t, tiles=tiles, stt=stt, ob=ob, obn=obn, T=T):
                kt, c0, di = tiles[t]
                i = stt[t]
                MM(ob[0:65, c0:512], VA[:, kt, 0:65], pt[i][:, c0:512], t == 0, t == T - 1, ["VA%d" % (kt // 8), "pt%d" % i], [obn])

            pipeline([A, C], T, [0, 2], hook=flush_pending_a, hook2=flush_pending_b, hook3=(next_q if g == 1 else None))
            pending.append((lambda h=h, qa=qa, ob=ob, obn=obn: fox_finish_a(h, qa, qa + 512, ob, obn),
                            lambda h=h, qa=qa, ob=ob, obn=obn: fox_finish_b(h, qa, qa + 512, ob, obn)))

    def pool_group(g, tg_cols, ucols, first16, wu, wun, out_cols):
        a, b = ucols
        n = b - a
        pi = rot("pl", 8)
        plb = plbs[pi]; plbn = "plb%d" % pi
        w = 2 ** (g + 1)
        bank, bn = psg()
        for kc in range(8):
            MM(bank[:, 0:n], wu[:, kc, g * 128:(g + 1) * 128], hT[:, kc, a:b], kc == 0, kc == 7, [hTn(a), wun], [bn])
        S.pool(lambda e: e.tensor_copy(out=uext[:, 0:16], in_=halo[:, g, :]), reads=["halo"], writes=["uext"])
        S.dve(lambda e: e.tensor_copy(out=uext[:, 16:16 + n], in_=bank[:, 0:n]), reads=[bn], writes=["uext"])
        S.pool(lambda e: e.tensor_copy(out=halo[:, g, :], in_=uext[:, n:n + 16]), reads=["uext"], writes=["halo"])
        E = 16 + n
        src, srcn = uext, "uext"
        bufs = [(pa, "pa"), (pb, "pb")]
        sh = 1
        for stp in range(g + 1):
            dst, dstn = bufs[stp % 2]
            lo = 2 * sh - 1
            S.pool(lambda e, dst=dst, src=src, lo=lo, sh=sh: e.tensor_tensor(out=dst[:, lo:E], in0=src[:, lo:E], in1=src[:, lo - sh:E - sh], op=ALU.add),
                   reads=[srcn], writes=[dstn])
            src, srcn = dst, dstn
            sh *= 2
        S.dve(lambda e, src=src: e.scalar_tensor_tensor(out=plb[:, 0:n], in0=src[:, 16:E], scalar=1.0 / w, in1=uext[:, 16:E], op0=ALU.mult, op1=ALU.subtract),
               reads=[srcn, "uext"], writes=[plbn])
        if first16:
            S.pool(lambda e, src=src: e.tensor_tensor(out=t16[:, :], in0=src[:, 16:32], in1=rc0[:, g * 16:(g + 1) * 16], op=ALU.mult), reads=[srcn, "rc0"], writes=["t16"])
            S.pool(lambda e: e.tensor_tensor(out=plb[:, 0:16], in0=t16[:, :], in1=uext[:, 16:32], op=ALU.subtract), reads=["t16", "uext", plbn], writes=[plbn])
        def part2():
            bank2, bn2 = psg()
            MM(bank2[:, 0:n], wplb[:, g, :], plb[:, 0:n], True, True, ["wplb", plbn], [bn2])
            oa, ob_ = out_cols
            S.act(lambda e: e.activation(out=aT[:, 4 + g, oa:ob_], in_=bank2[:, 0:n], func=AF.Copy, scale=pscs[:, g:g + 1]), reads=[bn2, "pscs"], writes=["aT"])
        return part2

    def f_rows(ncols, col0, ntiles):
        for half in range((ntiles + 3) // 4):
            bank, bn = psg()
            nt_ = min(4, ntiles - half * 4)
            for t in range(nt_):
                S.pe(lambda e, bank=bank, t=t, half=half: e.transpose(out=bank[0:8, t * 128:(t + 1) * 128], in_=lf_tm[:, half * 4 + t, :], identity=ident[:, :]),
                     reads=["lf_tm", "ident"], writes=[bn])
            n = nt_ * 128
            c0 = col0 + half * 512
            S.act(lambda e, bank=bank, n=n: e.activation(out=Fr[:, 0:n], in_=bank[0:8, 0:n], func=AF.Copy), reads=[bn], writes=["Fr"])
            S.dve(lambda e, n=n: e.tensor_tensor_scan(out=FT[:, 0:n], data0=onesf[0:8, 0:n], data1=Fr[:, 0:n],
                                                      initial=0.0, op0=ALU.mult, op1=ALU.add),
                  reads=["Fr", "onesf"], writes=["FT"])
            S.dve(lambda e, n=n: e.tensor_scalar(out=FT[:, 0:n], in0=FT[:, 0:n], scalar1=Fcar[:, 0:1], scalar2=None, op0=ALU.add),
                  reads=["FT", "Fcar"], writes=["FT"])
            S.pool(lambda e, n=n: e.tensor_copy(out=Fcar[:, 0:1], in_=FT[:, n - 1:n]), reads=["FT"], writes=["Fcar"])
            S.dve(lambda e, n=n: e.tensor_copy(out=Fs[0][:, 0:n], in_=FT[:, 0:n]), reads=["FT"], writes=["Fs"])
            S.dve(lambda e, n=n: e.tensor_tensor(out=Fr[:, 0:n], in0=FT[:, 0:n], in1=Fs[0][:, 0:n], op=ALU.subtract), reads=["FT", "Fs"], writes=["Fr"])
            S.dve(lambda e, n=n: e.tensor_copy(out=Fs[1][:, 0:n], in_=Fr[:, 0:n]), reads=["Fr"], writes=["Fs"])
            S.dve(lambda e, n=n: e.tensor_tensor(out=Fr[:, 0:n], in0=Fr[:, 0:n], in1=Fs[1][:, 0:n], op=ALU.subtract), reads=["Fr", "Fs"], writes=["Fr"])
            S.dve(lambda e, n=n: e.tensor_copy(out=Fs[2][:, 0:n], in_=Fr[:, 0:n]), reads=["Fr"], writes=["Fs"])
            for i in range(3):
                stdma(fsQ[:, i, c0:c0 + n], Fs[i][:, 0:n], ["Fs"], ["fsQ"])
            for i in range(3):
                S.pool(lambda e, i=i, n=n: e.tensor_scalar(out=Fs[i][:, 0:n], in0=Fs[i][:, 0:n], scalar1=-1.0, scalar2=None, op0=ALU.mult), reads=["Fs"], writes=["Fs"])
            for i in range(3):
                stdma(fsK[:, i, c0:c0 + n], Fs[i][:, 0:n], ["Fs"], ["fsK"])

    PSEG = [(0, NT, 0)]

    def l0_chunk(ci):
        load_x(xp[ci * NT:(ci + 1) * NT, :], NT)
        norm_mod(0, 0, NT, PSEG)
        wu, wun = load_w(Wab, 1544, 512, "Wab")
        wk, wkn = load_w(Wab, 512, 512, "Wab"); wv, wvn = load_w(Wab, 1024, 512, "Wab")
        p2s = []
        for g in range(4):
            for (a, b) in groups(NT):
                p2s.append(pool_group(g, None, (a, b), ci == 0 and a == 0, wu, wun, (a, b)))
        tokmajor(NT, [(wk, wkn)], fk_o, None, ci * NT, 0)
        tokmajor(NT, [(wv, wvn)], fv_o, v0, ci * NT, ci * NT)
        logf_proj(NT, fl_o, ci * NT)
        kT_proj(NT, [(wk, wkn)], k0T, ci * NT)
        f_rows(NT, ci * NT, 8)
        for p2 in p2s:
            p2()
        if ci == NCH - 1:
            bank, bn = psg()
            for kc in range(8):
                MM(bank[:, :], hT[:, kc, NT - 128:NT], wu[:, kc, :], kc == 0, kc == 7, [hTn(NT - 128), wun], [bn])
            S.act(lambda e, bank=bank: e.activation(out=stg[0][:, :], in_=bank[:, :], func=AF.Copy), reads=[bn], writes=["stg0"])
            stdma(pp_o, stg[0][:, :], ["stg0"], [])
        wq, wqn = load_w(Wab, 0, 512, "Wab")
        q_proj(NT, wq, wqn, 0, 0.125, 0)
        for h in range(8):
            nq = (lambda h=h: q_proj(NT, wq, wqn, (h + 1) * 64, 0.125, (h + 1) % 2)) if h < 7 else None
            fox_prompt(h, ci, h % 2, nq)
        flush_pending()
        linear_res(NT, PSEG, Woab, "Woab", 0, 2)
        norm_mod(0, 1, NT, PSEG)
        mlp(NT, PSEG, 0)
        stdma(x1s[:, :, ci * NT:(ci + 1) * NT], xT[:, :, :], ["xT"], ["x1s"])
        norm_mod(1, 0, NT, PSEG)
        wk1 = [load_w(Wsb, 1024, 512, "Wsb"), load_w(Wsb, 1536, 512, "Wsb")]
        tokmajor(NT, wk1, sk_o, None, ci * NT, 0)
        kT_proj(NT, wk1, k1T, ci * NT)
        wv1 = [load_w(Wsb, 2048, 512, "Wsb"), load_w(Wsb, 2560, 512, "Wsb")]
        tokmajor(NT, wv1, sv_o, v1, ci * NT, ci * NT)

    for ci in range(NCH):
        l0_chunk(ci)

    def sb_group(h, g, qa, qi=0, next_q=None):
        QA = QAs[qi]
        ob, obn = pso()
        S.pool(lambda e: e.memset(Lsum[:, :], 0.0), writes=["Lsum"])
        kts = list(range(16 * g + 15, -1, -1))
        T = len(kts)
        stt = [None] * T

        def A(t):
            kt = kts[t]
            if kt >= 16 * g:
                i0_, em = (kt - 16 * g) // 4, (kt - 16 * g) % 4
                c0 = i0_ * 128
            else:
                i0_, em, c0 = None, None, 0
            zb, zbn = psz()
            MM(zb[:, c0:512], KA[0:64, kt * 128:(kt + 1) * 128], QA[0:64, qa + c0:qa + 512], True, False, ["KA%d" % (kt // 8), "QA%d" % qi], [zbn])
            if i0_ is not None:
                MM(zb[:, c0:c0 + 128], identb[:, :], msb[:, em * 128:(em + 1) * 128], False, False, ["identb", "msb"], [zbn])
            i = rot("e", 2)
            S.act(lambda e, zb=zb, c0=c0, i=i: e.activation(out=e1b[i][:, c0:512], in_=zb[:, c0:512], func=AF.Exp), reads=[zbn], writes=["e1b%d" % i])
            j = rot("l", 3)
            S.act(lambda e, c0=c0, i=i, j=j: e.activation(out=Lb[j][:, c0:512], in_=e1b[i][:, c0:512], func=AF.Ln, bias=onesf[:, 0:1]), reads=["e1b%d" % i], writes=["Lb%d" % j])
            stt[t] = [zb, zbn, c0, j, None]

        def B(t):
            zb, zbn, c0, j, _ = stt[t]
            MM(zb[:, c0:512], trineg[:, :], Lb[j][:, c0:512], False, False, ["trineg", "Lb%d" % j], [zbn])
            MM(zb[:, c0:512], negones[:, :], Lsum[:, c0:512], False, True, ["negones", "Lsum"], [zbn])
            k = rot("p", 3)
            S.act(lambda e, zb=zb, c0=c0, k=k: e.activation(out=pt[k][:, c0:512], in_=zb[:, c0:512], func=AF.Exp), reads=[zbn], writes=["pt%d" % k])
            S.dve(lambda e, c0=c0, j=j: e.tensor_tensor(out=Lsum[:, c0:512], in0=Lsum[:, c0:512], in1=Lb[j][:, c0:512], op=ALU.add), reads=["Lsum", "Lb%d" % j], writes=["Lsum"])
            stt[t][4] = k

        def C(t):
            zb, zbn, c0, j, k = stt[t]
            MM(ob[0:64, c0:512], VA[:, kts[t], 0:64], pt[k][:, c0:512], t == 0, t == T - 1, ["VA%d" % (kts[t] // 8), "pt%d" % k], [obn])

        pipeline([A, B, C], T, hook3=next_q)
        store_head(h, qa, qa + 512, ob[0:64, 0:512], obn)

    def l1_own(oc):
        for mb in range(8):
            m = oc * 8 + mb
            for r in range(4):
                blk = 4 * m + r
                for hf in range(2):
                    buf, bufn = [(tmpf[0], "tmpf0"), (tmpf[1], "tmpf1"), (stg[0], "stg0"), (stg[1], "stg1")][rot("gx", 4)]
                    inv = buf[:, :].rearrange("p (c t) -> p c t", c=4)
                    ld(inv, x1s[:, hf * 4:(hf + 1) * 4, blk * 128:(blk + 1) * 128], ["x1s"], [bufn])
                    outv = xT[:, hf * 4:(hf + 1) * 4, mb * 128:(mb + 1) * 128]
                    if r == 0:
                        S.dve(lambda e, outv=outv, inv=inv, r=r: e.tensor_scalar(out=outv, in0=inv, scalar1=sel[:, r:r + 1], scalar2=None, op0=ALU.mult),
                              reads=[bufn, "sel"], writes=["xT"])
                    else:
                        S.dve(lambda e, outv=outv, inv=inv, r=r: e.scalar_tensor_tensor(out=outv, in0=inv, scalar=sel[:, r:r + 1], in1=outv, op0=ALU.mult, op1=ALU.add),
                              reads=[bufn, "sel", "xT"], writes=["xT"])
        norm_mod(1, 0, NT, PSEG)
        wq1 = [load_w(Wsb, 0, 512, "Wsb"), load_w(Wsb, 512, 512, "Wsb")]
        q_proj(NT, wq1[0][0], wq1[0][1], 0, 0.125, 0)
        for h in range(16):
            nq = (lambda h=h: q_proj(NT, wq1[(h + 1) // 8][0], wq1[(h + 1) // 8][1], ((h + 1) % 8) * 64, 0.125, (h + 1) % 2)) if h < 15 else None
            npc = 4 * oc + 4
            for pc in range(npc - 1, -1, -1):
                ka, kb = pc * 1024, (pc + 1) * 1024
                ld(KA[0:64, ka:kb], k1T[h * 64:(h + 1) * 64, ka:kb], ["kscr"], ["KA%d" % pc])
                ld(VA[:, pc * 8:pc * 8 + 8, 0:64], v1[ka:kb, h * 64:(h + 1) * 64].rearrange("(k p) d -> p k d", p=128), ["vscr"], ["VA%d" % pc])
            for gl in (1, 0):
                sb_group(h, 2 * oc + gl, gl * 512, h % 2, nq if gl == 0 else None)
        linear_res(NT, PSEG, Wosb, "Wosb", 1, 2)
        norm_mod(1, 1, NT, PSEG)
        mlp(NT, PSEG, 1)
        final_norm_store(NT, y_o[oc * NT:(oc + 1) * NT, :])

    for oc in range(2):
        l1_own(oc)

    SSEG = [(0, 16, 1), (16, 32, 2)]

    def sample_cache_T(src, h, hd_cols, nkeys=4096):
        for k0 in range(0, 32, 8):
            i = rot("b", 2)
            S.gdma(lambda e, k0=k0, i=i: e.dma_start(out=stb[i][:, 0:512].rearrange("p (k d) -> p k d", k=8),
                                                     in_=src[k0 * 128:(k0 + 8) * 128, h * 64:(h + 1) * 64].rearrange("(k p) d -> p k d", p=128)),
                   reads=[], writes=["stb%d" % i])
            bank, bn = psl()
            bankb = bank[:, :].bitcast(BF16)
            for t in range(8):
                S.pe(lambda e, bankb=bankb, t=t, i=i: e.transpose(out=bankb[0:64, t * 128:(t + 1) * 128], in_=stb[i][:, t * 64:(t + 1) * 64], identity=identb[:, :]),
                     reads=["stb%d" % i, "identb"], writes=[bn])
            S.act(lambda e, bankb=bankb, k0=k0: e.activation(out=KA[0:64, k0 * 128:(k0 + 8) * 128], in_=bankb[0:64, :], func=AF.Copy), reads=[bn], writes=["KA%d" % (k0 // 8)])

    def sample_cache_V(src, h):
        for k0 in range(0, 32, 8):
            S.gdma(lambda e, k0=k0: e.dma_start(out=VA[:, k0:k0 + 8, 0:64], in_=src[k0 * 128:(k0 + 8) * 128, h * 64:(h + 1) * 64].rearrange("(k p) d -> p k d", p=128)),
                   reads=[], writes=["VA%d" % (k0 // 8)])

    def sample_fox(h, sq_):
        qa = sq_ * 16
        sample_cache_T(cfk[sq_], h, None)
        ld(KA[0:64, 4096:4112], k0T[h * 64:(h + 1) * 64, SEQ + qa:SEQ + qa + 16], ["kscr"], ["KA4"])
        sample_cache_V(cfv[sq_], h)
        ld(VA[0:16, 32, 0:64], v0[SEQ + qa:SEQ + qa + 16, h * 64:(h + 1) * 64], ["vscr"], ["VA4"])
        ob, obn = pso()
        stt = [None] * 33

        def A(kt):
            n = 128 if kt < 32 else 16
            zb, zbn = psz()
            MM(zb[0:n, 0:16], KA[0:64, kt * 128:kt * 128 + n], QA[0:64, qa:qa + 16], True, kt < 32, ["KA%d" % (kt // 8), "QA0"], [zbn])
            if kt == 32:
                MM(zb[0:16, 0:16], identb[0:16, 0:16], masks[0:16, 0:16], False, True, ["identb", "masks"], [zbn])
            i = rot("p", 3)
            S.act(lambda e, zb=zb, n=n, i=i, kt=kt: e.activation(out=pt[i][0:n, 0:16], in_=zb[0:n, 0:16], func=AF.Exp,
                                                                 bias=sbias[0:n, (sq_ * 8 + h) * 33 + kt:(sq_ * 8 + h) * 33 + kt + 1]),
                  reads=[zbn, "sbias"], writes=["pt%d" % i])
            stt[kt] = i

        def C(kt):
            n = 128 if kt < 32 else 16
            i = stt[kt]
            MM(ob[0:65, 0:16], VA[0:n, kt, 0:65], pt[i][0:n, 0:16], kt == 0, kt == 32, ["VA%d" % (kt // 8), "pt%d" % i], [obn])

        pipeline([A, C], 33)
        fox_finish(h, qa, qa + 16, ob, obn)

    def sample_sb(h, sq_):
        qa = sq_ * 16
        sample_cache_T(csk[sq_], h, None)
        ld(KA[0:64, 4096:4112], k1T[h * 64:(h + 1) * 64, SEQ + qa:SEQ + qa + 16], ["kscr"], ["KA4"])
        sample_cache_V(csv[sq_], h)
        ld(VA[0:16, 32, 0:64], v1[SEQ + qa:SEQ + qa + 16, h * 64:(h + 1) * 64], ["vscr"], ["VA4"])
        ob, obn = pso()
        S.pool(lambda e: e.memset(Lsum[:, 0:16], 0.0), writes=["Lsum"])
        kts = list(range(32, -1, -1))
        stt = [None] * 33

        def A(t):
            kt = kts[t]
            n = 128 if kt < 32 else 16
            zb, zbn = psz()
            MM(zb[0:n, 0:16], KA[0:64, kt * 128:kt * 128 + n], QA[0:64, qa:qa + 16], True, False, ["KA%d" % (kt // 8), "QA0"], [zbn])
            if kt == 32:
                MM(zb[0:16, 0:16], identb[0:16, 0:16], masks[0:16, 16:32], False, False, ["identb", "masks"], [zbn])
            i = rot("t", 2)
            S.act(lambda e, zb=zb, n=n, i=i: e.activation(out=tmpf[i][0:n, 0:16], in_=zb[0:n, 0:16], func=AF.Exp), reads=[zbn], writes=["tmpf%d" % i])
            j = rot("l", 3)
            S.act(lambda e, n=n, i=i, j=j: e.activation(out=Lb[j][0:n, 0:16], in_=tmpf[i][0:n, 0:16], func=AF.Ln, bias=onesf[0:n, 0:1]), reads=["tmpf%d" % i], writes=["Lb%d" % j])
            stt[t] = [zb, zbn, n, j, None]

        def B(t):
            zb, zbn, n, j, _ = stt[t]
            MM(zb[0:n, 0:16], trineg[0:n, 0:n], Lb[j][0:n, 0:16], False, False, ["trineg", "Lb%d" % j], [zbn])
            MM(zb[0:n, 0:16], negones[:, 0:n], Lsum[:, 0:16], False, True, ["negones", "Lsum"], [zbn])
            k = rot("p", 3)
            S.act(lambda e, zb=zb, n=n, k=k: e.activation(out=pt[k][0:n, 0:16], in_=zb[0:n, 0:16], func=AF.Exp), reads=[zbn], writes=["pt%d" % k])
            S.dve(lambda e, n=n, j=j: e.tensor_tensor(out=Lsum[0:n, 0:16], in0=Lsum[0:n, 0:16], in1=Lb[j][0:n, 0:16], op=ALU.add), reads=["Lsum", "Lb%d" % j], writes=["Lsum"])
            stt[t][4] = k

        def C(t):
            zb, zbn, n, j, k = stt[t]
            MM(ob[0:64, 0:16], VA[0:n, kts[t], 0:64], pt[k][0:n, 0:16], t == 0, t == 32, ["VA%d" % (kts[t] // 8), "pt%d" % k], [obn])

        pipeline([A, B, C], 33)
        store_head(h, qa, qa + 16, ob[0:64, 0:16], obn)

    sbias = sb("sbias", [128, 2 * 8 * 33], F32)
    clf = sb("clf", [128, 32, 8], F32); csum = sb("csum", [128, 256], F32); ctot = sb("ctot", [128, 256], F32); cpre = sb("cpre", [128, 264], F32)
    triinc = sb("triincS", [128, 128], F32)
    triincd = din("triinc", [128, 128])
    ld(triinc[:], triincd, [], ["triinc"])

    def sample_layer0():
        load_x(xs, 32)
        norm_mod(0, 0, 32, SSEG)
        wk, wkn = load_w(Wab, 512, 512, "Wab"); wv, wvn = load_w(Wab, 1024, 512, "Wab")
        tokmajor(32, [(wk, wkn)], fks_o, None, 0, 0)
        tokmajor(32, [(wv, wvn)], fvs_o, v0, 0, SEQ)
        logf_proj(32, fls_o, 0)
        ld(lfs[0][:, :], lf_tm[0:16, 0, :], ["lf_tm"], ["lfs"])
        ld(lfs[1][:, :], lf_tm[16:32, 0, :], ["lf_tm"], ["lfs"])
        kT_proj(32, [(wk, wkn)], k0T, SEQ)
        wu, wun = load_w(Wab, 1544, 512, "Wab")
        bank, bn = psg()
        for kc in range(8):
            MM(bank[0:32, :], hT[:, kc, 0:32], wu[:, kc, :], kc == 0, kc == 7, [hTn(0), wun], [bn])
        S.act(lambda e, bank=bank: e.activation(out=stg[0][0:32, :], in_=bank[0:32, :], func=AF.Copy), reads=[bn], writes=["stg0"])
        stdma(pps_o, stg[0][0:32, :], ["stg0"], [])
        for sq_ in range(2):
            sps = stg[1]
            ld(sps[0:15, :], spool[sq_], [], ["stg1"])
            bank, bn = psg()
            for g in range(4):
                S.pe(lambda e, bank=bank, g=g: e.transpose(out=bank[:, g * 16 + 1:g * 16 + 16], in_=sps[0:15, g * 128:(g + 1) * 128], identity=ident[0:15, 0:15]),
                     reads=["stg1", "ident"], writes=[bn])
            S.pool(lambda e: e.memset(halo[:], 0.0), writes=["halo"])
            S.dve(lambda e, bank=bank: e.tensor_copy(out=halo[:, :, 1:16], in_=bank[:, 0:64].rearrange("p (g t) -> p g t", g=4)[:, :, 1:16]), reads=[bn], writes=["halo"])
            for g in range(4):
                pool_group(g, None, (sq_ * 16, sq_ * 16 + 16), False, wu, wun, (sq_ * 16, sq_ * 16 + 16))()
        if SAMPLE_ATT:
            for sq_ in range(2):
                ld(clf[:], cfl[sq_].rearrange("(k p) h -> p k h", p=128), [], ["clf"])
                bank, bn = psg()
                MM(bank[:, 0:256], triinc[:, :], clf[:].rearrange("p k h -> p (k h)"), True, True, ["triinc", "clf"], [bn])
                bank2, bn2 = psg()
                MM(bank2[:, 0:256], onesf[:, 0:128], clf[:].rearrange("p k h -> p (k h)"), True, True, ["onesf", "clf"], [bn2])
                S.act(lambda e, bank=bank: e.activation(out=csum[:, :], in_=bank[:, 0:256], func=AF.Copy), reads=[bn], writes=["csum"])
                S.act(lambda e, bank2=bank2: e.activation(out=ctot[:, :], in_=bank2[:, 0:256], func=AF.Copy), reads=[bn2], writes=["ctot"])
                S.pool(lambda e: e.memset(cpre[:, 248:264], 0.0), writes=["cpre"])
                for kt in range(30, -1, -1):
                    S.pool(lambda e, kt=kt: e.tensor_tensor(out=cpre[:, kt * 8:(kt + 1) * 8], in0=cpre[:, (kt + 1) * 8:(kt + 2) * 8], in1=ctot[:, (kt + 1) * 8:(kt + 2) * 8], op=ALU.add),
                           reads=["cpre", "ctot"], writes=["cpre"])
                for h in range(8):
                    col = (sq_ * 8 + h) * 33
                    S.pool(lambda e, h=h, col=col: e.tensor_tensor(out=sbias[:, col:col + 32], in0=ctot[:, :].rearrange("p (k h) -> p h k", h=8)[:, h, :],
                                                                   in1=csum[:, :].rearrange("p (k h) -> p h k", h=8)[:, h, :], op=ALU.subtract),
                           reads=["ctot", "csum"], writes=["sbias"])
                    S.pool(lambda e, h=h, col=col: e.tensor_tensor(out=sbias[:, col:col + 32], in0=sbias[:, col:col + 32],
                                                                   in1=cpre[:, 0:256].rearrange("p (k h) -> p h k", h=8)[:, h, :], op=ALU.add),
                           reads=["sbias", "cpre"], writes=["sbias"])
                bank3, bn3 = psg()
                MM(bank3[0:16, 0:8], triinc[0:16, 0:16], lfs[sq_][0:16, :], True, True, ["triinc", "lfs"], [bn3])
                for h in range(8):
                    col = (sq_ * 8 + h) * 33 + 32
                    S.dve(lambda e, h=h, col=col, bank3=bank3: e.tensor_scalar(out=sbias[0:16, col:col + 1], in0=bank3[0:16, h:h + 1], scalar1=-1.0, scalar2=None, op0=ALU.mult),
                          reads=[bn3], writes=["sbias"])
            wq, wqn = load_w(Wab, 0, 512, "Wab")
            for sq_ in range(2):
                for h in range(8):
                    q_proj(32, wq, wqn, h * 64, 0.125)
                    sample_fox(h, sq_)
        linear_res(32, SSEG, Woab, "Woab", 0, 2)
        norm_mod(0, 1, 32, SSEG)
        mlp(32, SSEG, 0)

    lfs = [sb("lfs%d" % i, [16, 8], F32) for i in range(2)]

    def sample_layer1():
        norm_mod(1, 0, 32, SSEG)
        wk1 = [load_w(Wsb, 1024, 512, "Wsb"), load_w(Wsb, 1536, 512, "Wsb")]
        tokmajor(32, wk1, sks_o, None, 0, 0)
        kT_proj(32, wk1, k1T, SEQ)
        wv1 = [load_w(Wsb, 2048, 512, "Wsb"), load_w(Wsb, 2560, 512, "Wsb")]
        tokmajor(32, wv1, svs_o, v1, 0, SEQ)
        if SAMPLE_ATT:
            wq1 = [load_w(Wsb, 0, 512, "Wsb"), load_w(Wsb, 512, 512, "Wsb")]
            for sq_ in range(2):
                for h in range(16):
                    wt, wn = wq1[h // 8]
                    q_proj(32, wt, wn, (h % 8) * 64, 0.125)
                    sample_sb(h, sq_)
        if DEBUG:
            dbg_a1 = dout("dbg_a1", [128, 8, 32], BF16)
            stdma(dbg_a1, aT[:, :, 0:32], ["aT"], [])
            dbg_h1 = dout("dbg_h1", [128, 8, 32], BF16)
            stdma(dbg_h1, hT[:, :, 0:32], ["hT0"], [])
        linear_res(32, SSEG, Wosb, "Wosb", 1, 2)
        norm_mod(1, 1, 32, SSEG)
        mlp(32, SSEG, 1)
        if DEBUG:
            dbg_x2 = dout("dbg_x2", [128, 8, 32], F32)
            stdma(dbg_x2, xT[:, :, 0:32], ["xT"], [])
        final_norm_store(32, ys_o)

    sample_layer0_pre = None
    sample_layer0()
    sample_layer1()

    S.emit()
    es.close()
    return nc


_NC = None


def kernel(x_prompt, x_sample, c_prompt, c_sample, cache_fox_k, cache_fox_v, cache_fox_logf, state_pool,
           cache_sb_k, cache_sb_v, w_ada, b_ada, norm_g, w_in_ab, b_forget, w_pool, pool_scale, w_out_ab,
           w_in_sb, w_out_sb, w_up, w_down, final_g):
    global _NC
    f32 = np.float32
    bf = ml_dtypes.bfloat16
    A = lambda a: np.ascontiguousarray(np.asarray(a, dtype=f32))
    x_prompt, x_sample, c_prompt, c_sample = A(x_prompt), A(x_sample), A(c_prompt), A(c_sample)
    cache_fox_k, cache_fox_v, cache_fox_logf, state_pool = A(cache_fox_k), A(cache_fox_v), A(cache_fox_logf), A(state_pool)
    cache_sb_k, cache_sb_v = A(cache_sb_k), A(cache_sb_v)
    w_ada, b_ada, norm_g, w_in_ab, b_forget, w_pool = A(w_ada), A(b_ada), A(norm_g), A(w_in_ab), A(b_forget), A(w_pool)
    pool_scale, w_out_ab, w_in_sb, w_out_sb, w_up, w_down, final_g = A(pool_scale), A(w_out_ab), A(w_in_sb), A(w_out_sb), A(w_up), A(w_down), A(final_g)
    if _NC is None:
        _NC = build_program()
    nc = _NC
    kk = np.arange(128)[:, None]; qq = np.arange(128)[None, :]
    ident = np.eye(128, dtype=f32)
    maskF = np.where(kk <= qq, 0.0, NEG).astype(f32)
    mstrict = np.where(kk < qq, 0.0, NEG).astype(f32)
    trineg = np.where(kk >= qq, -1.0, 0.0).astype(f32)
    triinc = np.where(kk <= qq, 1.0, 0.0).astype(f32)
    masks = np.full((128, 32), NEG, f32)
    masks[:16, 0:16] = maskF[:16, :16]; masks[:16, 16:32] = mstrict[:16, :16]
    rc0 = np.zeros((128, 64), f32)
    for g in range(4):
        for pos in range(16):
            rc0[:, g * 16 + pos] = 1.0 / min(pos + 1, 2 ** (g + 1))
    bexp = np.repeat(b_ada.reshape(2, 48, 128).transpose(2, 0, 1)[..., None], 3, axis=-1).reshape(128, 288)
    ngT = norm_g.reshape(2, 2, 8, 128).transpose(3, 0, 1, 2).reshape(128, 32)
    bfb = np.tile(b_forget.reshape(1, 8), (128, 1))
    pscT = pool_scale.reshape(4, 128).T
    fgT = final_g.reshape(8, 128).T
    common = {
        "w_ada": w_ada, "bexp": A(bexp), "ngT": A(ngT), "w_in_ab": w_in_ab[0], "bfb": A(bfb), "w_pool": w_pool[0],
        "pscT": A(pscT), "w_out_ab": w_out_ab[0], "w_in_sb": w_in_sb[0], "w_out_sb": w_out_sb[0], "w_up": w_up, "w_dn": w_down,
        "fgT": A(fgT), "ident": ident, "maskF": maskF.astype(bf), "trineg": trineg.astype(bf), "rc0": rc0,
        "masks": masks.astype(bf), "triinc": triinc,
    }
    in_maps = []
    for c in range(8):
        b, j = c // 4, c % 4
        call = np.stack([c_prompt[b], c_sample[2 * c], c_sample[2 * c + 1]])
        cT = call.reshape(3, 8, 128).transpose(2, 1, 0).reshape(128, 24)
        msb = np.zeros((128, 4, 128), f32)
        for e in range(4):
            msb[:, e, :] = 0.0 if e < j else (mstrict if e == j else NEG)
        sel = np.zeros((128, 4), f32); sel[:, j] = 1.0
        m = dict(common)
        m.update({
            "xp": x_prompt[b], "xs": A(x_sample[2 * c:2 * c + 2].reshape(32, D)), "cT": A(cT),
            "cfk": A(cache_fox_k[0, 2 * c:2 * c + 2].reshape(2, 4096, 512)), "cfv": A(cache_fox_v[0, 2 * c:2 * c + 2].reshape(2, 4096, 512)),
            "cfl": A(cache_fox_logf[0, 2 * c:2 * c + 2]), "spool": A(state_pool[0, 2 * c:2 * c + 2]),
            "csk": A(cache_sb_k[0, 2 * c:2 * c + 2].reshape(2, 4096, D)), "csv": A(cache_sb_v[0, 2 * c:2 * c + 2].reshape(2, 4096, D)),
            "msb": np.ascontiguousarray(msb.reshape(128, 512).astype(bf)), "sel": sel,
        })
        in_maps.append(m)
    res = run_bass_kernel_spmd(nc, in_maps, core_ids=list(range(8))).results
    if DEBUG:
        _LAST["res"] = res
    y_prompt = np.zeros((2, SEQ, D), f32); y_sample = np.zeros((16, 16, D), f32)
    fk_p = np.zeros((1, 2, SEQ, 8, 64), f32); fv_p = np.zeros_like(fk_p); fl_p = np.zeros((1, 2, SEQ, 8), f32)
    pool_p = np.zeros((1, 2, 15, 512), f32); sk_p = np.zeros((1, 2, SEQ, 16, 64), f32); sv_p = np.zeros_like(sk_p)
    fk_s = np.zeros((1, 16, 16, 8, 64), f32); fv_s = np.zeros_like(fk_s); fl_s = np.zeros((1, 16, 16, 8), f32)
    pool_s = np.zeros((1, 16, 15, 512), f32); sk_s = np.zeros((1, 16, 16, 16, 64), f32); sv_s = np.zeros_like(sk_s)
    for c in range(8):
        b, j = c // 4, c % 4
        r = res[c]
        yo = np.asarray(r["y_o"]).reshape(16, 128, D)
        for mm_ in range(16):
            blk = 4 * mm_ + j
            y_prompt[b, blk * 128:(blk + 1) * 128] = yo[mm_]
        y_sample[2 * c:2 * c + 2] = np.asarray(r["ys_o"]).reshape(2, 16, D)
        if j == 0:
            fk_p[0, b] = np.asarray(r["fk_o"]).reshape(SEQ, 8, 64); fv_p[0, b] = np.asarray(r["fv_o"]).reshape(SEQ, 8, 64)
            fl_p[0, b] = np.asarray(r["fl_o"]); pool_p[0, b] = np.asarray(r["pp_o"])[113:128]
            sk_p[0, b] = np.asarray(r["sk_o"]).reshape(SEQ, 16, 64); sv_p[0, b] = np.asarray(r["sv_o"]).reshape(SEQ, 16, 64)
        fk_s[0, 2 * c:2 * c + 2] = np.asarray(r["fks_o"]).reshape(2, 16, 8, 64); fv_s[0, 2 * c:2 * c + 2] = np.asarray(r["fvs_o"]).reshape(2, 16, 8, 64)
        fl_s[0, 2 * c:2 * c + 2] = np.asarray(r["fls_o"]).reshape(2, 16, 8)
        pool_s[0, 2 * c:2 * c + 2] = np.asarray(r["pps_o"]).reshape(2, 16, 512)[:, 1:16]
        sk_s[0, 2 * c:2 * c + 2] = np.asarray(r["sks_o"]).reshape(2, 16, 16, 64); sv_s[0, 2 * c:2 * c + 2] = np.asarray(r["svs_o"]).reshape(2, 16, 16, 64)
    return (y_prompt, y_sample, fk_p, fv_p, fl_p, pool_p, sk_p, sv_p, fk_s, fv_s, fl_s, pool_s, sk_s, sv_s)
```

```python
import contextlib
import numpy as np
import ml_dtypes
from concourse.bass_utils import run_bass_kernel_spmd
import concourse.bass as bass
import concourse.mybir as mybir

F32 = mybir.dt.float32
BF16 = mybir.dt.bfloat16
AF = mybir.ActivationFunctionType
ALU = mybir.AluOpType

DOMS = {
    "pe": "tensor", "act": "scalar", "dve": "vector", "pool": "gpsimd",
    "sp0": "sync", "sp1": "sync", "sp2": "sync", "sp3": "sync",
    "gq0": "gpsimd", "gq1": "gpsimd", "cc": "gpsimd",
}
DMA_DOMS = ("sp0", "sp1", "sp2", "sp3", "gq0", "gq1", "cc")
PHYS = ("tensor", "scalar", "vector", "gpsimd", "sync")


class Sched:
    def __init__(self, nc, cc_inc=16):
        self.nc = nc
        self.stream = {p: [] for p in PHYS}
        self.count = {d: 0 for d in DOMS}
        self.last_w = {}
        self.readers = {}
        self.waited = {p: {} for p in PHYS}
        self.inc = {d: (16 if d in DMA_DOMS else 1) for d in DOMS}
        self.inc["cc"] = cc_inc
        self.rr = 0

    def op(self, dom, fn, reads=(), writes=()):
        phys = DOMS[dom]
        idx = self.count[dom]
        self.count[dom] += 1
        deps = {}

        def add(d, raw=False):
            if d is None:
                return
            d2, i2 = d
            if d2 == dom:
                if dom in ("act", "dve", "pool") and deps.get(d2, -1) < i2:
                    deps[d2] = i2
                return
            if deps.get(d2, -1) < i2:
                deps[d2] = i2

        for r in reads:
            add(self.last_w.get(r), True)
        for w in writes:
            add(self.last_w.get(w), True)
            for d2, i2 in self.readers.get(w, {}).items():
                add((d2, i2))
        waits = []
        wd = self.waited[phys]
        for d2, i2 in deps.items():
            if wd.get(d2, -1) < i2:
                wd[d2] = i2
                waits.append((d2, i2))
        if dom in DMA_DOMS and idx > 0:
            if wd.get(dom, -1) < idx - 1:
                wd[dom] = idx - 1
                waits.append((dom, idx - 1))
        self.stream[phys].append((dom, idx, fn, waits))
        for r in reads:
            rd = self.readers.setdefault(r, {})
            if rd.get(dom, -1) < idx:
                rd[dom] = idx
        for w in writes:
            self.last_w[w] = (dom, idx)
            self.readers[w] = {}
        return (dom, idx)

    def pe(self, fn, reads=(), writes=()):
        return self.op("pe", fn, reads, writes)

    def act(self, fn, reads=(), writes=()):
        return self.op("act", fn, reads, writes)

    def dve(self, fn, reads=(), writes=()):
        return self.op("dve", fn, reads, writes)

    def pool(self, fn, reads=(), writes=()):
        return self.op("pool", fn, reads, writes)

    def dma(self, fn, reads=(), writes=()):
        d = ("sp0", "sp1", "sp2", "sp3")[self.rr % 4]
        self.rr += 1
        return self.op(d, fn, reads, writes)

    def gdma(self, fn, reads=(), writes=()):
        d = ("gq0", "gq1")[self.rr % 2]
        self.rr += 1
        return self.op(d, fn, reads, writes)

    def emit(self, final_waits=True):
        nc = self.nc
        with contextlib.ExitStack() as es:
            sems = {d: es.enter_context(nc.semaphore("s_" + d)) for d in DOMS}
            block = es.enter_context(nc.Block())
            sched = self

            def make(phys):
                def body(eng):
                    for dom, idx, fn, waits in sched.stream[phys]:
                        for d2, i2 in waits:
                            eng.wait_ge(sems[d2], (i2 + 1) * sched.inc[d2])
                        ins = fn(eng)
                        ins.then_inc(sems[dom], sched.inc[dom])
                    if phys == "sync" and final_waits:
                        for d in DOMS:
                            if sched.count[d] > 0:
                                eng.wait_ge(sems[d], sched.count[d] * sched.inc[d])
                return body

            block.tensor(make("tensor"))
            block.scalar(make("scalar"))
            block.vector(make("vector"))
            block.gpsimd(make("gpsimd"))
            block.sync(make("sync"))


D = 1024
SEQ = 8192
NT = 1024
NCH = SEQ // NT
EPS = 1e-6
NEG = -30000.0
SAMPLE_ATT = True
DEBUG = False
DEBUG_OUT = ("fsQ", "fsK", "x1s", "k0T", "v0")
_LAST = {}


def build_program():
    nc = bass.Bass("TRN2", target_bir_lowering=False)
    S = Sched(nc)
    es = contextlib.ExitStack()

    def din(name, shape, dt=F32):
        return nc.dram_tensor(name, list(shape), dt, kind="ExternalInput").ap()

    def dout(name, shape, dt=F32):
        return nc.dram_tensor(name, list(shape), dt, kind="ExternalOutput").ap()

    def dscr(name, shape, dt):
        return nc.dram_tensor(name, list(shape), dt, kind=("ExternalOutput" if (DEBUG and name in DEBUG_OUT) else "Internal")).ap()

    def sb(name, shape, dt):
        return es.enter_context(nc.sbuf_tensor(name, list(shape), dt))

    xp = din("xp", [SEQ, D]); xs = din("xs", [32, D]); cT = din("cT", [128, 24])
    cfk = din("cfk", [2, 4096, 512]); cfv = din("cfv", [2, 4096, 512]); cfl = din("cfl", [2, 4096, 8])
    spool = din("spool", [2, 15, 512]); csk = din("csk", [2, 4096, 1024]); csv = din("csv", [2, 4096, 1024])
    w_ada = din("w_ada", [2, D, 6 * D]); bexp = din("bexp", [128, 2 * 48 * 3]); ngT = din("ngT", [128, 32])
    w_in_ab = din("w_in_ab", [D, 2056]); bfb = din("bfb", [128, 8]); w_pool = din("w_pool", [4, 128, 128])
    pscT = din("pscT", [128, 4]); w_out_ab = din("w_out_ab", [D, D]); w_in_sb = din("w_in_sb", [D, 3 * D])
    w_out_sb = din("w_out_sb", [D, D]); w_up = din("w_up", [2, D, 4 * D]); w_dn = din("w_dn", [2, 4 * D, D])
    fgT = din("fgT", [128, 8])
    identd = din("ident", [128, 128]); maskFd = din("maskF", [128, 128], BF16); msbd = din("msb", [128, 512], BF16)
    trinegd = din("trineg", [128, 128], BF16); seld = din("sel", [128, 4]); rc0d = din("rc0", [128, 64])
    masksd = din("masks", [128, 32], BF16)

    y_o = dout("y_o", [2048, D]); ys_o = dout("ys_o", [32, D])
    fk_o = dout("fk_o", [SEQ, 512]); fv_o = dout("fv_o", [SEQ, 512]); fl_o = dout("fl_o", [SEQ, 8])
    pp_o = dout("pp_o", [128, 512]); sk_o = dout("sk_o", [SEQ, D]); sv_o = dout("sv_o", [SEQ, D])
    fks_o = dout("fks_o", [32, 512]); fvs_o = dout("fvs_o", [32, 512]); fls_o = dout("fls_o", [32, 8])
    pps_o = dout("pps_o", [32, 512]); sks_o = dout("sks_o", [32, D]); svs_o = dout("svs_o", [32, D])

    Wab = dscr("Wab", [D, 2056], BF16); Woab = dscr("Woab", [D, D], BF16); Wsb = dscr("Wsb", [D, 3 * D], BF16)
    Wosb = dscr("Wosb", [D, D], BF16); Wup = dscr("Wup", [2, D, 4 * D], BF16); Wdn = dscr("Wdn", [2, 4 * D, D], BF16)
    Wpl = dscr("Wpl", [4, 128, 128], BF16)
    k0T = dscr("k0T", [512, SEQ + 32], BF16); v0 = dscr("v0", [SEQ + 32, 512], BF16)
    k1T = dscr("k1T", [D, SEQ + 32], BF16); v1 = dscr("v1", [SEQ + 32, D], BF16)
    fsQ = dscr("fsQ", [8, 3, SEQ], BF16); fsK = dscr("fsK", [8, 3, SEQ], BF16)
    x1s = dscr("x1s", [128, 8, SEQ], F32)

    xT = sb("xT", [128, 8, NT], F32); hT = sb("hT", [128, 8, NT], BF16); aT = sb("aT", [128, 8, NT], BF16)
    KA = sb("KA", [128, SEQ], BF16); VA = sb("VA", [128, 64, 65], BF16); QAs = [sb("QA%d" % i, [128, NT], BF16) for i in range(2)]; QA = QAs[0]
    wb = [sb("wb%d" % i, [128, 4096], BF16) for i in range(4)]
    wfb = sb("wfb", [128, 8, 8], BF16); wplb = sb("wplb", [128, 4, 128], BF16)
    ident = sb("identS", [128, 128], F32); identb = sb("identb", [128, 128], BF16)
    maskF = sb("maskFS", [128, 128], BF16); msb = sb("msbS", [128, 512], BF16); trineg = sb("trinegS", [128, 128], BF16)
    masks = sb("masksS", [128, 32], BF16)
    onesb = sb("onesb", [128, 128], BF16); negones = sb("negones", [128, 128], BF16); onesf = sb("onesf", [128, 512], F32)
    sel = sb("selS", [128, 4], F32); rc0 = sb("rc0S", [128, 64], F32); epsb = sb("epsb", [128, 1], F32)
    cTs = sb("cTs", [128, 24], F32); silc = sb("silc", [128, 24], F32); bexps = sb("bexps", [128, 288], F32)
    ngs = sb("ngs", [128, 32], F32); fgs = sb("fgs", [128, 8], F32); pscs = sb("pscs", [128, 4], F32); bfbs = sb("bfbs", [128, 8], F32)
    modT = sb("modT", [128, 2, 48, 3], F32); scl = sb("scl", [128, 2, 2, 8, 3], F32)
    xst = [sb("xst%d" % i, [128, D], F32) for i in range(1)]
    sq = [sb("sq%d" % i, [128, 512], BF16) for i in range(2)]
    rstd = sb("rstd", [128, 512], F32); tmpf = [sb("tmpf%d" % i, [128, 512], F32) for i in range(2)]
    stg = [sb("stg%d" % i, [128, 512], F32) for i in range(2)]
    stb = [sb("stb%d" % i, [128, 512], BF16) for i in range(2)]
    pt = [sb("pt%d" % i, [128, 512], BF16) for i in range(3)]
    Lb = [sb("Lb%d" % i, [128, 512], BF16) for i in range(3)]; Lsum = sb("Lsum", [128, 512], BF16); e1b = [sb("e1b%d" % i, [128, 512], BF16) for i in range(2)]
    lf_tm = sb("lf_tm", [128, 8, 8], F32); fx = sb("fx", [128, 8], F32)
    FT = sb("FT", [8, 512], F32); Fr = sb("Fr", [8, 512], F32); Fcar = sb("Fcar", [8, 1], F32)
    Fs = [sb("Fs%d" % i, [8, 512], BF16) for i in range(3)]
    uext = sb("uext", [128, 528], F32); pa = sb("pa", [128, 528], F32); pb = sb("pb", [128, 528], F32)
    halo = sb("halo", [128, 4, 16], F32); plbs = [sb("plb%d" % i, [128, 512], BF16) for i in range(8)]; t16 = sb("t16", [128, 16], F32)
    rec = sb("rec", [1, 512], F32); osb = sb("osb", [64, 512], F32); osb2 = sb("osb2", [64, 512], F32)
    pbank = [es.enter_context(nc.psum_tensor("pb%d" % i, [128, 512], F32)) for i in range(8)]

    st = {"ps": 0, "pz": 0, "po": 0, "wb": 0, "x": 0, "t": 0, "g": 0, "b": 0, "p": 0, "q": 0, "l": 0, "e": 0, "ev": 0, "pl": 0, "pl3": 0, "gx": 0}

    def rot(key, n):
        st[key] = (st[key] + 1) % n
        return st[key]

    def psg():
        i = rot("ps", 6)
        return pbank[i], "ps%d" % i

    def psl():
        i = rot("pl3", 3)
        return pbank[i], "ps%d" % i

    def psz():
        i = 3 + rot("pz", 3)
        return pbank[i], "ps%d" % i

    def pso():
        i = 6 + rot("po", 2)
        return pbank[i], "ps%d" % i

    def hTn(col):
        return "hT%d" % (col // 512)

    def MM(out, lhsT, rhs, start, stop, reads, writes):
        S.pe(lambda e: e.matmul(out=out, lhsT=lhsT, rhs=rhs, start=start, stop=stop), reads=reads, writes=writes)

    def ld(out, in_, reads, writes):
        S.dma(lambda e: e.dma_start(out=out, in_=in_), reads=reads, writes=writes)

    def stdma(out, in_, reads, writes):
        S.gdma(lambda e: e.dma_start(out=out, in_=in_), reads=reads, writes=writes)

    def wview(i, k):
        return wb[i][:, :].rearrange("p (k n) -> p k n", k=k)

    def load_w(src2d, c0, ncol, rname):
        i = rot("wb", 4)
        v = wview(i, 8)
        ld(v[:, :, 0:ncol], src2d[:, c0:c0 + ncol].rearrange("(k p) n -> p k n", p=128), [rname], ["wb%d" % i])
        return v, "wb%d" % i

    def load_wrows(src2d, r0, rname):
        i = rot("wb", 4)
        v = wview(i, 4)
        ld(v, src2d[r0:r0 + 512, :].rearrange("(k p) n -> p k n", p=128), [rname], ["wb%d" % i])
        return v, "wb%d" % i

    for (t, d, n) in [(ident, identd, "ident"), (maskF, maskFd, "maskF"), (msb, msbd, "msb"), (trineg, trinegd, "trineg"),
                      (sel, seld, "sel"), (rc0, rc0d, "rc0"), (cTs, cT, "cTs"), (bexps, bexp, "bexps"), (ngs, ngT, "ngs"),
                      (fgs, fgT, "fgs"), (pscs, pscT, "pscs"), (bfbs, bfb, "bfbs"), (masks, masksd, "masks")]:
        ld(t[:], d, [], [n])
    S.pool(lambda e: e.memset(onesb[:], 1.0), writes=["onesb"])
    S.pool(lambda e: e.memset(negones[:], -1.0), writes=["negones"])
    S.pool(lambda e: e.memset(onesf[:], 1.0), writes=["onesf"])
    S.pool(lambda e: e.memset(epsb[:], EPS), writes=["epsb"])
    S.pool(lambda e: e.memset(Fcar[:], 0.0), writes=["Fcar"])
    S.pool(lambda e: e.memset(halo[:], 0.0), writes=["halo"])
    S.pool(lambda e: e.memset(VA[:], 1.0), writes=["VA%d" % _i for _i in range(8)])
    S.pool(lambda e: e.memset(KA[64:70, :], 1.0), writes=["KA%d" % _i for _i in range(8)])
    for _q in range(2):
        S.pool(lambda e, _q=_q: e.memset(QAs[_q][64:70, :], 1.0), writes=["QA%d" % _q])
    S.dve(lambda e: e.tensor_copy(out=identb[:], in_=ident[:]), reads=["ident"], writes=["identb"])

    def cast2d(dst, src, rows, name, step=256):
        d2 = dst.rearrange("a b -> (a b)").rearrange("(r c) -> r c", c=1024)
        s2 = src.rearrange("a b -> (a b)").rearrange("(r c) -> r c", c=1024)
        nr = d2.shape[0]
        for r0 in range(0, nr, 256):
            r1 = min(nr, r0 + 256)
            S.gdma(lambda e, r0=r0, r1=r1: e.dma_start(out=d2[r0:r1, :], in_=s2[r0:r1, :]), reads=[], writes=[name])
    cast2d(Wab, w_in_ab, D, "Wab"); cast2d(Woab, w_out_ab, D, "Woab")
    S.gdma(lambda e: e.dma_start(out=Wpl.rearrange("g c e -> (g c) e"), in_=w_pool.rearrange("g c e -> (g c) e")), reads=[], writes=["Wpl"])
    for l in range(2):
        cast2d(Wup[l], w_up[l], D, "Wup%d" % l, 128); cast2d(Wdn[l], w_dn[l], 4 * D, "Wdn%d" % l, 512)
    cast2d(Wsb, w_in_sb, D, "Wsb", 128); cast2d(Wosb, w_out_sb, D, "Wosb")
    ld(wplb[:], Wpl.rearrange("g c e -> c g e"), ["Wpl"], ["wplb"])
    ld(wfb[:], Wab[:, 1536:1544].rearrange("(k p) n -> p k n", p=128), ["Wab"], ["wfb"])

    S.act(lambda e: e.activation(out=silc[:], in_=cTs[:], func=AF.Silu), reads=["cTs"], writes=["silc"])
    for l in range(2):
        bank, bn = psg()
        for ct in range(24):
            ab = wb[ct % 2][:, :].bitcast(F32).rearrange("p (k n) -> p k n", k=8); an = "wb%d" % (ct % 2)
            ld(ab, w_ada[l][:, ct * 256:(ct + 1) * 256].rearrange("(k p) n -> p k n", p=128), [], [an])
            for f2 in range(2):
                fc = ct * 2 + f2
                for kc in range(8):
                    MM(bank[:, fc * 3:fc * 3 + 3], ab[:, kc, f2 * 128:(f2 + 1) * 128], silc[:, kc * 3:kc * 3 + 3],
                       kc == 0, kc == 7, [an, "silc"], [bn])
        S.dve(lambda e, l=l, bank=bank: e.tensor_tensor(out=modT[:, l].rearrange("p f s -> p (f s)"), in0=bank[:, 0:144],
                                                        in1=bexps[:, l * 144:(l + 1) * 144], op=ALU.add),
              reads=[bn, "bexps"], writes=["modT"])
        for w in range(2):
            for c in range(8):
                S.dve(lambda e, l=l, w=w, c=c: e.tensor_scalar(out=scl[:, l, w, c, :], in0=modT[:, l, (8 if w == 0 else 32) + c, :],
                                                               scalar1=1.0, scalar2=ngs[:, (l * 2 + w) * 8 + c:(l * 2 + w) * 8 + c + 1],
                                                               op0=ALU.add, op1=ALU.mult),
                      reads=["modT", "ngs"], writes=["scl"])

    def mod(l, k, c, s):
        return modT[:, l, k * 8 + c, s:s + 1]

    def groups(ncols):
        return [(a, min(a + 512, ncols)) for a in range(0, ncols, 512)]

    def segsplit(a, b, segs):
        return [(max(a, sa), min(b, sb_), s) for (sa, sb_, s) in segs if max(a, sa) < min(b, sb_)]

    def load_x(src, ntok):
        XB = [(tmpf[0], "tmpf0"), (tmpf[1], "tmpf1"), (stg[0], "stg0"), (stg[1], "stg1")]
        for t0 in range(0, ntok, 128):
            n = min(128, ntok - t0)
            for c4 in range(2):
                buf, bufn = XB[rot("gx", 4)]
                ld(buf[0:n, :], src[t0:t0 + n, c4 * 512:(c4 + 1) * 512], [], [bufn])
                bank, bn = psg()
                for cc in range(4):
                    S.pe(lambda e, bank=bank, cc=cc, buf=buf, n=n: e.transpose(out=bank[:, cc * 128:cc * 128 + n], in_=buf[0:n, cc * 128:(cc + 1) * 128],
                                                                               identity=ident[0:n, 0:n]),
                         reads=[bufn, "ident"], writes=[bn])
                outv = xT[:, c4 * 4:c4 * 4 + 4, t0:t0 + n]
                inv = bank[:, :].rearrange("p (c t) -> p c t", c=4)[:, :, 0:n]
                if c4 == 0:
                    S.act(lambda e, outv=outv, inv=inv: e.activation(out=outv, in_=inv, func=AF.Copy), reads=[bn], writes=["xT"])
                else:
                    S.dve(lambda e, outv=outv, inv=inv: e.tensor_copy(out=outv, in_=inv), reads=[bn], writes=["xT"])

    def rms_rstd(a, b):
        bank, bn = psg()
        for c in range(8):
            i = rot("q", 2)
            S.act(lambda e, c=c, i=i: e.activation(out=sq[i][:, 0:b - a], in_=xT[:, c, a:b], func=AF.Square), reads=["xT"], writes=["sq%d" % i])
            MM(bank[:, 0:b - a], onesb[:, :], sq[i][:, 0:b - a], c == 0, c == 7, ["sq%d" % i, "onesb"], [bn])
        S.act(lambda e, bank=bank: e.activation(out=rstd[:, 0:b - a], in_=bank[:, 0:b - a], func=AF.Sqrt, bias=epsb[:, 0:1], scale=1.0 / D),
              reads=[bn, "epsb"], writes=["rstd"])
        S.dve(lambda e: e.reciprocal(out=rstd[:, 0:b - a], in_=rstd[:, 0:b - a]), reads=["rstd"], writes=["rstd"])

    def norm_mod(l, w, ncols, segs):
        for (a, b) in groups(ncols):
            rms_rstd(a, b)
            for c in range(8):
                for (sa, sb_, s) in segsplit(a, b, segs):
                    i = rot("t", 2)
                    S.dve(lambda e, c=c, sa=sa, sb_=sb_, s=s, i=i, a=a: e.scalar_tensor_tensor(
                        out=tmpf[i][:, 0:sb_ - sa], in0=xT[:, c, sa:sb_], scalar=scl[:, l, w, c, s:s + 1], in1=rstd[:, sa - a:sb_ - a],
                        op0=ALU.mult, op1=ALU.mult), reads=["xT", "scl", "rstd"], writes=["tmpf%d" % i])
                    S.act(lambda e, c=c, sa=sa, sb_=sb_, s=s, i=i: e.activation(
                        out=hT[:, c, sa:sb_], in_=tmpf[i][:, 0:sb_ - sa], func=AF.Identity, bias=mod(l, 0 if w == 0 else 3, c, s)),
                        reads=["tmpf%d" % i, "modT"], writes=[hTn(sa)])

    def final_norm_store(ncols, dst):
        for (a, b) in groups(ncols):
            rms_rstd(a, b)
            for t0 in range(a, b, 128):
                n = min(128, b - t0)
                i = rot("x", 1)
                for c4 in range(2):
                    bank, bn = psg()
                    for cc in range(4):
                        c = c4 * 4 + cc
                        j = rot("t", 2)
                        S.dve(lambda e, c=c, t0=t0, n=n, j=j, a=a: e.scalar_tensor_tensor(
                            out=tmpf[j][:, 0:n], in0=xT[:, c, t0:t0 + n], scalar=fgs[:, c:c + 1], in1=rstd[:, t0 - a:t0 - a + n],
                            op0=ALU.mult, op1=ALU.mult), reads=["xT", "fgs", "rstd"], writes=["tmpf%d" % j])
                        S.pe(lambda e, bank=bank, cc=cc, j=j, n=n: e.transpose(out=bank[0:n, cc * 128:(cc + 1) * 128], in_=tmpf[j][:, 0:n], identity=ident[:, :]),
                             reads=["tmpf%d" % j, "ident"], writes=[bn])
                    k = rot("g", 2)
                    S.act(lambda e, bank=bank, k=k, n=n: e.activation(out=stg[k][0:n, :], in_=bank[0:n, :], func=AF.Copy),
                          reads=[bn], writes=["stg%d" % k])
                    stdma(dst[t0:t0 + n, c4 * 512:(c4 + 1) * 512], stg[k][0:n, :], ["stg%d" % k], [])

    def tokmajor(ncols, wlist, out_d, scr, row0, vrow0):
        for t0 in range(0, ncols, 128):
            n = min(128, ncols - t0)
            for wi, (wt, wn) in enumerate(wlist):
                bank, bn = psg()
                for kc in range(8):
                    MM(bank[0:n, :], hT[:, kc, t0:t0 + n], wt[:, kc, :], kc == 0, kc == 7, [hTn(t0), wn], [bn])
                i = rot("g", 2)
                if rot("ev", 2) == 0:
                    S.act(lambda e, bank=bank, i=i, n=n: e.activation(out=stg[i][0:n, :], in_=bank[0:n, :], func=AF.Copy), reads=[bn], writes=["stg%d" % i])
                else:
                    S.dve(lambda e, bank=bank, i=i, n=n: e.tensor_copy(out=stg[i][0:n, :], in_=bank[0:n, :]), reads=[bn], writes=["stg%d" % i])
                stdma(out_d[row0 + t0:row0 + t0 + n, wi * 512:(wi + 1) * 512], stg[i][0:n, :], ["stg%d" % i], [])
                if scr is not None:
                    j = rot("b", 2)
                    (S.pool if j == 0 else S.dve)(lambda e, i=i, j=j, n=n: e.tensor_copy(out=stb[j][0:n, 0:512], in_=stg[i][0:n, :]), reads=["stg%d" % i], writes=["stb%d" % j])
                    stdma(scr[vrow0 + t0:vrow0 + t0 + n, wi * 512:(wi + 1) * 512], stb[j][0:n, 0:512], ["stb%d" % j], ["vscr"])

    def logf_proj(ncols, nf, row0):
        for t0 in range(0, ncols, 128):
            n = min(128, ncols - t0)
            bank, bn = psg()
            for kc in range(8):
                MM(bank[0:n, 0:8], hT[:, kc, t0:t0 + n], wfb[:, kc, :], kc == 0, kc == 7, [hTn(t0), "wfb"], [bn])
            tt = t0 // 128
            S.dve(lambda e, bank=bank, n=n: e.tensor_tensor(out=fx[0:n, :], in0=bank[0:n, 0:8], in1=bfbs[0:n, :], op=ALU.add), reads=[bn, "bfbs"], writes=["fx"])
            S.act(lambda e, n=n: e.activation(out=fx[0:n, :], in_=fx[0:n, :], func=AF.Exp, scale=-1.0), reads=["fx"], writes=["fx"])
            S.act(lambda e, n=n: e.activation(out=fx[0:n, :], in_=fx[0:n, :], func=AF.Ln, bias=onesf[0:n, 0:1]), reads=["fx"], writes=["fx"])
            S.dve(lambda e, n=n, tt=tt: e.tensor_scalar(out=lf_tm[0:n, tt, :], in0=fx[0:n, :], scalar1=-1.0, scalar2=None, op0=ALU.mult), reads=["fx"], writes=["lf_tm"])
            stdma(nf[row0 + t0:row0 + t0 + n, :], lf_tm[0:n, tt, :], ["lf_tm"], [])

    def kT_proj(ncols, wlist, kscr, col0):
        for wi, (wt, wn) in enumerate(wlist):
            for fc in range(4):
                for (a, b) in groups(ncols):
                    j = rot("b", 2)
                    bank, bn = psg()
                    for kc in range(8):
                        MM(bank[:, 0:b - a], wt[:, kc, fc * 128:(fc + 1) * 128], hT[:, kc, a:b], kc == 0, kc == 7, [hTn(a), wn], [bn])
                    if rot("ev", 2) == 0:
                        S.act(lambda e, bank=bank, a=a, b=b, j=j: e.activation(out=stb[j][:, 0:b - a], in_=bank[:, 0:b - a], func=AF.Copy), reads=[bn], writes=["stb%d" % j])
                    else:
                        S.dve(lambda e, bank=bank, a=a, b=b, j=j: e.tensor_copy(out=stb[j][:, 0:b - a], in_=bank[:, 0:b - a]), reads=[bn], writes=["stb%d" % j])
                    r0 = wi * 512 + fc * 128
                    stdma(kscr[r0:r0 + 128, col0 + a:col0 + b], stb[j][:, 0:b - a], ["stb%d" % j], ["kscr"])

    def q_proj(ncols, wt, wn, hcol, scale, qi=0):
        QA = QAs[qi]
        for (a, b) in groups(ncols):
            bank, bn = psl()
            for kc in range(8):
                MM(bank[0:64, 0:b - a], wt[:, kc, hcol:hcol + 64], hT[:, kc, a:b], kc == 0, kc == 7, [hTn(a), wn], [bn])
            S.act(lambda e, bank=bank, a=a, b=b: e.activation(out=QA[0:64, a:b], in_=bank[0:64, 0:b - a], func=AF.Copy, scale=scale), reads=[bn], writes=["QA%d" % qi])

    def linear_res(ncols, segs, wsrc, wname, l, gk):
        for wi in range(2):
            wt, wn = load_w(wsrc, wi * 512, 512, wname)
            for oc in range(4):
                c = wi * 4 + oc
                for (a, b) in groups(ncols):
                    bank, bn = psg()
                    for kc in range(8):
                        MM(bank[:, 0:b - a], wt[:, kc, oc * 128:(oc + 1) * 128], aT[:, kc, a:b], kc == 0, kc == 7, ["aT", wn], [bn])
                    for (sa, sb_, s) in segsplit(a, b, segs):
                        S.dve(lambda e, bank=bank, c=c, sa=sa, sb_=sb_, s=s, a=a: e.scalar_tensor_tensor(
                            out=xT[:, c, sa:sb_], in0=bank[:, sa - a:sb_ - a], scalar=mod(l, gk, c, s), in1=xT[:, c, sa:sb_], op0=ALU.mult, op1=ALU.add),
                            reads=[bn, "modT", "xT"], writes=["xT"])

    def mlp(ncols, segs, l):
        for ff2 in range(4):
            wus = [load_w(Wup[l], ff2 * 1024 + hh * 512, 512, "Wup%d" % l) for hh in range(2)]
            for hh, (wu, wun) in enumerate(wus):
                for fc in range(4):
                    for (a, b) in groups(ncols):
                        bank, bn = psg()
                        for kc in range(8):
                            MM(bank[:, 0:b - a], wu[:, kc, fc * 128:(fc + 1) * 128], hT[:, kc, a:b], kc == 0, kc == 7, [hTn(a), wun], [bn])
                        i = rot("t", 2)
                        S.act(lambda e, bank=bank, a=a, b=b, i=i: e.activation(out=tmpf[i][:, 0:b - a], in_=bank[:, 0:b - a], func=AF.Relu), reads=[bn], writes=["tmpf%d" % i])
                        S.dve(lambda e, fc=fc, hh=hh, a=a, b=b, i=i: e.tensor_tensor(out=aT[:, hh * 4 + fc, a:b], in0=tmpf[i][:, 0:b - a], in1=tmpf[i][:, 0:b - a], op=ALU.mult),
                              reads=["tmpf%d" % i], writes=["aT"])
            wds = [load_wrows(Wdn[l], ff2 * 1024 + hh * 512, "Wdn%d" % l) for hh in range(2)]
            for c in range(8):
                for (a, b) in groups(ncols):
                    bank, bn = psg()
                    for f8 in range(8):
                        wd, wdn = wds[f8 // 4]
                        MM(bank[:, 0:b - a], wd[:, f8 % 4, c * 128:(c + 1) * 128], aT[:, f8, a:b], f8 == 0, f8 == 7, ["aT", wdn], [bn])
                    for (sa, sb_, s) in segsplit(a, b, segs):
                        S.dve(lambda e, bank=bank, c=c, sa=sa, sb_=sb_, s=s, a=a: e.scalar_tensor_tensor(
                            out=xT[:, c, sa:sb_], in0=bank[:, sa - a:sb_ - a], scalar=mod(l, 5, c, s), in1=xT[:, c, sa:sb_], op0=ALU.mult, op1=ALU.add),
                            reads=[bn, "modT", "xT"], writes=["xT"])

    def store_head(h, a, b, src_ap, srcn):
        c, p0 = h // 2, (h % 2) * 64
        S.act(lambda e: e.activation(out=aT[p0:p0 + 64, c, a:b], in_=src_ap, func=AF.Copy), reads=[srcn], writes=["aT"])

    def fox_finish_a(h, a, b, ob, obn):
        n = b - a
        S.act(lambda e: e.activation(out=rec[0:1, 0:n], in_=ob[64:65, 0:n], func=AF.Copy), reads=[obn], writes=["rec"])
        S.dve(lambda e: e.reciprocal(out=rec[0:1, 0:n], in_=rec[0:1, 0:n]), reads=["rec"], writes=["rec"])
        S.act(lambda e: e.activation(out=osb[:, 0:n], in_=ob[0:64, 0:n], func=AF.Copy), reads=[obn], writes=["osb"])

    def fox_finish_b(h, a, b, ob, obn):
        n = b - a
        bank, bn = psg()
        MM(bank[0:64, 0:n], onesf[0:1, 0:64], rec[0:1, 0:n], True, True, ["onesf", "rec"], [bn])
        S.dve(lambda e: e.tensor_tensor(out=osb2[:, 0:n], in0=osb[:, 0:n], in1=bank[0:64, 0:n], op=ALU.mult), reads=["osb", bn], writes=["osb2"])
        store_head(h, a, b, osb2[:, 0:n], "osb2")

    def fox_finish(h, a, b, ob, obn):
        fox_finish_a(h, a, b, ob, obn)
        fox_finish_b(h, a, b, ob, obn)

    def pipeline(stages, T, skews=None, hook=None, hook_at=2, hook2=None, hook2_at=9, hook3=None, hook3_at=4):
        if skews is None:
            skews = list(range(len(stages)))
        for step in range(T + max(skews)):
            if hook is not None and step == hook_at:
                hook()
            if hook2 is not None and step == hook2_at:
                hook2()
            if hook3 is not None and step == hook3_at:
                hook3()
            for fn, k in zip(stages, skews):
                t = step - k
                if 0 <= t < T:
                    fn(t)
        if hook is not None and T + max(skews) <= hook_at:
            hook()
        if hook2 is not None and T + max(skews) <= hook2_at:
            hook2()
        if hook3 is not None and T + max(skews) <= hook3_at:
            hook3()

    pending = []
    pending_b = []

    def flush_pending_a():
        while pending:
            fa, fb = pending.pop(0)
            fa()
            pending_b.append(fb)

    def flush_pending_b():
        while pending_b:
            pending_b.pop(0)()

    def flush_pending():
        flush_pending_a()
        flush_pending_b()

    def fox_prompt(h, ci, qi=0, next_q=None):
        QA = QAs[qi]
        nk = (ci + 1) * NT
        ld(QA[67:70, 0:NT], fsQ[h, :, ci * NT:(ci + 1) * NT], ["fsQ"], ["QA%d" % qi])
        for pc in range(nk // 1024):
            ka, kb = pc * 1024, (pc + 1) * 1024
            ld(KA[0:64, ka:kb], k0T[h * 64:(h + 1) * 64, ka:kb], ["kscr"], ["KA%d" % pc])
            ld(KA[64:67, ka:kb], fsK[h, :, ka:kb], ["fsK"], ["KA%d" % pc])
            ld(VA[:, pc * 8:pc * 8 + 8, 0:64], v0[ka:kb, h * 64:(h + 1) * 64].rearrange("(k p) d -> p k d", p=128), ["vscr"], ["VA%d" % pc])
        for g in range(2):
            qa = g * 512
            ob, obn = pso()
            nfull = 8 * ci + 4 * g
            tiles = [(kt, 0, None) for kt in range(nfull)] + [(nfull + i, i * 128, i) for i in range(4)]
            T = len(tiles)
            stt = [None] * T

            def A(t, tiles=tiles, stt=stt, qa=qa):
                kt, c0, di = tiles[t]
                zb, zbn = psz()
                MM(zb[:, c0:512], KA[0:70, kt * 128:(kt + 1) * 128], QA[0:70, qa + c0:qa + 512], True, di is None, ["KA%d" % (kt // 8), "QA%d" % qi], [zbn])
                if di is not None:
                    MM(zb[:, c0:c0 + 128], identb[:, :], maskF[:, :], False, True, ["identb", "maskF"], [zbn])
                i = rot("p", 3)
                S.act(lambda e, zb=zb, c0=c0, i=i: e.activation(out=pt[i][:, c0:512], in_=zb[:, c0:512], func=AF.Exp), reads=[zbn], writes=["pt%d" % i])
                stt[t] = i

            def C(t, tiles=tiles, stt=stt, ob=ob, obn=obn, T=T):
                kt, c0, di = tiles[t]
                i = stt[t]
                MM(ob[0:65, c0:512], VA[:, kt, 0:65], pt[i][:, c0:512], t == 0, t == T - 1, ["VA%d" % (kt // 8), "pt%d" % i], [obn])

            pipeline([A, C], T, [0, 2], hook=flush_pending_a, hook2=flush_pending_b, hook3=(next_q if g == 1 else None))
            pending.append((lambda h=h, qa=qa, ob=ob, obn=obn: fox_finish_a(h, qa, qa + 512, ob, obn),
                            lambda h=h, qa=qa, ob=ob, obn=obn: fox_finish_b(h, qa, qa + 512, ob, obn)))

    def pool_group(g, tg_cols, ucols, first16, wu, wun, out_cols):
        a, b = ucols
        n = b - a
        pi = rot("pl", 8)
        plb = plbs[pi]; plbn = "plb%d" % pi
        w = 2 ** (g + 1)
        bank, bn = psg()
        for kc in range(8):
            MM(bank[:, 0:n], wu[:, kc, g * 128:(g + 1) * 128], hT[:, kc, a:b], kc == 0, kc == 7, [hTn(a), wun], [bn])
        S.pool(lambda e: e.tensor_copy(out=uext[:, 0:16], in_=halo[:, g, :]), reads=["halo"], writes=["uext"])
        S.dve(lambda e: e.tensor_copy(out=uext[:, 16:16 + n], in_=bank[:, 0:n]), reads=[bn], writes=["uext"])
        S.pool(lambda e: e.tensor_copy(out=halo[:, g, :], in_=uext[:, n:n + 16]), reads=["uext"], writes=["halo"])
        E = 16 + n
        src, srcn = uext, "uext"
        bufs = [(pa, "pa"), (pb, "pb")]
        sh = 1
        for stp in range(g + 1):
            dst, dstn = bufs[stp % 2]
            lo = 2 * sh - 1
            S.pool(lambda e, dst=dst, src=src, lo=lo, sh=sh: e.tensor_tensor(out=dst[:, lo:E], in0=src[:, lo:E], in1=src[:, lo - sh:E - sh], op=ALU.add),
                   reads=[srcn], writes=[dstn])
            src, srcn = dst, dstn
            sh *= 2
        S.dve(lambda e, src=src: e.scalar_tensor_tensor(out=plb[:, 0:n], in0=src[:, 16:E], scalar=1.0 / w, in1=uext[:, 16:E], op0=ALU.mult, op1=ALU.subtract),
               reads=[srcn, "uext"], writes=[plbn])
        if first16:
            S.pool(lambda e, src=src: e.tensor_tensor(out=t16[:, :], in0=src[:, 16:32], in1=rc0[:, g * 16:(g + 1) * 16], op=ALU.mult), reads=[srcn, "rc0"], writes=["t16"])
            S.pool(lambda e: e.tensor_tensor(out=plb[:, 0:16], in0=t16[:, :], in1=uext[:, 16:32], op=ALU.subtract), reads=["t16", "uext", plbn], writes=[plbn])
        def part2():
            bank2, bn2 = psg()
            MM(bank2[:, 0:n], wplb[:, g, :], plb[:, 0:n], True, True, ["wplb", plbn], [bn2])
            oa, ob_ = out_cols
            S.act(lambda e: e.activation(out=aT[:, 4 + g, oa:ob_], in_=bank2[:, 0:n], func=AF.Copy, scale=pscs[:, g:g + 1]), reads=[bn2, "pscs"], writes=["aT"])
        return part2

    def f_rows(ncols, col0, ntiles):
        for half in range((ntiles + 3) // 4):
            bank, bn = psg()
            nt_ = min(4, ntiles - half * 4)
            for t in range(nt_):
                S.pe(lambda e, bank=bank, t=t, half=half: e.transpose(out=bank[0:8, t * 128:(t + 1) * 128], in_=lf_tm[:, half * 4 + t, :], identity=ident[:, :]),
                     reads=["lf_tm", "ident"], writes=[bn])
            n = nt_ * 128
            c0 = col0 + half * 512
            S.act(lambda e, bank=bank, n=n: e.activation(out=Fr[:, 0:n], in_=bank[0:8, 0:n], func=AF.Copy), reads=[bn], writes=["Fr"])
            S.dve(lambda e, n=n: e.tensor_tensor_scan(out=FT[:, 0:n], data0=onesf[0:8, 0:n], data1=Fr[:, 0:n],
                                                      initial=0.0, op0=ALU.mult, op1=ALU.add),
                  reads=["Fr", "onesf"], writes=["FT"])
            S.dve(lambda e, n=n: e.tensor_scalar(out=FT[:, 0:n], in0=FT[:, 0:n], scalar1=Fcar[:, 0:1], scalar2=None, op0=ALU.add),
                  reads=["FT", "Fcar"], writes=["FT"])
            S.pool(lambda e, n=n: e.tensor_copy(out=Fcar[:, 0:1], in_=FT[:, n - 1:n]), reads=["FT"], writes=["Fcar"])
            S.dve(lambda e, n=n: e.tensor_copy(out=Fs[0][:, 0:n], in_=FT[:, 0:n]), reads=["FT"], writes=["Fs"])
            S.dve(lambda e, n=n: e.tensor_tensor(out=Fr[:, 0:n], in0=FT[:, 0:n], in1=Fs[0][:, 0:n], op=ALU.subtract), reads=["FT", "Fs"], writes=["Fr"])
            S.dve(lambda e, n=n: e.tensor_copy(out=Fs[1][:, 0:n], in_=Fr[:, 0:n]), reads=["Fr"], writes=["Fs"])
            S.dve(lambda e, n=n: e.tensor_tensor(out=Fr[:, 0:n], in0=Fr[:, 0:n], in1=Fs[1][:, 0:n], op=ALU.subtract), reads=["Fr", "Fs"], writes=["Fr"])
            S.dve(lambda e, n=n: e.tensor_copy(out=Fs[2][:, 0:n], in_=Fr[:, 0:n]), reads=["Fr"], writes=["Fs"])
            for i in range(3):
                stdma(fsQ[:, i, c0:c0 + n], Fs[i][:, 0:n], ["Fs"], ["fsQ"])
            for i in range(3):
                S.dve(lambda e, i=i, n=n: e.tensor_scalar(out=Fs[i][:, 0:n], in0=Fs[i][:, 0:n], scalar1=-1.0, scalar2=None, op0=ALU.mult), reads=["Fs"], writes=["Fs"])
            for i in range(3):
                stdma(fsK[:, i, c0:c0 + n], Fs[i][:, 0:n], ["Fs"], ["fsK"])

    PSEG = [(0, NT, 0)]

    def l0_chunk(ci):
        load_x(xp[ci * NT:(ci + 1) * NT, :], NT)
        norm_mod(0, 0, NT, PSEG)
        wu, wun = load_w(Wab, 1544, 512, "Wab")
        wk, wkn = load_w(Wab, 512, 512, "Wab"); wv, wvn = load_w(Wab, 1024, 512, "Wab")
        p2s = []
        for g in range(4):
            for (a, b) in groups(NT):
                p2s.append(pool_group(g, None, (a, b), ci == 0 and a == 0, wu, wun, (a, b)))
        tokmajor(NT, [(wk, wkn)], fk_o, None, ci * NT, 0)
        tokmajor(NT, [(wv, wvn)], fv_o, v0, ci * NT, ci * NT)
        logf_proj(NT, fl_o, ci * NT)
        kT_proj(NT, [(wk, wkn)], k0T, ci * NT)
        f_rows(NT, ci * NT, 8)
        for p2 in p2s:
            p2()
        if ci == NCH - 1:
            bank, bn = psg()
            for kc in range(8):
                MM(bank[:, :], hT[:, kc, NT - 128:NT], wu[:, kc, :], kc == 0, kc == 7, [hTn(NT - 128), wun], [bn])
            S.act(lambda e, bank=bank: e.activation(out=stg[0][:, :], in_=bank[:, :], func=AF.Copy), reads=[bn], writes=["stg0"])
            stdma(pp_o, stg[0][:, :], ["stg0"], [])
        wq, wqn = load_w(Wab, 0, 512, "Wab")
        q_proj(NT, wq, wqn, 0, 0.125, 0)
        for h in range(8):
            nq = (lambda h=h: q_proj(NT, wq, wqn, (h + 1) * 64, 0.125, (h + 1) % 2)) if h < 7 else None
            fox_prompt(h, ci, h % 2, nq)
        flush_pending()
        linear_res(NT, PSEG, Woab, "Woab", 0, 2)
        norm_mod(0, 1, NT, PSEG)
        mlp(NT, PSEG, 0)
        stdma(x1s[:, :, ci * NT:(ci + 1) * NT], xT[:, :, :], ["xT"], ["x1s"])
        norm_mod(1, 0, NT, PSEG)
        wk1 = [load_w(Wsb, 1024, 512, "Wsb"), load_w(Wsb, 1536, 512, "Wsb")]
        tokmajor(NT, wk1, sk_o, None, ci * NT, 0)
        kT_proj(NT, wk1, k1T, ci * NT)
        wv1 = [load_w(Wsb, 2048, 512, "Wsb"), load_w(Wsb, 2560, 512, "Wsb")]
        tokmajor(NT, wv1, sv_o, v1, ci * NT, ci * NT)

    for ci in range(NCH):
        l0_chunk(ci)

    def sb_group(h, g, qa, qi=0, next_q=None):
        QA = QAs[qi]
        ob, obn = pso()
        S.pool(lambda e: e.memset(Lsum[:, :], 0.0), writes=["Lsum"])
        kts = list(range(16 * g + 15, -1, -1))
        T = len(kts)
        stt = [None] * T

        def A(t):
            kt = kts[t]
            if kt >= 16 * g:
                i0_, em = (kt - 16 * g) // 4, (kt - 16 * g) % 4
                c0 = i0_ * 128
            else:
                i0_, em, c0 = None, None, 0
            zb, zbn = psz()
            MM(zb[:, c0:512], KA[0:64, kt * 128:(kt + 1) * 128], QA[0:64, qa + c0:qa + 512], True, False, ["KA%d" % (kt // 8), "QA%d" % qi], [zbn])
            if i0_ is not None:
                MM(zb[:, c0:c0 + 128], identb[:, :], msb[:, em * 128:(em + 1) * 128], False, False, ["identb", "msb"], [zbn])
            i = rot("e", 2)
            S.act(lambda e, zb=zb, c0=c0, i=i: e.activation(out=e1b[i][:, c0:512], in_=zb[:, c0:512], func=AF.Exp), reads=[zbn], writes=["e1b%d" % i])
            j = rot("l", 3)
            S.act(lambda e, c0=c0, i=i, j=j: e.activation(out=Lb[j][:, c0:512], in_=e1b[i][:, c0:512], func=AF.Ln, bias=onesf[:, 0:1]), reads=["e1b%d" % i], writes=["Lb%d" % j])
            stt[t] = [zb, zbn, c0, j, None]

        def B(t):
            zb, zbn, c0, j, _ = stt[t]
            MM(zb[:, c0:512], trineg[:, :], Lb[j][:, c0:512], False, False, ["trineg", "Lb%d" % j], [zbn])
            MM(zb[:, c0:512], negones[:, :], Lsum[:, c0:512], False, True, ["negones", "Lsum"], [zbn])
            k = rot("p", 3)
            S.act(lambda e, zb=zb, c0=c0, k=k: e.activation(out=pt[k][:, c0:512], in_=zb[:, c0:512], func=AF.Exp), reads=[zbn], writes=["pt%d" % k])
            S.dve(lambda e, c0=c0, j=j: e.tensor_tensor(out=Lsum[:, c0:512], in0=Lsum[:, c0:512], in1=Lb[j][:, c0:512], op=ALU.add), reads=["Lsum", "Lb%d" % j], writes=["Lsum"])
            stt[t][4] = k

        def C(t):
            zb, zbn, c0, j, k = stt[t]
            MM(ob[0:64, c0:512], VA[:, kts[t], 0:64], pt[k][:, c0:512], t == 0, t == T - 1, ["VA%d" % (kts[t] // 8), "pt%d" % k], [obn])

        pipeline([A, B, C], T, hook3=next_q)
        store_head(h, qa, qa + 512, ob[0:64, 0:512], obn)

    def l1_own(oc):
        for mb in range(8):
            m = oc * 8 + mb
            for r in range(4):
                blk = 4 * m + r
                for hf in range(2):
                    buf, bufn = [(tmpf[0], "tmpf0"), (tmpf[1], "tmpf1"), (stg[0], "stg0"), (stg[1], "stg1")][rot("gx", 4)]
                    inv = buf[:, :].rearrange("p (c t) -> p c t", c=4)
                    ld(inv, x1s[:, hf * 4:(hf + 1) * 4, blk * 128:(blk + 1) * 128], ["x1s"], [bufn])
                    outv = xT[:, hf * 4:(hf + 1) * 4, mb * 128:(mb + 1) * 128]
                    if r == 0:
                        S.dve(lambda e, outv=outv, inv=inv, r=r: e.tensor_scalar(out=outv, in0=inv, scalar1=sel[:, r:r + 1], scalar2=None, op0=ALU.mult),
                              reads=[bufn, "sel"], writes=["xT"])
                    else:
                        S.dve(lambda e, outv=outv, inv=inv, r=r: e.scalar_tensor_tensor(out=outv, in0=inv, scalar=sel[:, r:r + 1], in1=outv, op0=ALU.mult, op1=ALU.add),
                              reads=[bufn, "sel", "xT"], writes=["xT"])
        norm_mod(1, 0, NT, PSEG)
        wq1 = [load_w(Wsb, 0, 512, "Wsb"), load_w(Wsb, 512, 512, "Wsb")]
        q_proj(NT, wq1[0][0], wq1[0][1], 0, 0.125, 0)
        for h in range(16):
            nq = (lambda h=h: q_proj(NT, wq1[(h + 1) // 8][0], wq1[(h + 1) // 8][1], ((h + 1) % 8) * 64, 0.125, (h + 1) % 2)) if h < 15 else None
            npc = 4 * oc + 4
            for pc in range(npc - 1, -1, -1):
                ka, kb = pc * 1024, (pc + 1) * 1024
                ld(KA[0:64, ka:kb], k1T[h * 64:(h + 1) * 64, ka:kb], ["kscr"], ["KA%d" % pc])
                ld(VA[:, pc * 8:pc * 8 + 8, 0:64], v1[ka:kb, h * 64:(h + 1) * 64].rearrange("(k p) d -> p k d", p=128), ["vscr"], ["VA%d" % pc])
            for gl in (1, 0):
                sb_group(h, 2 * oc + gl, gl * 512, h % 2, nq if gl == 0 else None)
        linear_res(NT, PSEG, Wosb, "Wosb", 1, 2)
        norm_mod(1, 1, NT, PSEG)
        mlp(NT, PSEG, 1)
        final_norm_store(NT, y_o[oc * NT:(oc + 1) * NT, :])

    for oc in range(2):
        l1_own(oc)

    SSEG = [(0, 16, 1), (16, 32, 2)]

    def sample_cache_T(src, h, hd_cols, nkeys=4096):
        for k0 in range(0, 32, 8):
            i = rot("b", 2)
            S.gdma(lambda e, k0=k0, i=i: e.dma_start(out=stb[i][:, 0:512].rearrange("p (k d) -> p k d", k=8),
                                                     in_=src[k0 * 128:(k0 + 8) * 128, h * 64:(h + 1) * 64].rearrange("(k p) d -> p k d", p=128)),
                   reads=[], writes=["stb%d" % i])
            bank, bn = psl()
            bankb = bank[:, :].bitcast(BF16)
            for t in range(8):
                S.pe(lambda e, bankb=bankb, t=t, i=i: e.transpose(out=bankb[0:64, t * 128:(t + 1) * 128], in_=stb[i][:, t * 64:(t + 1) * 64], identity=identb[:, :]),
                     reads=["stb%d" % i, "identb"], writes=[bn])
            S.act(lambda e, bankb=bankb, k0=k0: e.activation(out=KA[0:64, k0 * 128:(k0 + 8) * 128], in_=bankb[0:64, :], func=AF.Copy), reads=[bn], writes=["KA%d" % (k0 // 8)])

    def sample_cache_V(src, h):
        for k0 in range(0, 32, 8):
            S.gdma(lambda e, k0=k0: e.dma_start(out=VA[:, k0:k0 + 8, 0:64], in_=src[k0 * 128:(k0 + 8) * 128, h * 64:(h + 1) * 64].rearrange("(k p) d -> p k d", p=128)),
                   reads=[], writes=["VA%d" % (k0 // 8)])

    def sample_fox(h, sq_):
        qa = sq_ * 16
        sample_cache_T(cfk[sq_], h, None)
        ld(KA[0:64, 4096:4112], k0T[h * 64:(h + 1) * 64, SEQ + qa:SEQ + qa + 16], ["kscr"], ["KA4"])
        sample_cache_V(cfv[sq_], h)
        ld(VA[0:16, 32, 0:64], v0[SEQ + qa:SEQ + qa + 16, h * 64:(h + 1) * 64], ["vscr"], ["VA4"])
        ob, obn = pso()
        stt = [None] * 33

        def A(kt):
            n = 128 if kt < 32 else 16
            zb, zbn = psz()
            MM(zb[0:n, 0:16], KA[0:64, kt * 128:kt * 128 + n], QA[0:64, qa:qa + 16], True, kt < 32, ["KA%d" % (kt // 8), "QA0"], [zbn])
            if kt == 32:
                MM(zb[0:16, 0:16], identb[0:16, 0:16], masks[0:16, 0:16], False, True, ["identb", "masks"], [zbn])
            i = rot("p", 3)
            S.act(lambda e, zb=zb, n=n, i=i, kt=kt: e.activation(out=pt[i][0:n, 0:16], in_=zb[0:n, 0:16], func=AF.Exp,
                                                                 bias=sbias[0:n, (sq_ * 8 + h) * 33 + kt:(sq_ * 8 + h) * 33 + kt + 1]),
                  reads=[zbn, "sbias"], writes=["pt%d" % i])
            stt[kt] = i

        def C(kt):
            n = 128 if kt < 32 else 16
            i = stt[kt]
            MM(ob[0:65, 0:16], VA[0:n, kt, 0:65], pt[i][0:n, 0:16], kt == 0, kt == 32, ["VA%d" % (kt // 8), "pt%d" % i], [obn])

        pipeline([A, C], 33)
        fox_finish(h, qa, qa + 16, ob, obn)

    def sample_sb(h, sq_):
        qa = sq_ * 16
        sample_cache_T(csk[sq_], h, None)
        ld(KA[0:64, 4096:4112], k1T[h * 64:(h + 1) * 64, SEQ + qa:SEQ + qa + 16], ["kscr"], ["KA4"])
        sample_cache_V(csv[sq_], h)
        ld(VA[0:16, 32, 0:64], v1[SEQ + qa:SEQ + qa + 16, h * 64:(h + 1) * 64], ["vscr"], ["VA4"])
        ob, obn = pso()
        S.pool(lambda e: e.memset(Lsum[:, 0:16], 0.0), writes=["Lsum"])
        kts = list(range(32, -1, -1))
        stt = [None] * 33

        def A(t):
            kt = kts[t]
            n = 128 if kt < 32 else 16
            zb, zbn = psz()
            MM(zb[0:n, 0:16], KA[0:64, kt * 128:kt * 128 + n], QA[0:64, qa:qa + 16], True, False, ["KA%d" % (kt // 8), "QA0"], [zbn])
            if kt == 32:
                MM(zb[0:16, 0:16], identb[0:16, 0:16], masks[0:16, 16:32], False, False, ["identb", "masks"], [zbn])
            i = rot("t", 2)
            S.act(lambda e, zb=zb, n=n, i=i: e.activation(out=tmpf[i][0:n, 0:16], in_=zb[0:n, 0:16], func=AF.Exp), reads=[zbn], writes=["tmpf%d" % i])
            j = rot("l", 3)
            S.act(lambda e, n=n, i=i, j=j: e.activation(out=Lb[j][0:n, 0:16], in_=tmpf[i][0:n, 0:16], func=AF.Ln, bias=onesf[0:n, 0:1]), reads=["tmpf%d" % i], writes=["Lb%d" % j])
            stt[t] = [zb, zbn, n, j, None]

        def B(t):
            zb, zbn, n, j, _ = stt[t]
            MM(zb[0:n, 0:16], trineg[0:n, 0:n], Lb[j][0:n, 0:16], False, False, ["trineg", "Lb%d" % j], [zbn])
            MM(zb[0:n, 0:16], negones[:, 0:n], Lsum[:, 0:16], False, True, ["negones", "Lsum"], [zbn])
            k = rot("p", 3)
            S.act(lambda e, zb=zb, n=n, k=k: e.activation(out=pt[k][0:n, 0:16], in_=zb[0:n, 0:16], func=AF.Exp), reads=[zbn], writes=["pt%d" % k])
            S.dve(lambda e, n=n, j=j: e.tensor_tensor(out=Lsum[0:n, 0:16], in0=Lsum[0:n, 0:16], in1=Lb[j][0:n, 0:16], op=ALU.add), reads=["Lsum", "Lb%d" % j], writes=["Lsum"])
            stt[t][4] = k

        def C(t):
            zb, zbn, n, j, k = stt[t]
            MM(ob[0:64, 0:16], VA[0:n, kts[t], 0:64], pt[k][0:n, 0:16], t == 0, t == 32, ["VA%d" % (kts[t] // 8), "pt%d" % k], [obn])

        pipeline([A, B, C], 33)
        store_head(h, qa, qa + 16, ob[0:64, 0:16], obn)

    sbias = sb("sbias", [128, 2 * 8 * 33], F32)
    clf = sb("clf", [128, 32, 8], F32); csum = sb("csum", [128, 256], F32); ctot = sb("ctot", [128, 256], F32); cpre = sb("cpre", [128, 264], F32)
    triinc = sb("triincS", [128, 128], F32)
    triincd = din("triinc", [128, 128])
    ld(triinc[:], triincd, [], ["triinc"])

    def sample_layer0():
        load_x(xs, 32)
        norm_mod(0, 0, 32, SSEG)
        wk, wkn = load_w(Wab, 512, 512, "Wab"); wv, wvn = load_w(Wab, 1024, 512, "Wab")
        tokmajor(32, [(wk, wkn)], fks_o, None, 0, 0)
        tokmajor(32, [(wv, wvn)], fvs_o, v0, 0, SEQ)
        logf_proj(32, fls_o, 0)
        ld(lfs[0][:, :], lf_tm[0:16, 0, :], ["lf_tm"], ["lfs"])
        ld(lfs[1][:, :], lf_tm[16:32, 0, :], ["lf_tm"], ["lfs"])
        kT_proj(32, [(wk, wkn)], k0T, SEQ)
        wu, wun = load_w(Wab, 1544, 512, "Wab")
        bank, bn = psg()
        for kc in range(8):
            MM(bank[0:32, :], hT[:, kc, 0:32], wu[:, kc, :], kc == 0, kc == 7, [hTn(0), wun], [bn])
        S.act(lambda e, bank=bank: e.activation(out=stg[0][0:32, :], in_=bank[0:32, :], func=AF.Copy), reads=[bn], writes=["stg0"])
        stdma(pps_o, stg[0][0:32, :], ["stg0"], [])
        for sq_ in range(2):
            sps = stg[1]
            ld(sps[0:15, :], spool[sq_], [], ["stg1"])
            bank, bn = psg()
            for g in range(4):
                S.pe(lambda e, bank=bank, g=g: e.transpose(out=bank[:, g * 16 + 1:g * 16 + 16], in_=sps[0:15, g * 128:(g + 1) * 128], identity=ident[0:15, 0:15]),
                     reads=["stg1", "ident"], writes=[bn])
            S.pool(lambda e: e.memset(halo[:], 0.0), writes=["halo"])
            S.dve(lambda e, bank=bank: e.tensor_copy(out=halo[:, :, 1:16], in_=bank[:, 0:64].rearrange("p (g t) -> p g t", g=4)[:, :, 1:16]), reads=[bn], writes=["halo"])
            for g in range(4):
                pool_group(g, None, (sq_ * 16, sq_ * 16 + 16), False, wu, wun, (sq_ * 16, sq_ * 16 + 16))()
        if SAMPLE_ATT:
            for sq_ in range(2):
                ld(clf[:], cfl[sq_].rearrange("(k p) h -> p k h", p=128), [], ["clf"])
                bank, bn = psg()
                MM(bank[:, 0:256], triinc[:, :], clf[:].rearrange("p k h -> p (k h)"), True, True, ["triinc", "clf"], [bn])
                bank2, bn2 = psg()
                MM(bank2[:, 0:256], onesf[:, 0:128], clf[:].rearrange("p k h -> p (k h)"), True, True, ["onesf", "clf"], [bn2])
                S.act(lambda e, bank=bank: e.activation(out=csum[:, :], in_=bank[:, 0:256], func=AF.Copy), reads=[bn], writes=["csum"])
                S.act(lambda e, bank2=bank2: e.activation(out=ctot[:, :], in_=bank2[:, 0:256], func=AF.Copy), reads=[bn2], writes=["ctot"])
                S.pool(lambda e: e.memset(cpre[:, 248:264], 0.0), writes=["cpre"])
                for kt in range(30, -1, -1):
                    S.pool(lambda e, kt=kt: e.tensor_tensor(out=cpre[:, kt * 8:(kt + 1) * 8], in0=cpre[:, (kt + 1) * 8:(kt + 2) * 8], in1=ctot[:, (kt + 1) * 8:(kt + 2) * 8], op=ALU.add),
                           reads=["cpre", "ctot"], writes=["cpre"])
                for h in range(8):
                    col = (sq_ * 8 + h) * 33
                    S.pool(lambda e, h=h, col=col: e.tensor_tensor(out=sbias[:, col:col + 32], in0=ctot[:, :].rearrange("p (k h) -> p h k", h=8)[:, h, :],
                                                                   in1=csum[:, :].rearrange("p (k h) -> p h k", h=8)[:, h, :], op=ALU.subtract),
                           reads=["ctot", "csum"], writes=["sbias"])
                    S.pool(lambda e, h=h, col=col: e.tensor_tensor(out=sbias[:, col:col + 32], in0=sbias[:, col:col + 32],
                                                                   in1=cpre[:, 0:256].rearrange("p (k h) -> p h k", h=8)[:, h, :], op=ALU.add),
                           reads=["sbias", "cpre"], writes=["sbias"])
                bank3, bn3 = psg()
                MM(bank3[0:16, 0:8], triinc[0:16, 0:16], lfs[sq_][0:16, :], True, True, ["triinc", "lfs"], [bn3])
                for h in range(8):
                    col = (sq_ * 8 + h) * 33 + 32
                    S.dve(lambda e, h=h, col=col, bank3=bank3: e.tensor_scalar(out=sbias[0:16, col:col + 1], in0=bank3[0:16, h:h + 1], scalar1=-1.0, scalar2=None, op0=ALU.mult),
                          reads=[bn3], writes=["sbias"])
            wq, wqn = load_w(Wab, 0, 512, "Wab")
            for sq_ in range(2):
                for h in range(8):
                    q_proj(32, wq, wqn, h * 64, 0.125)
                    sample_fox(h, sq_)
        linear_res(32, SSEG, Woab, "Woab", 0, 2)
        norm_mod(0, 1, 32, SSEG)
        mlp(32, SSEG, 0)

    lfs = [sb("lfs%d" % i, [16, 8], F32) for i in range(2)]

    def sample_layer1():
        norm_mod(1, 0, 32, SSEG)
        wk1 = [load_w(Wsb, 1024, 512, "Wsb"), load_w(Wsb, 1536, 512, "Wsb")]
        tokmajor(32, wk1, sks_o, None, 0, 0)
        kT_proj(32, wk1, k1T, SEQ)
        wv1 = [load_w(Wsb, 2048, 512, "Wsb"), load_w(Wsb, 2560, 512, "Wsb")]
        tokmajor(32, wv1, svs_o, v1, 0, SEQ)
        if SAMPLE_ATT:
            wq1 = [load_w(Wsb, 0, 512, "Wsb"), load_w(Wsb, 512, 512, "Wsb")]
            for sq_ in range(2):
                for h in range(16):
                    wt, wn = wq1[h // 8]
                    q_proj(32, wt, wn, (h % 8) * 64, 0.125)
                    sample_sb(h, sq_)
        if DEBUG:
            dbg_a1 = dout("dbg_a1", [128, 8, 32], BF16)
            stdma(dbg_a1, aT[:, :, 0:32], ["aT"], [])
            dbg_h1 = dout("dbg_h1", [128, 8, 32], BF16)
            stdma(dbg_h1, hT[:, :, 0:32], ["hT0"], [])
        linear_res(32, SSEG, Wosb, "Wosb", 1, 2)
        norm_mod(1, 1, 32, SSEG)
        mlp(32, SSEG, 1)
        if DEBUG:
            dbg_x2 = dout("dbg_x2", [128, 8, 32], F32)
            stdma(dbg_x2, xT[:, :, 0:32], ["xT"], [])
        final_norm_store(32, ys_o)

    sample_layer0_pre = None
    sample_layer0()
    sample_layer1()

    S.emit()
    es.close()
    return nc


_NC = None


def kernel(x_prompt, x_sample, c_prompt, c_sample, cache_fox_k, cache_fox_v, cache_fox_logf, state_pool,
           cache_sb_k, cache_sb_v, w_ada, b_ada, norm_g, w_in_ab, b_forget, w_pool, pool_scale, w_out_ab,
           w_in_sb, w_out_sb, w_up, w_down, final_g):
    global _NC
    f32 = np.float32
    bf = ml_dtypes.bfloat16
    A = lambda a: np.ascontiguousarray(np.asarray(a, dtype=f32))
    x_prompt, x_sample, c_prompt, c_sample = A(x_prompt), A(x_sample), A(c_prompt), A(c_sample)
    cache_fox_k, cache_fox_v, cache_fox_logf, state_pool = A(cache_fox_k), A(cache_fox_v), A(cache_fox_logf), A(state_pool)
    cache_sb_k, cache_sb_v = A(cache_sb_k), A(cache_sb_v)
    w_ada, b_ada, norm_g, w_in_ab, b_forget, w_pool = A(w_ada), A(b_ada), A(norm_g), A(w_in_ab), A(b_forget), A(w_pool)
    pool_scale, w_out_ab, w_in_sb, w_out_sb, w_up, w_down, final_g = A(pool_scale), A(w_out_ab), A(w_in_sb), A(w_out_sb), A(w_up), A(w_down), A(final_g)
    if _NC is None:
        _NC = build_program()
    nc = _NC
    kk = np.arange(128)[:, None]; qq = np.arange(128)[None, :]
    ident = np.eye(128, dtype=f32)
    maskF = np.where(kk <= qq, 0.0, NEG).astype(f32)
    mstrict = np.where(kk < qq, 0.0, NEG).astype(f32)
    trineg = np.where(kk >= qq, -1.0, 0.0).astype(f32)
    triinc = np.where(kk <= qq, 1.0, 0.0).astype(f32)
    masks = np.full((128, 32), NEG, f32)
    masks[:16, 0:16] = maskF[:16, :16]; masks[:16, 16:32] = mstrict[:16, :16]
    rc0 = np.zeros((128, 64), f32)
    for g in range(4):
        for pos in range(16):
            rc0[:, g * 16 + pos] = 1.0 / min(pos + 1, 2 ** (g + 1))
    bexp = np.repeat(b_ada.reshape(2, 48, 128).transpose(2, 0, 1)[..., None], 3, axis=-1).reshape(128, 288)
    ngT = norm_g.reshape(2, 2, 8, 128).transpose(3, 0, 1, 2).reshape(128, 32)
    bfb = np.tile(b_forget.reshape(1, 8), (128, 1))
    pscT = pool_scale.reshape(4, 128).T
    fgT = final_g.reshape(8, 128).T
    common = {
        "w_ada": w_ada, "bexp": A(bexp), "ngT": A(ngT), "w_in_ab": w_in_ab[0], "bfb": A(bfb), "w_pool": w_pool[0],
        "pscT": A(pscT), "w_out_ab": w_out_ab[0], "w_in_sb": w_in_sb[0], "w_out_sb": w_out_sb[0], "w_up": w_up, "w_dn": w_down,
        "fgT": A(fgT), "ident": ident, "maskF": maskF.astype(bf), "trineg": trineg.astype(bf), "rc0": rc0,
        "masks": masks.astype(bf), "triinc": triinc,
    }
    in_maps = []
    for c in range(8):
        b, j = c // 4, c % 4
        call = np.stack([c_prompt[b], c_sample[2 * c], c_sample[2 * c + 1]])
        cT = call.reshape(3, 8, 128).transpose(2, 1, 0).reshape(128, 24)
        msb = np.zeros((128, 4, 128), f32)
        for e in range(4):
            msb[:, e, :] = 0.0 if e < j else (mstrict if e == j else NEG)
        sel = np.zeros((128, 4), f32); sel[:, j] = 1.0
        m = dict(common)
        m.update({
            "xp": x_prompt[b], "xs": A(x_sample[2 * c:2 * c + 2].reshape(32, D)), "cT": A(cT),
            "cfk": A(cache_fox_k[0, 2 * c:2 * c + 2].reshape(2, 4096, 512)), "cfv": A(cache_fox_v[0, 2 * c:2 * c + 2].reshape(2, 4096, 512)),
            "cfl": A(cache_fox_logf[0, 2 * c:2 * c + 2]), "spool": A(state_pool[0, 2 * c:2 * c + 2]),
            "csk": A(cache_sb_k[0, 2 * c:2 * c + 2].reshape(2, 4096, D)), "csv": A(cache_sb_v[0, 2 * c:2 * c + 2].reshape(2, 4096, D)),
            "msb": np.ascontiguousarray(msb.reshape(128, 512).astype(bf)), "sel": sel,
        })
        in_maps.append(m)
    res = run_bass_kernel_spmd(nc, in_maps, core_ids=list(range(8))).results
    if DEBUG:
        _LAST["res"] = res
    y_prompt = np.zeros((2, SEQ, D), f32); y_sample = np.zeros((16, 16, D), f32)
    fk_p = np.zeros((1, 2, SEQ, 8, 64), f32); fv_p = np.zeros_like(fk_p); fl_p = np.zeros((1, 2, SEQ, 8), f32)
    pool_p = np.zeros((1, 2, 15, 512), f32); sk_p = np.zeros((1, 2, SEQ, 16, 64), f32); sv_p = np.zeros_like(sk_p)
    fk_s = np.zeros((1, 16, 16, 8, 64), f32); fv_s = np.zeros_like(fk_s); fl_s = np.zeros((1, 16, 16, 8), f32)
    pool_s = np.zeros((1, 16, 15, 512), f32); sk_s = np.zeros((1, 16, 16, 16, 64), f32); sv_s = np.zeros_like(sk_s)
    for c in range(8):
        b, j = c // 4, c % 4
        r = res[c]
        yo = np.asarray(r["y_o"]).reshape(16, 128, D)
        for mm_ in range(16):
            blk = 4 * mm_ + j
            y_prompt[b, blk * 128:(blk + 1) * 128] = yo[mm_]
        y_sample[2 * c:2 * c + 2] = np.asarray(r["ys_o"]).reshape(2, 16, D)
        if j == 0:
            fk_p[0, b] = np.asarray(r["fk_o"]).reshape(SEQ, 8, 64); fv_p[0, b] = np.asarray(r["fv_o"]).reshape(SEQ, 8, 64)
            fl_p[0, b] = np.asarray(r["fl_o"]); pool_p[0, b] = np.asarray(r["pp_o"])[113:128]
            sk_p[0, b] = np.asarray(r["sk_o"]).reshape(SEQ, 16, 64); sv_p[0, b] = np.asarray(r["sv_o"]).reshape(SEQ, 16, 64)
        fk_s[0, 2 * c:2 * c + 2] = np.asarray(r["fks_o"]).reshape(2, 16, 8, 64); fv_s[0, 2 * c:2 * c + 2] = np.asarray(r["fvs_o"]).reshape(2, 16, 8, 64)
        fl_s[0, 2 * c:2 * c + 2] = np.asarray(r["fls_o"]).reshape(2, 16, 8)
        pool_s[0, 2 * c:2 * c + 2] = np.asarray(r["pps_o"]).reshape(2, 16, 512)[:, 1:16]
        sk_s[0, 2 * c:2 * c + 2] = np.asarray(r["sks_o"]).reshape(2, 16, 16, 64); sv_s[0, 2 * c:2 * c + 2] = np.asarray(r["svs_o"]).reshape(2, 16, 16, 64)
    return (y_prompt, y_sample, fk_p, fv_p, fl_p, pool_p, sk_p, sv_p, fk_s, fv_s, fl_s, pool_s, sk_s, sv_s)
```

```python
import contextlib
import numpy as np
import ml_dtypes
from concourse.bass_utils import run_bass_kernel_spmd
import concourse.bass as bass
import concourse.mybir as mybir

F32 = mybir.dt.float32
BF16 = mybir.dt.bfloat16
AF = mybir.ActivationFunctionType
ALU = mybir.AluOpType

DOMS = {
    "pe": "tensor", "act": "scalar", "dve": "vector", "pool": "gpsimd",
    "sp0": "sync", "sp1": "sync", "sp2": "sync", "sp3": "sync",
    "gq0": "gpsimd", "gq1": "gpsimd", "cc": "gpsimd",
}
DMA_DOMS = ("sp0", "sp1", "sp2", "sp3", "gq0", "gq1", "cc")
PHYS = ("tensor", "scalar", "vector", "gpsimd", "sync")


class Sched:
    def __init__(self, nc, cc_inc=16):
        self.nc = nc
        self.stream = {p: [] for p in PHYS}
        self.count = {d: 0 for d in DOMS}
        self.last_w = {}
        self.readers = {}
        self.waited = {p: {} for p in PHYS}
        self.inc = {d: (16 if d in DMA_DOMS else 1) for d in DOMS}
        self.inc["cc"] = cc_inc
        self.rr = 0

    def op(self, dom, fn, reads=(), writes=()):
        phys = DOMS[dom]
        idx = self.count[dom]
        self.count[dom] += 1
        deps = {}

        def add(d, raw=False):
            if d is None:
                return
            d2, i2 = d
            if d2 == dom:
                if dom in ("act", "dve", "pool") and deps.get(d2, -1) < i2:
                    deps[d2] = i2
                return
            if deps.get(d2, -1) < i2:
                deps[d2] = i2

        for r in reads:
            add(self.last_w.get(r), True)
        for w in writes:
            add(self.last_w.get(w), True)
            for d2, i2 in self.readers.get(w, {}).items():
                add((d2, i2))
        waits = []
        wd = self.waited[phys]
        for d2, i2 in deps.items():
            if wd.get(d2, -1) < i2:
                wd[d2] = i2
                waits.append((d2, i2))
        if dom in DMA_DOMS and idx > 0:
            if wd.get(dom, -1) < idx - 1:
                wd[dom] = idx - 1
                waits.append((dom, idx - 1))
        self.stream[phys].append((dom, idx, fn, waits))
        for r in reads:
            rd = self.readers.setdefault(r, {})
            if rd.get(dom, -1) < idx:
                rd[dom] = idx
        for w in writes:
            self.last_w[w] = (dom, idx)
            self.readers[w] = {}
        return (dom, idx)

    def pe(self, fn, reads=(), writes=()):
        return self.op("pe", fn, reads, writes)

    def act(self, fn, reads=(), writes=()):
        return self.op("act", fn, reads, writes)

    def dve(self, fn, reads=(), writes=()):
        return self.op("dve", fn, reads, writes)

    def pool(self, fn, reads=(), writes=()):
        return self.op("pool", fn, reads, writes)

    def dma(self, fn, reads=(), writes=()):
        d = ("sp0", "sp1", "sp2", "sp3")[self.rr % 4]
        self.rr += 1
        return self.op(d, fn, reads, writes)

    def gdma(self, fn, reads=(), writes=()):
        d = ("gq0", "gq1")[self.rr % 2]
        self.rr += 1
        return self.op(d, fn, reads, writes)

    def emit(self, final_waits=True):
        nc = self.nc
        with contextlib.ExitStack() as es:
            sems = {d: es.enter_context(nc.semaphore("s_" + d)) for d in DOMS}
            block = es.enter_context(nc.Block())
            sched = self

            def make(phys):
                def body(eng):
                    for dom, idx, fn, waits in sched.stream[phys]:
                        for d2, i2 in waits:
                            eng.wait_ge(sems[d2], (i2 + 1) * sched.inc[d2])
                        ins = fn(eng)
                        ins.then_inc(sems[dom], sched.inc[dom])
                    if phys == "sync" and final_waits:
                        for d in DOMS:
                            if sched.count[d] > 0:
                                eng.wait_ge(sems[d], sched.count[d] * sched.inc[d])
                return body

            block.tensor(make("tensor"))
            block.scalar(make("scalar"))
            block.vector(make("vector"))
            block.gpsimd(make("gpsimd"))
            block.sync(make("sync"))


D = 1024
SEQ = 8192
NT = 1024
NCH = SEQ // NT
EPS = 1e-6
NEG = -30000.0
SAMPLE_ATT = True
DEBUG = False
DEBUG_OUT = ("fsQ", "fsK", "x1s", "k0T", "v0")
_LAST = {}


def build_program():
    nc = bass.Bass("TRN2", target_bir_lowering=False)
    S = Sched(nc)
    es = contextlib.ExitStack()

    def din(name, shape, dt=F32):
        return nc.dram_tensor(name, list(shape), dt, kind="ExternalInput").ap()

    def dout(name, shape, dt=F32):
        return nc.dram_tensor(name, list(shape), dt, kind="ExternalOutput").ap()

    def dscr(name, shape, dt):
        return nc.dram_tensor(name, list(shape), dt, kind=("ExternalOutput" if (DEBUG and name in DEBUG_OUT) else "Internal")).ap()

    def sb(name, shape, dt):
        return es.enter_context(nc.sbuf_tensor(name, list(shape), dt))

    xp = din("xp", [SEQ, D]); xs = din("xs", [32, D]); cT = din("cT", [128, 24])
    cfk = din("cfk", [2, 4096, 512]); cfv = din("cfv", [2, 4096, 512]); cfl = din("cfl", [2, 4096, 8])
    spool = din("spool", [2, 15, 512]); csk = din("csk", [2, 4096, 1024]); csv = din("csv", [2, 4096, 1024])
    w_ada = din("w_ada", [2, D, 6 * D]); bexp = din("bexp", [128, 2 * 48 * 3]); ngT = din("ngT", [128, 32])
    w_in_ab = din("w_in_ab", [D, 2056]); bfb = din("bfb", [128, 8]); w_pool = din("w_pool", [4, 128, 128])
    pscT = din("pscT", [128, 4]); w_out_ab = din("w_out_ab", [D, D]); w_in_sb = din("w_in_sb", [D, 3 * D])
    w_out_sb = din("w_out_sb", [D, D]); w_up = din("w_up", [2, D, 4 * D]); w_dn = din("w_dn", [2, 4 * D, D])
    fgT = din("fgT", [128, 8])
    identd = din("ident", [128, 128]); maskFd = din("maskF", [128, 128], BF16); msbd = din("msb", [128, 512], BF16)
    trinegd = din("trineg", [128, 128], BF16); seld = din("sel", [128, 4]); rc0d = din("rc0", [128, 64])
    masksd = din("masks", [128, 32], BF16)

    y_o = dout("y_o", [2048, D]); ys_o = dout("ys_o", [32, D])
    fk_o = dout("fk_o", [SEQ, 512]); fv_o = dout("fv_o", [SEQ, 512]); fl_o = dout("fl_o", [SEQ, 8])
    pp_o = dout("pp_o", [128, 512]); sk_o = dout("sk_o", [SEQ, D]); sv_o = dout("sv_o", [SEQ, D])
    fks_o = dout("fks_o", [32, 512]); fvs_o = dout("fvs_o", [32, 512]); fls_o = dout("fls_o", [32, 8])
    pps_o = dout("pps_o", [32, 512]); sks_o = dout("sks_o", [32, D]); svs_o = dout("svs_o", [32, D])

    Wab = dscr("Wab", [D, 2056], BF16); Woab = dscr("Woab", [D, D], BF16); Wsb = dscr("Wsb", [D, 3 * D], BF16)
    Wosb = dscr("Wosb", [D, D], BF16); Wup = dscr("Wup", [2, D, 4 * D], BF16); Wdn = dscr("Wdn", [2, 4 * D, D], BF16)
    Wpl = dscr("Wpl", [4, 128, 128], BF16)
    k0T = dscr("k0T", [512, SEQ + 32], BF16); v0 = dscr("v0", [SEQ + 32, 512], BF16)
    k1T = dscr("k1T", [D, SEQ + 32], BF16); v1 = dscr("v1", [SEQ + 32, D], BF16)
    fsQ = dscr("fsQ", [8, 3, SEQ], BF16); fsK = dscr("fsK", [8, 3, SEQ], BF16)
    x1s = dscr("x1s", [128, 8, SEQ], F32)

    xT = sb("xT", [128, 8, NT], F32); hT = sb("hT", [128, 8, NT], BF16); aT = sb("aT", [128, 8, NT], BF16)
    KA = sb("KA", [128, SEQ], BF16); VA = sb("VA", [128, 64, 65], BF16); QAs = [sb("QA%d" % i, [128, NT], BF16) for i in range(2)]; QA = QAs[0]
    wb = [sb("wb%d" % i, [128, 4096], BF16) for i in range(4)]
    wfb = sb("wfb", [128, 8, 8], BF16); wplb = sb("wplb", [128, 4, 128], BF16)
    ident = sb("identS", [128, 128], F32); identb = sb("identb", [128, 128], BF16)
    maskF = sb("maskFS", [128, 128], BF16); msb = sb("msbS", [128, 512], BF16); trineg = sb("trinegS", [128, 128], BF16)
    masks = sb("masksS", [128, 32], BF16)
    onesb = sb("onesb", [128, 128], BF16); negones = sb("negones", [128, 128], BF16); onesf = sb("onesf", [128, 512], F32)
    sel = sb("selS", [128, 4], F32); rc0 = sb("rc0S", [128, 64], F32); epsb = sb("epsb", [128, 1], F32)
    cTs = sb("cTs", [128, 24], F32); silc = sb("silc", [128, 24], F32); bexps = sb("bexps", [128, 288], F32)
    ngs = sb("ngs", [128, 32], F32); fgs = sb("fgs", [128, 8], F32); pscs = sb("pscs", [128, 4], F32); bfbs = sb("bfbs", [128, 8], F32)
    modT = sb("modT", [128, 2, 48, 3], F32); scl = sb("scl", [128, 2, 2, 8, 3], F32)
    xst = [sb("xst%d" % i, [128, D], F32) for i in range(1)]
    sq = [sb("sq%d" % i, [128, 512], BF16) for i in range(2)]
    rstd = sb("rstd", [128, 512], F32); tmpf = [sb("tmpf%d" % i, [128, 512], F32) for i in range(2)]
    stg = [sb("stg%d" % i, [128, 512], F32) for i in range(2)]
    stb = [sb("stb%d" % i, [128, 512], BF16) for i in range(2)]
    pt = [sb("pt%d" % i, [128, 512], BF16) for i in range(3)]
    Lb = [sb("Lb%d" % i, [128, 512], BF16) for i in range(3)]; Lsum = sb("Lsum", [128, 512], BF16); e1b = [sb("e1b%d" % i, [128, 512], BF16) for i in range(2)]
    lf_tm = sb("lf_tm", [128, 8, 8], F32); fx = sb("fx", [128, 8], F32)
    FT = sb("FT", [8, 512], F32); Fr = sb("Fr", [8, 512], F32); Fcar = sb("Fcar", [8, 1], F32)
    Fs = [sb("Fs%d" % i, [8, 512], BF16) for i in range(3)]
    uext = sb("uext", [128, 528], F32); pa = sb("pa", [128, 528], F32); pb = sb("pb", [128, 528], F32)
    halo = sb("halo", [128, 4, 16], F32); plbs = [sb("plb%d" % i, [128, 512], BF16) for i in range(8)]; t16 = sb("t16", [128, 16], F32)
    rec = sb("rec", [1, 512], F32); osb = sb("osb", [64, 512], F32); osb2 = sb("osb2", [64, 512], F32)
    pbank = [es.enter_context(nc.psum_tensor("pb%d" % i, [128, 512], F32)) for i in range(8)]

    st = {"ps": 0, "pz": 0, "po": 0, "wb": 0, "x": 0, "t": 0, "g": 0, "b": 0, "p": 0, "q": 0, "l": 0, "e": 0, "ev": 0, "pl": 0, "pl3": 0, "gx": 0}

    def rot(key, n):
        st[key] = (st[key] + 1) % n
        return st[key]

    def psg():
        i = rot("ps", 6)
        return pbank[i], "ps%d" % i

    def psl():
        i = rot("pl3", 3)
        return pbank[i], "ps%d" % i

    def psz():
        i = 3 + rot("pz", 3)
        return pbank[i], "ps%d" % i

    def pso():
        i = 6 + rot("po", 2)
        return pbank[i], "ps%d" % i

    def hTn(col):
        return "hT%d" % (col // 512)

    def MM(out, lhsT, rhs, start, stop, reads, writes):
        S.pe(lambda e: e.matmul(out=out, lhsT=lhsT, rhs=rhs, start=start, stop=stop), reads=reads, writes=writes)

    def ld(out, in_, reads, writes):
        S.dma(lambda e: e.dma_start(out=out, in_=in_), reads=reads, writes=writes)

    def stdma(out, in_, reads, writes):
        S.gdma(lambda e: e.dma_start(out=out, in_=in_), reads=reads, writes=writes)

    def wview(i, k):
        return wb[i][:, :].rearrange("p (k n) -> p k n", k=k)

    def load_w(src2d, c0, ncol, rname):
        i = rot("wb", 4)
        v = wview(i, 8)
        ld(v[:, :, 0:ncol], src2d[:, c0:c0 + ncol].rearrange("(k p) n -> p k n", p=128), [rname], ["wb%d" % i])
        return v, "wb%d" % i

    def load_wrows(src2d, r0, rname):
        i = rot("wb", 4)
        v = wview(i, 4)
        ld(v, src2d[r0:r0 + 512, :].rearrange("(k p) n -> p k n", p=128), [rname], ["wb%d" % i])
        return v, "wb%d" % i

    for (t, d, n) in [(ident, identd, "ident"), (maskF, maskFd, "maskF"), (msb, msbd, "msb"), (trineg, trinegd, "trineg"),
                      (sel, seld, "sel"), (rc0, rc0d, "rc0"), (cTs, cT, "cTs"), (bexps, bexp, "bexps"), (ngs, ngT, "ngs"),
                      (fgs, fgT, "fgs"), (pscs, pscT, "pscs"), (bfbs, bfb, "bfbs"), (masks, masksd, "masks")]:
        ld(t[:], d, [], [n])
    S.pool(lambda e: e.memset(onesb[:], 1.0), writes=["onesb"])
    S.pool(lambda e: e.memset(negones[:], -1.0), writes=["negones"])
    S.pool(lambda e: e.memset(onesf[:], 1.0), writes=["onesf"])
    S.pool(lambda e: e.memset(epsb[:], EPS), writes=["epsb"])
    S.pool(lambda e: e.memset(Fcar[:], 0.0), writes=["Fcar"])
    S.pool(lambda e: e.memset(halo[:], 0.0), writes=["halo"])
    S.pool(lambda e: e.memset(VA[:], 1.0), writes=["VA%d" % _i for _i in range(8)])
    S.pool(lambda e: e.memset(KA[64:70, :], 1.0), writes=["KA%d" % _i for _i in range(8)])
    for _q in range(2):
        S.pool(lambda e, _q=_q: e.memset(QAs[_q][64:70, :], 1.0), writes=["QA%d" % _q])
    S.dve(lambda e: e.tensor_copy(out=identb[:], in_=ident[:]), reads=["ident"], writes=["identb"])

    def cast2d(dst, src, rows, name, step=256):
        d2 = dst.rearrange("a b -> (a b)").rearrange("(r c) -> r c", c=1024)
        s2 = src.rearrange("a b -> (a b)").rearrange("(r c) -> r c", c=1024)
        nr = d2.shape[0]
        for r0 in range(0, nr, 256):
            r1 = min(nr, r0 + 256)
            S.gdma(lambda e, r0=r0, r1=r1: e.dma_start(out=d2[r0:r1, :], in_=s2[r0:r1, :]), reads=[], writes=[name])
    cast2d(Wab, w_in_ab, D, "Wab"); cast2d(Woab, w_out_ab, D, "Woab")
    S.gdma(lambda e: e.dma_start(out=Wpl.rearrange("g c e -> (g c) e"), in_=w_pool.rearrange("g c e -> (g c) e")), reads=[], writes=["Wpl"])
    for l in range(2):
        cast2d(Wup[l], w_up[l], D, "Wup%d" % l, 128); cast2d(Wdn[l], w_dn[l], 4 * D, "Wdn%d" % l, 512)
    cast2d(Wsb, w_in_sb, D, "Wsb", 128); cast2d(Wosb, w_out_sb, D, "Wosb")
    ld(wplb[:], Wpl.rearrange("g c e -> c g e"), ["Wpl"], ["wplb"])
    ld(wfb[:], Wab[:, 1536:1544].rearrange("(k p) n -> p k n", p=128), ["Wab"], ["wfb"])

    S.act(lambda e: e.activation(out=silc[:], in_=cTs[:], func=AF.Silu), reads=["cTs"], writes=["silc"])
    for l in range(2):
        bank, bn = psg()
        for ct in range(24):
            ab = wb[ct % 2][:, :].bitcast(F32).rearrange("p (k n) -> p k n", k=8); an = "wb%d" % (ct % 2)
            ld(ab, w_ada[l][:, ct * 256:(ct + 1) * 256].rearrange("(k p) n -> p k n", p=128), [], [an])
            for f2 in range(2):
                fc = ct * 2 + f2
                for kc in range(8):
                    MM(bank[:, fc * 3:fc * 3 + 3], ab[:, kc, f2 * 128:(f2 + 1) * 128], silc[:, kc * 3:kc * 3 + 3],
                       kc == 0, kc == 7, [an, "silc"], [bn])
        S.dve(lambda e, l=l, bank=bank: e.tensor_tensor(out=modT[:, l].rearrange("p f s -> p (f s)"), in0=bank[:, 0:144],
                                                        in1=bexps[:, l * 144:(l + 1) * 144], op=ALU.add),
              reads=[bn, "bexps"], writes=["modT"])
        for w in range(2):
            for c in range(8):
                S.dve(lambda e, l=l, w=w, c=c: e.tensor_scalar(out=scl[:, l, w, c, :], in0=modT[:, l, (8 if w == 0 else 32) + c, :],
                                                               scalar1=1.0, scalar2=ngs[:, (l * 2 + w) * 8 + c:(l * 2 + w) * 8 + c + 1],
                                                               op0=ALU.add, op1=ALU.mult),
                      reads=["modT", "ngs"], writes=["scl"])

    def mod(l, k, c, s):
        return modT[:, l, k * 8 + c, s:s + 1]

    def groups(ncols):
        return [(a, min(a + 512, ncols)) for a in range(0, ncols, 512)]

    def segsplit(a, b, segs):
        return [(max(a, sa), min(b, sb_), s) for (sa, sb_, s) in segs if max(a, sa) < min(b, sb_)]

    def load_x(src, ntok):
        XB = [(tmpf[0], "tmpf0"), (tmpf[1], "tmpf1"), (stg[0], "stg0"), (stg[1], "stg1")]
        for t0 in range(0, ntok, 128):
            n = min(128, ntok - t0)
            for c4 in range(2):
                buf, bufn = XB[rot("gx", 4)]
                ld(buf[0:n, :], src[t0:t0 + n, c4 * 512:(c4 + 1) * 512], [], [bufn])
                bank, bn = psg()
                for cc in range(4):
                    S.pe(lambda e, bank=bank, cc=cc, buf=buf, n=n: e.transpose(out=bank[:, cc * 128:cc * 128 + n], in_=buf[0:n, cc * 128:(cc + 1) * 128],
                                                                               identity=ident[0:n, 0:n]),
                         reads=[bufn, "ident"], writes=[bn])
                outv = xT[:, c4 * 4:c4 * 4 + 4, t0:t0 + n]
                inv = bank[:, :].rearrange("p (c t) -> p c t", c=4)[:, :, 0:n]
                if c4 == 0:
                    S.act(lambda e, outv=outv, inv=inv: e.activation(out=outv, in_=inv, func=AF.Copy), reads=[bn], writes=["xT"])
                else:
                    S.dve(lambda e, outv=outv, inv=inv: e.tensor_copy(out=outv, in_=inv), reads=[bn], writes=["xT"])

    def rms_rstd(a, b):
        bank, bn = psg()
        for c in range(8):
            i = rot("q", 2)
            S.act(lambda e, c=c, i=i: e.activation(out=sq[i][:, 0:b - a], in_=xT[:, c, a:b], func=AF.Square), reads=["xT"], writes=["sq%d" % i])
            MM(bank[:, 0:b - a], onesb[:, :], sq[i][:, 0:b - a], c == 0, c == 7, ["sq%d" % i, "onesb"], [bn])
        S.act(lambda e, bank=bank: e.activation(out=rstd[:, 0:b - a], in_=bank[:, 0:b - a], func=AF.Sqrt, bias=epsb[:, 0:1], scale=1.0 / D),
              reads=[bn, "epsb"], writes=["rstd"])
        S.dve(lambda e: e.reciprocal(out=rstd[:, 0:b - a], in_=rstd[:, 0:b - a]), reads=["rstd"], writes=["rstd"])

    def norm_mod(l, w, ncols, segs):
        for (a, b) in groups(ncols):
            rms_rstd(a, b)
            for c in range(8):
                for (sa, sb_, s) in segsplit(a, b, segs):
                    i = rot("t", 2)
                    S.dve(lambda e, c=c, sa=sa, sb_=sb_, s=s, i=i, a=a: e.scalar_tensor_tensor(
                        out=tmpf[i][:, 0:sb_ - sa], in0=xT[:, c, sa:sb_], scalar=scl[:, l, w, c, s:s + 1], in1=rstd[:, sa - a:sb_ - a],
                        op0=ALU.mult, op1=ALU.mult), reads=["xT", "scl", "rstd"], writes=["tmpf%d" % i])
                    S.act(lambda e, c=c, sa=sa, sb_=sb_, s=s, i=i: e.activation(
                        out=hT[:, c, sa:sb_], in_=tmpf[i][:, 0:sb_ - sa], func=AF.Identity, bias=mod(l, 0 if w == 0 else 3, c, s)),
                        reads=["tmpf%d" % i, "modT"], writes=[hTn(sa)])

    def final_norm_store(ncols, dst):
        for (a, b) in groups(ncols):
            rms_rstd(a, b)
            for t0 in range(a, b, 128):
                n = min(128, b - t0)
                i = rot("x", 1)
                for c4 in range(2):
                    bank, bn = psg()
                    for cc in range(4):
                        c = c4 * 4 + cc
                        j = rot("t", 2)
                        S.dve(lambda e, c=c, t0=t0, n=n, j=j, a=a: e.scalar_tensor_tensor(
                            out=tmpf[j][:, 0:n], in0=xT[:, c, t0:t0 + n], scalar=fgs[:, c:c + 1], in1=rstd[:, t0 - a:t0 - a + n],
                            op0=ALU.mult, op1=ALU.mult), reads=["xT", "fgs", "rstd"], writes=["tmpf%d" % j])
                        S.pe(lambda e, bank=bank, cc=cc, j=j, n=n: e.transpose(out=bank[0:n, cc * 128:(cc + 1) * 128], in_=tmpf[j][:, 0:n], identity=ident[:, :]),
                             reads=["tmpf%d" % j, "ident"], writes=[bn])
                    k = rot("g", 2)
                    S.act(lambda e, bank=bank, k=k, n=n: e.activation(out=stg[k][0:n, :], in_=bank[0:n, :], func=AF.Copy),
                          reads=[bn], writes=["stg%d" % k])
                    stdma(dst[t0:t0 + n, c4 * 512:(c4 + 1) * 512], stg[k][0:n, :], ["stg%d" % k], [])

    def tokmajor(ncols, wlist, out_d, scr, row0, vrow0):
        for t0 in range(0, ncols, 128):
            n = min(128, ncols - t0)
            for wi, (wt, wn) in enumerate(wlist):
                bank, bn = psg()
                for kc in range(8):
                    MM(bank[0:n, :], hT[:, kc, t0:t0 + n], wt[:, kc, :], kc == 0, kc == 7, [hTn(t0), wn], [bn])
                i = rot("g", 2)
                if rot("ev", 2) == 0:
                    S.act(lambda e, bank=bank, i=i, n=n: e.activation(out=stg[i][0:n, :], in_=bank[0:n, :], func=AF.Copy), reads=[bn], writes=["stg%d" % i])
                else:
                    S.dve(lambda e, bank=bank, i=i, n=n: e.tensor_copy(out=stg[i][0:n, :], in_=bank[0:n, :]), reads=[bn], writes=["stg%d" % i])
                stdma(out_d[row0 + t0:row0 + t0 + n, wi * 512:(wi + 1) * 512], stg[i][0:n, :], ["stg%d" % i], [])
                if scr is not None:
                    j = rot("b", 2)
                    S.dve(lambda e, i=i, j=j, n=n: e.tensor_copy(out=stb[j][0:n, 0:512], in_=stg[i][0:n, :]), reads=["stg%d" % i], writes=["stb%d" % j])
                    stdma(scr[vrow0 + t0:vrow0 + t0 + n, wi * 512:(wi + 1) * 512], stb[j][0:n, 0:512], ["stb%d" % j], ["vscr"])

    def logf_proj(ncols, nf, row0):
        for t0 in range(0, ncols, 128):
            n = min(128, ncols - t0)
            bank, bn = psg()
            for kc in range(8):
                MM(bank[0:n, 0:8], hT[:, kc, t0:t0 + n], wfb[:, kc, :], kc == 0, kc == 7, [hTn(t0), "wfb"], [bn])
            tt = t0 // 128
            S.dve(lambda e, bank=bank, n=n: e.tensor_tensor(out=fx[0:n, :], in0=bank[0:n, 0:8], in1=bfbs[0:n, :], op=ALU.add), reads=[bn, "bfbs"], writes=["fx"])
            S.act(lambda e, n=n: e.activation(out=fx[0:n, :], in_=fx[0:n, :], func=AF.Exp, scale=-1.0), reads=["fx"], writes=["fx"])
            S.act(lambda e, n=n: e.activation(out=fx[0:n, :], in_=fx[0:n, :], func=AF.Ln, bias=onesf[0:n, 0:1]), reads=["fx"], writes=["fx"])
            S.dve(lambda e, n=n, tt=tt: e.tensor_scalar(out=lf_tm[0:n, tt, :], in0=fx[0:n, :], scalar1=-1.0, scalar2=None, op0=ALU.mult), reads=["fx"], writes=["lf_tm"])
            stdma(nf[row0 + t0:row0 + t0 + n, :], lf_tm[0:n, tt, :], ["lf_tm"], [])

    def kT_proj(ncols, wlist, kscr, col0):
        for wi, (wt, wn) in enumerate(wlist):
            for fc in range(4):
                for (a, b) in groups(ncols):
                    j = rot("b", 2)
                    bank, bn = psg()
                    for kc in range(8):
                        MM(bank[:, 0:b - a], wt[:, kc, fc * 128:(fc + 1) * 128], hT[:, kc, a:b], kc == 0, kc == 7, [hTn(a), wn], [bn])
                    if rot("ev", 2) == 0:
                        S.act(lambda e, bank=bank, a=a, b=b, j=j: e.activation(out=stb[j][:, 0:b - a], in_=bank[:, 0:b - a], func=AF.Copy), reads=[bn], writes=["stb%d" % j])
                    else:
                        S.dve(lambda e, bank=bank, a=a, b=b, j=j: e.tensor_copy(out=stb[j][:, 0:b - a], in_=bank[:, 0:b - a]), reads=[bn], writes=["stb%d" % j])
                    r0 = wi * 512 + fc * 128
                    stdma(kscr[r0:r0 + 128, col0 + a:col0 + b], stb[j][:, 0:b - a], ["stb%d" % j], ["kscr"])

    def q_proj(ncols, wt, wn, hcol, scale, qi=0):
        QA = QAs[qi]
        for (a, b) in groups(ncols):
            bank, bn = psl()
            for kc in range(8):
                MM(bank[0:64, 0:b - a], wt[:, kc, hcol:hcol + 64], hT[:, kc, a:b], kc == 0, kc == 7, [hTn(a), wn], [bn])
            S.act(lambda e, bank=bank, a=a, b=b: e.activation(out=QA[0:64, a:b], in_=bank[0:64, 0:b - a], func=AF.Copy, scale=scale), reads=[bn], writes=["QA%d" % qi])

    def linear_res(ncols, segs, wsrc, wname, l, gk):
        for wi in range(2):
            wt, wn = load_w(wsrc, wi * 512, 512, wname)
            for oc in range(4):
                c = wi * 4 + oc
                for (a, b) in groups(ncols):
                    bank, bn = psg()
                    for kc in range(8):
                        MM(bank[:, 0:b - a], wt[:, kc, oc * 128:(oc + 1) * 128], aT[:, kc, a:b], kc == 0, kc == 7, ["aT", wn], [bn])
                    for (sa, sb_, s) in segsplit(a, b, segs):
                        S.dve(lambda e, bank=bank, c=c, sa=sa, sb_=sb_, s=s, a=a: e.scalar_tensor_tensor(
                            out=xT[:, c, sa:sb_], in0=bank[:, sa - a:sb_ - a], scalar=mod(l, gk, c, s), in1=xT[:, c, sa:sb_], op0=ALU.mult, op1=ALU.add),
                            reads=[bn, "modT", "xT"], writes=["xT"])

    def mlp(ncols, segs, l):
        for ff2 in range(4):
            wus = [load_w(Wup[l], ff2 * 1024 + hh * 512, 512, "Wup%d" % l) for hh in range(2)]
            for hh, (wu, wun) in enumerate(wus):
                for fc in range(4):
                    for (a, b) in groups(ncols):
                        bank, bn = psg()
                        for kc in range(8):
                            MM(bank[:, 0:b - a], wu[:, kc, fc * 128:(fc + 1) * 128], hT[:, kc, a:b], kc == 0, kc == 7, [hTn(a), wun], [bn])
                        i = rot("t", 2)
                        S.act(lambda e, bank=bank, a=a, b=b, i=i: e.activation(out=tmpf[i][:, 0:b - a], in_=bank[:, 0:b - a], func=AF.Relu), reads=[bn], writes=["tmpf%d" % i])
                        S.dve(lambda e, fc=fc, hh=hh, a=a, b=b, i=i: e.tensor_tensor(out=aT[:, hh * 4 + fc, a:b], in0=tmpf[i][:, 0:b - a], in1=tmpf[i][:, 0:b - a], op=ALU.mult),
                              reads=["tmpf%d" % i], writes=["aT"])
            wds = [load_wrows(Wdn[l], ff2 * 1024 + hh * 512, "Wdn%d" % l) for hh in range(2)]
            for c in range(8):
                for (a, b) in groups(ncols):
                    bank, bn = psg()
                    for f8 in range(8):
                        wd, wdn = wds[f8 // 4]
                        MM(bank[:, 0:b - a], wd[:, f8 % 4, c * 128:(c + 1) * 128], aT[:, f8, a:b], f8 == 0, f8 == 7, ["aT", wdn], [bn])
                    for (sa, sb_, s) in segsplit(a, b, segs):
                        S.dve(lambda e, bank=bank, c=c, sa=sa, sb_=sb_, s=s, a=a: e.scalar_tensor_tensor(
                            out=xT[:, c, sa:sb_], in0=bank[:, sa - a:sb_ - a], scalar=mod(l, 5, c, s), in1=xT[:, c, sa:sb_], op0=ALU.mult, op1=ALU.add),
                            reads=[bn, "modT", "xT"], writes=["xT"])

    def store_head(h, a, b, src_ap, srcn):
        c, p0 = h // 2, (h % 2) * 64
        S.act(lambda e: e.activation(out=aT[p0:p0 + 64, c, a:b], in_=src_ap, func=AF.Copy), reads=[srcn], writes=["aT"])

    def fox_finish_a(h, a, b, ob, obn):
        n = b - a
        S.act(lambda e: e.activation(out=rec[0:1, 0:n], in_=ob[64:65, 0:n], func=AF.Copy), reads=[obn], writes=["rec"])
        S.dve(lambda e: e.reciprocal(out=rec[0:1, 0:n], in_=rec[0:1, 0:n]), reads=["rec"], writes=["rec"])
        S.act(lambda e: e.activation(out=osb[:, 0:n], in_=ob[0:64, 0:n], func=AF.Copy), reads=[obn], writes=["osb"])

    def fox_finish_b(h, a, b, ob, obn):
        n = b - a
        bank, bn = psg()
        MM(bank[0:64, 0:n], onesf[0:1, 0:64], rec[0:1, 0:n], True, True, ["onesf", "rec"], [bn])
        S.dve(lambda e: e.tensor_tensor(out=osb2[:, 0:n], in0=osb[:, 0:n], in1=bank[0:64, 0:n], op=ALU.mult), reads=["osb", bn], writes=["osb2"])
        store_head(h, a, b, osb2[:, 0:n], "osb2")

    def fox_finish(h, a, b, ob, obn):
        fox_finish_a(h, a, b, ob, obn)
        fox_finish_b(h, a, b, ob, obn)

    def pipeline(stages, T, skews=None, hook=None, hook_at=2, hook2=None, hook2_at=9, hook3=None, hook3_at=4):
        if skews is None:
            skews = list(range(len(stages)))
        for step in range(T + max(skews)):
            if hook is not None and step == hook_at:
                hook()
            if hook2 is not None and step == hook2_at:
                hook2()
            if hook3 is not None and step == hook3_at:
                hook3()
            for fn, k in zip(stages, skews):
                t = step - k
                if 0 <= t < T:
                    fn(t)
        if hook is not None and T + max(skews) <= hook_at:
            hook()
        if hook2 is not None and T + max(skews) <= hook2_at:
            hook2()
        if hook3 is not None and T + max(skews) <= hook3_at:
            hook3()

    pending = []
    pending_b = []

    def flush_pending_a():
        while pending:
            fa, fb = pending.pop(0)
            fa()
            pending_b.append(fb)

    def flush_pending_b():
        while pending_b:
            pending_b.pop(0)()

    def flush_pending():
        flush_pending_a()
        flush_pending_b()

    def fox_prompt(h, ci, qi=0, next_q=None):
        QA = QAs[qi]
        nk = (ci + 1) * NT
        ld(QA[67:70, 0:NT], fsQ[h, :, ci * NT:(ci + 1) * NT], ["fsQ"], ["QA%d" % qi])
        for pc in range(nk // 1024):
            ka, kb = pc * 1024, (pc + 1) * 1024
            ld(KA[0:64, ka:kb], k0T[h * 64:(h + 1) * 64, ka:kb], ["kscr"], ["KA%d" % pc])
            ld(KA[64:67, ka:kb], fsK[h, :, ka:kb], ["fsK"], ["KA%d" % pc])
            ld(VA[:, pc * 8:pc * 8 + 8, 0:64], v0[ka:kb, h * 64:(h + 1) * 64].rearrange("(k p) d -> p k d", p=128), ["vscr"], ["VA%d" % pc])
        for g in range(2):
            qa = g * 512
            ob, obn = pso()
            nfull = 8 * ci + 4 * g
            tiles = [(kt, 0, None) for kt in range(nfull)] + [(nfull + i, i * 128, i) for i in range(4)]
            T = len(tiles)
            stt = [None] * T

            def A(t, tiles=tiles, stt=stt, qa=qa):
                kt, c0, di = tiles[t]
                zb, zbn = psz()
                MM(zb[:, c0:512], KA[0:70, kt * 128:(kt + 1) * 128], QA[0:70, qa + c0:qa + 512], True, di is None, ["KA%d" % (kt // 8), "QA%d" % qi], [zbn])
                if di is not None:
                    MM(zb[:, c0:c0 + 128], identb[:, :], maskF[:, :], False, True, ["identb", "maskF"], [zbn])
                i = rot("p", 3)
                S.act(lambda e, zb=zb, c0=c0, i=i: e.activation(out=pt[i][:, c0:512], in_=zb[:, c0:512], func=AF.Exp), reads=[zbn], writes=["pt%d" % i])
                stt[t] = i

            def C(t, tiles=tiles, stt=stt, ob=ob, obn=obn, T=T):
                kt, c0, di = tiles[t]
                i = stt[t]
                MM(ob[0:65, c0:512], VA[:, kt, 0:65], pt[i][:, c0:512], t == 0, t == T - 1, ["VA%d" % (kt // 8), "pt%d" % i], [obn])

            pipeline([A, C], T, [0, 2], hook=flush_pending_a, hook2=flush_pending_b, hook3=(next_q if g == 1 else None))
            pending.append((lambda h=h, qa=qa, ob=ob, obn=obn: fox_finish_a(h, qa, qa + 512, ob, obn),
                            lambda h=h, qa=qa, ob=ob, obn=obn: fox_finish_b(h, qa, qa + 512, ob, obn)))

    def pool_group(g, tg_cols, ucols, first16, wu, wun, out_cols):
        a, b = ucols
        n = b - a
        pi = rot("pl", 8)
        plb = plbs[pi]; plbn = "plb%d" % pi
        w = 2 ** (g + 1)
        bank, bn = psg()
        for kc in range(8):
            MM(bank[:, 0:n], wu[:, kc, g * 128:(g + 1) * 128], hT[:, kc, a:b], kc == 0, kc == 7, [hTn(a), wun], [bn])
        S.dve(lambda e: e.tensor_copy(out=uext[:, 0:16], in_=halo[:, g, :]), reads=["halo"], writes=["uext"])
        S.dve(lambda e: e.tensor_copy(out=uext[:, 16:16 + n], in_=bank[:, 0:n]), reads=[bn], writes=["uext"])
        S.dve(lambda e: e.tensor_copy(out=halo[:, g, :], in_=uext[:, n:n + 16]), reads=["uext"], writes=["halo"])
        E = 16 + n
        src, srcn = uext, "uext"
        bufs = [(pa, "pa"), (pb, "pb")]
        sh = 1
        for stp in range(g + 1):
            dst, dstn = bufs[stp % 2]
            lo = 2 * sh - 1
            S.dve(lambda e, dst=dst, src=src, lo=lo, sh=sh: e.tensor_tensor(out=dst[:, lo:E], in0=src[:, lo:E], in1=src[:, lo - sh:E - sh], op=ALU.add),
                   reads=[srcn], writes=[dstn])
            src, srcn = dst, dstn
            sh *= 2
        S.dve(lambda e, src=src: e.scalar_tensor_tensor(out=plb[:, 0:n], in0=src[:, 16:E], scalar=1.0 / w, in1=uext[:, 16:E], op0=ALU.mult, op1=ALU.subtract),
               reads=[srcn, "uext"], writes=[plbn])
        if first16:
            S.pool(lambda e, src=src: e.tensor_tensor(out=t16[:, :], in0=src[:, 16:32], in1=rc0[:, g * 16:(g + 1) * 16], op=ALU.mult), reads=[srcn, "rc0"], writes=["t16"])
            S.pool(lambda e: e.tensor_tensor(out=plb[:, 0:16], in0=t16[:, :], in1=uext[:, 16:32], op=ALU.subtract), reads=["t16", "uext", plbn], writes=[plbn])
        def part2():
            bank2, bn2 = psg()
            MM(bank2[:, 0:n], wplb[:, g, :], plb[:, 0:n], True, True, ["wplb", plbn], [bn2])
            oa, ob_ = out_cols
            S.act(lambda e: e.activation(out=aT[:, 4 + g, oa:ob_], in_=bank2[:, 0:n], func=AF.Copy, scale=pscs[:, g:g + 1]), reads=[bn2, "pscs"], writes=["aT"])
        return part2

    def f_rows(ncols, col0, ntiles):
        for half in range((ntiles + 3) // 4):
            bank, bn = psg()
            nt_ = min(4, ntiles - half * 4)
            for t in range(nt_):
                S.pe(lambda e, bank=bank, t=t, half=half: e.transpose(out=bank[0:8, t * 128:(t + 1) * 128], in_=lf_tm[:, half * 4 + t, :], identity=ident[:, :]),
                     reads=["lf_tm", "ident"], writes=[bn])
            n = nt_ * 128
            c0 = col0 + half * 512
            S.act(lambda e, bank=bank, n=n: e.activation(out=Fr[:, 0:n], in_=bank[0:8, 0:n], func=AF.Copy), reads=[bn], writes=["Fr"])
            S.dve(lambda e, n=n: e.tensor_tensor_scan(out=FT[:, 0:n], data0=onesf[0:8, 0:n], data1=Fr[:, 0:n],
                                                      initial=0.0, op0=ALU.mult, op1=ALU.add),
                  reads=["Fr", "onesf"], writes=["FT"])
            S.dve(lambda e, n=n: e.tensor_scalar(out=FT[:, 0:n], in0=FT[:, 0:n], scalar1=Fcar[:, 0:1], scalar2=None, op0=ALU.add),
                  reads=["FT", "Fcar"], writes=["FT"])
            S.pool(lambda e, n=n: e.tensor_copy(out=Fcar[:, 0:1], in_=FT[:, n - 1:n]), reads=["FT"], writes=["Fcar"])
            S.dve(lambda e, n=n: e.tensor_copy(out=Fs[0][:, 0:n], in_=FT[:, 0:n]), reads=["FT"], writes=["Fs"])
            S.dve(lambda e, n=n: e.tensor_tensor(out=Fr[:, 0:n], in0=FT[:, 0:n], in1=Fs[0][:, 0:n], op=ALU.subtract), reads=["FT", "Fs"], writes=["Fr"])
            S.dve(lambda e, n=n: e.tensor_copy(out=Fs[1][:, 0:n], in_=Fr[:, 0:n]), reads=["Fr"], writes=["Fs"])
            S.dve(lambda e, n=n: e.tensor_tensor(out=Fr[:, 0:n], in0=Fr[:, 0:n], in1=Fs[1][:, 0:n], op=ALU.subtract), reads=["Fr", "Fs"], writes=["Fr"])
            S.dve(lambda e, n=n: e.tensor_copy(out=Fs[2][:, 0:n], in_=Fr[:, 0:n]), reads=["Fr"], writes=["Fs"])
            for i in range(3):
                stdma(fsQ[:, i, c0:c0 + n], Fs[i][:, 0:n], ["Fs"], ["fsQ"])
            for i in range(3):
                S.dve(lambda e, i=i, n=n: e.tensor_scalar(out=Fs[i][:, 0:n], in0=Fs[i][:, 0:n], scalar1=-1.0, scalar2=None, op0=ALU.mult), reads=["Fs"], writes=["Fs"])
            for i in range(3):
                stdma(fsK[:, i, c0:c0 + n], Fs[i][:, 0:n], ["Fs"], ["fsK"])

    PSEG = [(0, NT, 0)]

    def l0_chunk(ci):
        load_x(xp[ci * NT:(ci + 1) * NT, :], NT)
        norm_mod(0, 0, NT, PSEG)
        wu, wun = load_w(Wab, 1544, 512, "Wab")
        wk, wkn = load_w(Wab, 512, 512, "Wab"); wv, wvn = load_w(Wab, 1024, 512, "Wab")
        p2s = []
        for g in range(4):
            for (a, b) in groups(NT):
                p2s.append(pool_group(g, None, (a, b), ci == 0 and a == 0, wu, wun, (a, b)))
        tokmajor(NT, [(wk, wkn)], fk_o, None, ci * NT, 0)
        tokmajor(NT, [(wv, wvn)], fv_o, v0, ci * NT, ci * NT)
        logf_proj(NT, fl_o, ci * NT)
        kT_proj(NT, [(wk, wkn)], k0T, ci * NT)
        f_rows(NT, ci * NT, 8)
        for p2 in p2s:
            p2()
        if ci == NCH - 1:
            bank, bn = psg()
            for kc in range(8):
                MM(bank[:, :], hT[:, kc, NT - 128:NT], wu[:, kc, :], kc == 0, kc == 7, [hTn(NT - 128), wun], [bn])
            S.act(lambda e, bank=bank: e.activation(out=stg[0][:, :], in_=bank[:, :], func=AF.Copy), reads=[bn], writes=["stg0"])
            stdma(pp_o, stg[0][:, :], ["stg0"], [])
        wq, wqn = load_w(Wab, 0, 512, "Wab")
        q_proj(NT, wq, wqn, 0, 0.125, 0)
        for h in range(8):
            nq = (lambda h=h: q_proj(NT, wq, wqn, (h + 1) * 64, 0.125, (h + 1) % 2)) if h < 7 else None
            fox_prompt(h, ci, h % 2, nq)
        flush_pending()
        linear_res(NT, PSEG, Woab, "Woab", 0, 2)
        norm_mod(0, 1, NT, PSEG)
        mlp(NT, PSEG, 0)
        stdma(x1s[:, :, ci * NT:(ci + 1) * NT], xT[:, :, :], ["xT"], ["x1s"])
        norm_mod(1, 0, NT, PSEG)
        wk1 = [load_w(Wsb, 1024, 512, "Wsb"), load_w(Wsb, 1536, 512, "Wsb")]
        tokmajor(NT, wk1, sk_o, None, ci * NT, 0)
        kT_proj(NT, wk1, k1T, ci * NT)
        wv1 = [load_w(Wsb, 2048, 512, "Wsb"), load_w(Wsb, 2560, 512, "Wsb")]
        tokmajor(NT, wv1, sv_o, v1, ci * NT, ci * NT)

    for ci in range(NCH):
        l0_chunk(ci)

    def sb_group(h, g, qa, qi=0, next_q=None):
        QA = QAs[qi]
        ob, obn = pso()
        S.pool(lambda e: e.memset(Lsum[:, :], 0.0), writes=["Lsum"])
        kts = list(range(16 * g + 15, -1, -1))
        T = len(kts)
        stt = [None] * T

        def A(t):
            kt = kts[t]
            if kt >= 16 * g:
                i0_, em = (kt - 16 * g) // 4, (kt - 16 * g) % 4
                c0 = i0_ * 128
            else:
                i0_, em, c0 = None, None, 0
            zb, zbn = psz()
            MM(zb[:, c0:512], KA[0:64, kt * 128:(kt + 1) * 128], QA[0:64, qa + c0:qa + 512], True, False, ["KA%d" % (kt // 8), "QA%d" % qi], [zbn])
            if i0_ is not None:
                MM(zb[:, c0:c0 + 128], identb[:, :], msb[:, em * 128:(em + 1) * 128], False, False, ["identb", "msb"], [zbn])
            i = rot("e", 2)
            S.act(lambda e, zb=zb, c0=c0, i=i: e.activation(out=e1b[i][:, c0:512], in_=zb[:, c0:512], func=AF.Exp), reads=[zbn], writes=["e1b%d" % i])
            j = rot("l", 3)
            S.act(lambda e, c0=c0, i=i, j=j: e.activation(out=Lb[j][:, c0:512], in_=e1b[i][:, c0:512], func=AF.Ln, bias=onesf[:, 0:1]), reads=["e1b%d" % i], writes=["Lb%d" % j])
            stt[t] = [zb, zbn, c0, j, None]

        def B(t):
            zb, zbn, c0, j, _ = stt[t]
            MM(zb[:, c0:512], trineg[:, :], Lb[j][:, c0:512], False, False, ["trineg", "Lb%d" % j], [zbn])
            MM(zb[:, c0:512], negones[:, :], Lsum[:, c0:512], False, True, ["negones", "Lsum"], [zbn])
            k = rot("p", 3)
            S.act(lambda e, zb=zb, c0=c0, k=k: e.activation(out=pt[k][:, c0:512], in_=zb[:, c0:512], func=AF.Exp), reads=[zbn], writes=["pt%d" % k])
            S.dve(lambda e, c0=c0, j=j: e.tensor_tensor(out=Lsum[:, c0:512], in0=Lsum[:, c0:512], in1=Lb[j][:, c0:512], op=ALU.add), reads=["Lsum", "Lb%d" % j], writes=["Lsum"])
            stt[t][4] = k

        def C(t):
            zb, zbn, c0, j, k = stt[t]
            MM(ob[0:64, c0:512], VA[:, kts[t], 0:64], pt[k][:, c0:512], t == 0, t == T - 1, ["VA%d" % (kts[t] // 8), "pt%d" % k], [obn])

        pipeline([A, B, C], T, hook3=next_q)
        store_head(h, qa, qa + 512, ob[0:64, 0:512], obn)

    def l1_own(oc):
        for mb in range(8):
            m = oc * 8 + mb
            for r in range(4):
                blk = 4 * m + r
                for hf in range(2):
                    buf, bufn = [(tmpf[0], "tmpf0"), (tmpf[1], "tmpf1"), (stg[0], "stg0"), (stg[1], "stg1")][rot("gx", 4)]
                    inv = buf[:, :].rearrange("p (c t) -> p c t", c=4)
                    ld(inv, x1s[:, hf * 4:(hf + 1) * 4, blk * 128:(blk + 1) * 128], ["x1s"], [bufn])
                    outv = xT[:, hf * 4:(hf + 1) * 4, mb * 128:(mb + 1) * 128]
                    if r == 0:
                        S.dve(lambda e, outv=outv, inv=inv, r=r: e.tensor_scalar(out=outv, in0=inv, scalar1=sel[:, r:r + 1], scalar2=None, op0=ALU.mult),
                              reads=[bufn, "sel"], writes=["xT"])
                    else:
                        S.dve(lambda e, outv=outv, inv=inv, r=r: e.scalar_tensor_tensor(out=outv, in0=inv, scalar=sel[:, r:r + 1], in1=outv, op0=ALU.mult, op1=ALU.add),
                              reads=[bufn, "sel", "xT"], writes=["xT"])
        norm_mod(1, 0, NT, PSEG)
        wq1 = [load_w(Wsb, 0, 512, "Wsb"), load_w(Wsb, 512, 512, "Wsb")]
        q_proj(NT, wq1[0][0], wq1[0][1], 0, 0.125, 0)
        for h in range(16):
            nq = (lambda h=h: q_proj(NT, wq1[(h + 1) // 8][0], wq1[(h + 1) // 8][1], ((h + 1) % 8) * 64, 0.125, (h + 1) % 2)) if h < 15 else None
            npc = 4 * oc + 4
            for pc in range(npc - 1, -1, -1):
                ka, kb = pc * 1024, (pc + 1) * 1024
                ld(KA[0:64, ka:kb], k1T[h * 64:(h + 1) * 64, ka:kb], ["kscr"], ["KA%d" % pc])
                ld(VA[:, pc * 8:pc * 8 + 8, 0:64], v1[ka:kb, h * 64:(h + 1) * 64].rearrange("(k p) d -> p k d", p=128), ["vscr"], ["VA%d" % pc])
            for gl in (1, 0):
                sb_group(h, 2 * oc + gl, gl * 512, h % 2, nq if gl == 0 else None)
        linear_res(NT, PSEG, Wosb, "Wosb", 1, 2)
        norm_mod(1, 1, NT, PSEG)
        mlp(NT, PSEG, 1)
        final_norm_store(NT, y_o[oc * NT:(oc + 1) * NT, :])

    for oc in range(2):
        l1_own(oc)

    SSEG = [(0, 16, 1), (16, 32, 2)]

    def sample_cache_T(src, h, hd_cols, nkeys=4096):
        for k0 in range(0, 32, 8):
            i = rot("b", 2)
            S.gdma(lambda e, k0=k0, i=i: e.dma_start(out=stb[i][:, 0:512].rearrange("p (k d) -> p k d", k=8),
                                                     in_=src[k0 * 128:(k0 + 8) * 128, h * 64:(h + 1) * 64].rearrange("(k p) d -> p k d", p=128)),
                   reads=[], writes=["stb%d" % i])
            bank, bn = psl()
            bankb = bank[:, :].bitcast(BF16)
            for t in range(8):
                S.pe(lambda e, bankb=bankb, t=t, i=i: e.transpose(out=bankb[0:64, t * 128:(t + 1) * 128], in_=stb[i][:, t * 64:(t + 1) * 64], identity=identb[:, :]),
                     reads=["stb%d" % i, "identb"], writes=[bn])
            S.act(lambda e, bankb=bankb, k0=k0: e.activation(out=KA[0:64, k0 * 128:(k0 + 8) * 128], in_=bankb[0:64, :], func=AF.Copy), reads=[bn], writes=["KA%d" % (k0 // 8)])

    def sample_cache_V(src, h):
        for k0 in range(0, 32, 8):
            S.gdma(lambda e, k0=k0: e.dma_start(out=VA[:, k0:k0 + 8, 0:64], in_=src[k0 * 128:(k0 + 8) * 128, h * 64:(h + 1) * 64].rearrange("(k p) d -> p k d", p=128)),
                   reads=[], writes=["VA%d" % (k0 // 8)])

    def sample_fox(h, sq_):
        qa = sq_ * 16
        sample_cache_T(cfk[sq_], h, None)
        ld(KA[0:64, 4096:4112], k0T[h * 64:(h + 1) * 64, SEQ + qa:SEQ + qa + 16], ["kscr"], ["KA4"])
        sample_cache_V(cfv[sq_], h)
        ld(VA[0:16, 32, 0:64], v0[SEQ + qa:SEQ + qa + 16, h * 64:(h + 1) * 64], ["vscr"], ["VA4"])
        ob, obn = pso()
        stt = [None] * 33

        def A(kt):
            n = 128 if kt < 32 else 16
            zb, zbn = psz()
            MM(zb[0:n, 0:16], KA[0:64, kt * 128:kt * 128 + n], QA[0:64, qa:qa + 16], True, kt < 32, ["KA%d" % (kt // 8), "QA0"], [zbn])
            if kt == 32:
                MM(zb[0:16, 0:16], identb[0:16, 0:16], masks[0:16, 0:16], False, True, ["identb", "masks"], [zbn])
            i = rot("p", 3)
            S.act(lambda e, zb=zb, n=n, i=i, kt=kt: e.activation(out=pt[i][0:n, 0:16], in_=zb[0:n, 0:16], func=AF.Exp,
                                                                 bias=sbias[0:n, (sq_ * 8 + h) * 33 + kt:(sq_ * 8 + h) * 33 + kt + 1]),
                  reads=[zbn, "sbias"], writes=["pt%d" % i])
            stt[kt] = i

        def C(kt):
            n = 128 if kt < 32 else 16
            i = stt[kt]
            MM(ob[0:65, 0:16], VA[0:n, kt, 0:65], pt[i][0:n, 0:16], kt == 0, kt == 32, ["VA%d" % (kt // 8), "pt%d" % i], [obn])

        pipeline([A, C], 33)
        fox_finish(h, qa, qa + 16, ob, obn)

    def sample_sb(h, sq_):
        qa = sq_ * 16
        sample_cache_T(csk[sq_], h, None)
        ld(KA[0:64, 4096:4112], k1T[h * 64:(h + 1) * 64, SEQ + qa:SEQ + qa + 16], ["kscr"], ["KA4"])
        sample_cache_V(csv[sq_], h)
        ld(VA[0:16, 32, 0:64], v1[SEQ + qa:SEQ + qa + 16, h * 64:(h + 1) * 64], ["vscr"], ["VA4"])
        ob, obn = pso()
        S.pool(lambda e: e.memset(Lsum[:, 0:16], 0.0), writes=["Lsum"])
        kts = list(range(32, -1, -1))
        stt = [None] * 33

        def A(t):
            kt = kts[t]
            n = 128 if kt < 32 else 16
            zb, zbn = psz()
            MM(zb[0:n, 0:16], KA[0:64, kt * 128:kt * 128 + n], QA[0:64, qa:qa + 16], True, False, ["KA%d" % (kt // 8), "QA0"], [zbn])
            if kt == 32:
                MM(zb[0:16, 0:16], identb[0:16, 0:16], masks[0:16, 16:32], False, False, ["identb", "masks"], [zbn])
            i = rot("t", 2)
            S.act(lambda e, zb=zb, n=n, i=i: e.activation(out=tmpf[i][0:n, 0:16], in_=zb[0:n, 0:16], func=AF.Exp), reads=[zbn], writes=["tmpf%d" % i])
            j = rot("l", 3)
            S.act(lambda e, n=n, i=i, j=j: e.activation(out=Lb[j][0:n, 0:16], in_=tmpf[i][0:n, 0:16], func=AF.Ln, bias=onesf[0:n, 0:1]), reads=["tmpf%d" % i], writes=["Lb%d" % j])
            stt[t] = [zb, zbn, n, j, None]

        def B(t):
            zb, zbn, n, j, _ = stt[t]
            MM(zb[0:n, 0:16], trineg[0:n, 0:n], Lb[j][0:n, 0:16], False, False, ["trineg", "Lb%d" % j], [zbn])
            MM(zb[0:n, 0:16], negones[:, 0:n], Lsum[:, 0:16], False, True, ["negones", "Lsum"], [zbn])
            k = rot("p", 3)
            S.act(lambda e, zb=zb, n=n, k=k: e.activation(out=pt[k][0:n, 0:16], in_=zb[0:n, 0:16], func=AF.Exp), reads=[zbn], writes=["pt%d" % k])
            S.dve(lambda e, n=n, j=j: e.tensor_tensor(out=Lsum[0:n, 0:16], in0=Lsum[0:n, 0:16], in1=Lb[j][0:n, 0:16], op=ALU.add), reads=["Lsum", "Lb%d" % j], writes=["Lsum"])
            stt[t][4] = k

        def C(t):
            zb, zbn, n, j, k = stt[t]
            MM(ob[0:64, 0:16], VA[0:n, kts[t], 0:64], pt[k][0:n, 0:16], t == 0, t == 32, ["VA%d" % (kts[t] // 8), "pt%d" % k], [obn])

        pipeline([A, B, C], 33)
        store_head(h, qa, qa + 16, ob[0:64, 0:16], obn)

    sbias = sb("sbias", [128, 2 * 8 * 33], F32)
    clf = sb("clf", [128, 32, 8], F32); csum = sb("csum", [128, 256], F32); ctot = sb("ctot", [128, 256], F32); cpre = sb("cpre", [128, 264], F32)
    triinc = sb("triincS", [128, 128], F32)
    triincd = din("triinc", [128, 128])
    ld(triinc[:], triincd, [], ["triinc"])

    def sample_layer0():
        load_x(xs, 32)
        norm_mod(0, 0, 32, SSEG)
        wk, wkn = load_w(Wab, 512, 512, "Wab"); wv, wvn = load_w(Wab, 1024, 512, "Wab")
        tokmajor(32, [(wk, wkn)], fks_o, None, 0, 0)
        tokmajor(32, [(wv, wvn)], fvs_o, v0, 0, SEQ)
        logf_proj(32, fls_o, 0)
        ld(lfs[0][:, :], lf_tm[0:16, 0, :], ["lf_tm"], ["lfs"])
        ld(lfs[1][:, :], lf_tm[16:32, 0, :], ["lf_tm"], ["lfs"])
        kT_proj(32, [(wk, wkn)], k0T, SEQ)
        wu, wun = load_w(Wab, 1544, 512, "Wab")
        bank, bn = psg()
        for kc in range(8):
            MM(bank[0:32, :], hT[:, kc, 0:32], wu[:, kc, :], kc == 0, kc == 7, [hTn(0), wun], [bn])
        S.act(lambda e, bank=bank: e.activation(out=stg[0][0:32, :], in_=bank[0:32, :], func=AF.Copy), reads=[bn], writes=["stg0"])
        stdma(pps_o, stg[0][0:32, :], ["stg0"], [])
        for sq_ in range(2):
            sps = stg[1]
            ld(sps[0:15, :], spool[sq_], [], ["stg1"])
            bank, bn = psg()
            for g in range(4):
                S.pe(lambda e, bank=bank, g=g: e.transpose(out=bank[:, g * 16 + 1:g * 16 + 16], in_=sps[0:15, g * 128:(g + 1) * 128], identity=ident[0:15, 0:15]),
                     reads=["stg1", "ident"], writes=[bn])
            S.pool(lambda e: e.memset(halo[:], 0.0), writes=["halo"])
            S.dve(lambda e, bank=bank: e.tensor_copy(out=halo[:, :, 1:16], in_=bank[:, 0:64].rearrange("p (g t) -> p g t", g=4)[:, :, 1:16]), reads=[bn], writes=["halo"])
            for g in range(4):
                pool_group(g, None, (sq_ * 16, sq_ * 16 + 16), False, wu, wun, (sq_ * 16, sq_ * 16 + 16))()
        if SAMPLE_ATT:
            for sq_ in range(2):
                ld(clf[:], cfl[sq_].rearrange("(k p) h -> p k h", p=128), [], ["clf"])
                bank, bn = psg()
                MM(bank[:, 0:256], triinc[:, :], clf[:].rearrange("p k h -> p (k h)"), True, True, ["triinc", "clf"], [bn])
                bank2, bn2 = psg()
                MM(bank2[:, 0:256], onesf[:, 0:128], clf[:].rearrange("p k h -> p (k h)"), True, True, ["onesf", "clf"], [bn2])
                S.act(lambda e, bank=bank: e.activation(out=csum[:, :], in_=bank[:, 0:256], func=AF.Copy), reads=[bn], writes=["csum"])
                S.act(lambda e, bank2=bank2: e.activation(out=ctot[:, :], in_=bank2[:, 0:256], func=AF.Copy), reads=[bn2], writes=["ctot"])
                S.pool(lambda e: e.memset(cpre[:, 248:264], 0.0), writes=["cpre"])
                for kt in range(30, -1, -1):
                    S.pool(lambda e, kt=kt: e.tensor_tensor(out=cpre[:, kt * 8:(kt + 1) * 8], in0=cpre[:, (kt + 1) * 8:(kt + 2) * 8], in1=ctot[:, (kt + 1) * 8:(kt + 2) * 8], op=ALU.add),
                           reads=["cpre", "ctot"], writes=["cpre"])
                for h in range(8):
                    col = (sq_ * 8 + h) * 33
                    S.pool(lambda e, h=h, col=col: e.tensor_tensor(out=sbias[:, col:col + 32], in0=ctot[:, :].rearrange("p (k h) -> p h k", h=8)[:, h, :],
                                                                   in1=csum[:, :].rearrange("p (k h) -> p h k", h=8)[:, h, :], op=ALU.subtract),
                           reads=["ctot", "csum"], writes=["sbias"])
                    S.pool(lambda e, h=h, col=col: e.tensor_tensor(out=sbias[:, col:col + 32], in0=sbias[:, col:col + 32],
                                                                   in1=cpre[:, 0:256].rearrange("p (k h) -> p h k", h=8)[:, h, :], op=ALU.add),
                           reads=["sbias", "cpre"], writes=["sbias"])
                bank3, bn3 = psg()
                MM(bank3[0:16, 0:8], triinc[0:16, 0:16], lfs[sq_][0:16, :], True, True, ["triinc", "lfs"], [bn3])
                for h in range(8):
                    col = (sq_ * 8 + h) * 33 + 32
                    S.dve(lambda e, h=h, col=col, bank3=bank3: e.tensor_scalar(out=sbias[0:16, col:col + 1], in0=bank3[0:16, h:h + 1], scalar1=-1.0, scalar2=None, op0=ALU.mult),
                          reads=[bn3], writes=["sbias"])
            wq, wqn = load_w(Wab, 0, 512, "Wab")
            for sq_ in range(2):
                for h in range(8):
                    q_proj(32, wq, wqn, h * 64, 0.125)
                    sample_fox(h, sq_)
        linear_res(32, SSEG, Woab, "Woab", 0, 2)
        norm_mod(0, 1, 32, SSEG)
        mlp(32, SSEG, 0)

    lfs = [sb("lfs%d" % i, [16, 8], F32) for i in range(2)]

    def sample_layer1():
        norm_mod(1, 0, 32, SSEG)
        wk1 = [load_w(Wsb, 1024, 512, "Wsb"), load_w(Wsb, 1536, 512, "Wsb")]
        tokmajor(32, wk1, sks_o, None, 0, 0)
        kT_proj(32, wk1, k1T, SEQ)
        wv1 = [load_w(Wsb, 2048, 512, "Wsb"), load_w(Wsb, 2560, 512, "Wsb")]
        tokmajor(32, wv1, svs_o, v1, 0, SEQ)
        if SAMPLE_ATT:
            wq1 = [load_w(Wsb, 0, 512, "Wsb"), load_w(Wsb, 512, 512, "Wsb")]
            for sq_ in range(2):
                for h in range(16):
                    wt, wn = wq1[h // 8]
                    q_proj(32, wt, wn, (h % 8) * 64, 0.125)
                    sample_sb(h, sq_)
        if DEBUG:
            dbg_a1 = dout("dbg_a1", [128, 8, 32], BF16)
            stdma(dbg_a1, aT[:, :, 0:32], ["aT"], [])
            dbg_h1 = dout("dbg_h1", [128, 8, 32], BF16)
            stdma(dbg_h1, hT[:, :, 0:32], ["hT0"], [])
        linear_res(32, SSEG, Wosb, "Wosb", 1, 2)
        norm_mod(1, 1, 32, SSEG)
        mlp(32, SSEG, 1)
        if DEBUG:
            dbg_x2 = dout("dbg_x2", [128, 8, 32], F32)
            stdma(dbg_x2, xT[:, :, 0:32], ["xT"], [])
        final_norm_store(32, ys_o)

    sample_layer0_pre = None
    sample_layer0()
    sample_layer1()

    S.emit()
    es.close()
    return nc


_NC = None


def kernel(x_prompt, x_sample, c_prompt, c_sample, cache_fox_k, cache_fox_v, cache_fox_logf, state_pool,
           cache_sb_k, cache_sb_v, w_ada, b_ada, norm_g, w_in_ab, b_forget, w_pool, pool_scale, w_out_ab,
           w_in_sb, w_out_sb, w_up, w_down, final_g):
    global _NC
    f32 = np.float32
    bf = ml_dtypes.bfloat16
    A = lambda a: np.ascontiguousarray(np.asarray(a, dtype=f32))
    x_prompt, x_sample, c_prompt, c_sample = A(x_prompt), A(x_sample), A(c_prompt), A(c_sample)
    cache_fox_k, cache_fox_v, cache_fox_logf, state_pool = A(cache_fox_k), A(cache_fox_v), A(cache_fox_logf), A(state_pool)
    cache_sb_k, cache_sb_v = A(cache_sb_k), A(cache_sb_v)
    w_ada, b_ada, norm_g, w_in_ab, b_forget, w_pool = A(w_ada), A(b_ada), A(norm_g), A(w_in_ab), A(b_forget), A(w_pool)
    pool_scale, w_out_ab, w_in_sb, w_out_sb, w_up, w_down, final_g = A(pool_scale), A(w_out_ab), A(w_in_sb), A(w_out_sb), A(w_up), A(w_down), A(final_g)
    if _NC is None:
        _NC = build_program()
    nc = _NC
    kk = np.arange(128)[:, None]; qq = np.arange(128)[None, :]
    ident = np.eye(128, dtype=f32)
    maskF = np.where(kk <= qq, 0.0, NEG).astype(f32)
    mstrict = np.where(kk < qq, 0.0, NEG).astype(f32)
    trineg = np.where(kk >= qq, -1.0, 0.0).astype(f32)
    triinc = np.where(kk <= qq, 1.0, 0.0).astype(f32)
    masks = np.full((128, 32), NEG, f32)
    masks[:16, 0:16] = maskF[:16, :16]; masks[:16, 16:32] = mstrict[:16, :16]
    rc0 = np.zeros((128, 64), f32)
    for g in range(4):
        for pos in range(16):
            rc0[:, g * 16 + pos] = 1.0 / min(pos + 1, 2 ** (g + 1))
    bexp = np.repeat(b_ada.reshape(2, 48, 128).transpose(2, 0, 1)[..., None], 3, axis=-1).reshape(128, 288)
    ngT = norm_g.reshape(2, 2, 8, 128).transpose(3, 0, 1, 2).reshape(128, 32)
    bfb = np.tile(b_forget.reshape(1, 8), (128, 1))
    pscT = pool_scale.reshape(4, 128).T
    fgT = final_g.reshape(8, 128).T
    common = {
        "w_ada": w_ada, "bexp": A(bexp), "ngT": A(ngT), "w_in_ab": w_in_ab[0], "bfb": A(bfb), "w_pool": w_pool[0],
        "pscT": A(pscT), "w_out_ab": w_out_ab[0], "w_in_sb": w_in_sb[0], "w_out_sb": w_out_sb[0], "w_up": w_up, "w_dn": w_down,
        "fgT": A(fgT), "ident": ident, "maskF": maskF.astype(bf), "trineg": trineg.astype(bf), "rc0": rc0,
        "masks": masks.astype(bf), "triinc": triinc,
    }
    in_maps = []
    for c in range(8):
        b, j = c // 4, c % 4
        call = np.stack([c_prompt[b], c_sample[2 * c], c_sample[2 * c + 1]])
        cT = call.reshape(3, 8, 128).transpose(2, 1, 0).reshape(128, 24)
        msb = np.zeros((128, 4, 128), f32)
        for e in range(4):
            msb[:, e, :] = 0.0 if e < j else (mstrict if e == j else NEG)
        sel = np.zeros((128, 4), f32); sel[:, j] = 1.0
        m = dict(common)
        m.update({
            "xp": x_prompt[b], "xs": A(x_sample[2 * c:2 * c + 2].reshape(32, D)), "cT": A(cT),
            "cfk": A(cache_fox_k[0, 2 * c:2 * c + 2].reshape(2, 4096, 512)), "cfv": A(cache_fox_v[0, 2 * c:2 * c + 2].reshape(2, 4096, 512)),
            "cfl": A(cache_fox_logf[0, 2 * c:2 * c + 2]), "spool": A(state_pool[0, 2 * c:2 * c + 2]),
            "csk": A(cache_sb_k[0, 2 * c:2 * c + 2].reshape(2, 4096, D)), "csv": A(cache_sb_v[0, 2 * c:2 * c + 2].reshape(2, 4096, D)),
            "msb": np.ascontiguousarray(msb.reshape(128, 512).astype(bf)), "sel": sel,
        })
        in_maps.append(m)
    res = run_bass_kernel_spmd(nc, in_maps, core_ids=list(range(8))).results
    if DEBUG:
        _LAST["res"] = res
    y_prompt = np.zeros((2, SEQ, D), f32); y_sample = np.zeros((16, 16, D), f32)
    fk_p = np.zeros((1, 2, SEQ, 8, 64), f32); fv_p = np.zeros_like(fk_p); fl_p = np.zeros((1, 2, SEQ, 8), f32)
    pool_p = np.zeros((1, 2, 15, 512), f32); sk_p = np.zeros((1, 2, SEQ, 16, 64), f32); sv_p = np.zeros_like(sk_p)
    fk_s = np.zeros((1, 16, 16, 8, 64), f32); fv_s = np.zeros_like(fk_s); fl_s = np.zeros((1, 16, 16, 8), f32)
    pool_s = np.zeros((1, 16, 15, 512), f32); sk_s = np.zeros((1, 16, 16, 16, 64), f32); sv_s = np.zeros_like(sk_s)
    for c in range(8):
        b, j = c // 4, c % 4
        r = res[c]
        yo = np.asarray(r["y_o"]).reshape(16, 128, D)
        for mm_ in range(16):
            blk = 4 * mm_ + j
            y_prompt[b, blk * 128:(blk + 1) * 128] = yo[mm_]
        y_sample[2 * c:2 * c + 2] = np.asarray(r["ys_o"]).reshape(2, 16, D)
        if j == 0:
            fk_p[0, b] = np.asarray(r["fk_o"]).reshape(SEQ, 8, 64); fv_p[0, b] = np.asarray(r["fv_o"]).reshape(SEQ, 8, 64)
            fl_p[0, b] = np.asarray(r["fl_o"]); pool_p[0, b] = np.asarray(r["pp_o"])[113:128]
            sk_p[0, b] = np.asarray(r["sk_o"]).reshape(SEQ, 16, 64); sv_p[0, b] = np.asarray(r["sv_o"]).reshape(SEQ, 16, 64)
        fk_s[0, 2 * c:2 * c + 2] = np.asarray(r["fks_o"]).reshape(2, 16, 8, 64); fv_s[0, 2 * c:2 * c + 2] = np.asarray(r["fvs_o"]).reshape(2, 16, 8, 64)
        fl_s[0, 2 * c:2 * c + 2] = np.asarray(r["fls_o"]).reshape(2, 16, 8)
        pool_s[0, 2 * c:2 * c + 2] = np.asarray(r["pps_o"]).reshape(2, 16, 512)[:, 1:16]
        sk_s[0, 2 * c:2 * c + 2] = np.asarray(r["sks_o"]).reshape(2, 16, 16, 64); sv_s[0, 2 * c:2 * c + 2] = np.asarray(r["svs_o"]).reshape(2, 16, 16, 64)
    return (y_prompt, y_sample, fk_p, fv_p, fl_p, pool_p, sk_p, sv_p, fk_s, fv_s, fl_s, pool_s, sk_s, sv_s)
```
